# Optimizing a Trainium2 kernel written in Bass

```python
import jax, jax.numpy as jnp
from jax import lax
import numpy as np

D_MODEL = 1024
BATCH = 8
SEQ = 2048
DEPTH = 2

GRID_W = 64
CTX_LEN = 256
D_MIX = D_MODEL

A_HEADS = 6
A_HEAD_DIM = 64
A_WIDTH = A_HEADS * A_HEAD_DIM
DECAY_LORA = 64
ICL_LORA = 64
GATE_LORA = 128
LOG_DECAY_SCALE = 0.606531
GN_EPS = A_HEAD_DIM * 1e-5

B_HEADS = 6
QK_NOPE = 64
QK_ROPE = 32
V_HEAD = 64
B_WIDTH = B_HEADS * V_HEAD
Q_LORA = 768
KV_LORA = 256
ROPE_BASE = 10000.0
Q_BLOCK = 128
ATTN_SCALE = (QK_NOPE + QK_ROPE) ** -0.5

C_GROUPS = 4
C_WIDTH = D_MIX - A_WIDTH - B_WIDTH
CONV_W = 3

D_FF = -(-8 * D_MODEL // (3 * 256)) * 256

A_IN = 3 * A_WIDTH + DECAY_LORA + ICL_LORA + GATE_LORA
B_IN = Q_LORA + KV_LORA + QK_ROPE
C_IN = 3 * C_WIDTH
P_IN = A_IN + B_IN + C_IN
EPS = 1e-6

kernel_name = 'hybrid_rwkv7_mla_shortconv_prefix_dit'


def rms_norm(x, g):
    xf = x.astype(jnp.float32)
    y = xf * lax.rsqrt(jnp.mean(xf * xf, axis=-1, keepdims=True) + EPS)
    return (y * g.astype(jnp.float32)).astype(x.dtype)


def shift_prev(u):
    return jnp.pad(u, ((0, 0), (1, 0), (0, 0)))[:, :-1]


def shift_next(u):
    return jnp.pad(u, ((0, 0), (0, 1), (0, 0)))[:, 1:]


def adaln_params(cond, ada_w, ada_b):
    return jnp.split(jax.nn.silu(cond) @ ada_w + ada_b, 6, axis=-1)


def modulated_norm(x, g, shift, scale):
    return rms_norm(x, g) * (1 + scale) + shift


def axial_rope_tables(rows, dtype):
    r_pos, c_pos = jnp.meshgrid(jnp.arange(rows, dtype=jnp.float32),
                                jnp.arange(GRID_W, dtype=jnp.float32), indexing='ij')
    axis_dim = QK_ROPE // 2
    inv_freq = 1.0 / (ROPE_BASE ** (jnp.arange(0, axis_dim, 2, dtype=jnp.float32) / axis_dim))
    ang_r = r_pos.reshape(-1)[:, None] * inv_freq
    ang_c = c_pos.reshape(-1)[:, None] * inv_freq
    return tuple(t.astype(dtype) for t in (jnp.cos(ang_r), jnp.sin(ang_r), jnp.cos(ang_c), jnp.sin(ang_c)))


def rope_axis(x, cos, sin):
    half = x.shape[-1] // 2
    x1, x2 = x[..., :half], x[..., half:]
    return jnp.concatenate([x1 * cos - x2 * sin, x2 * cos + x1 * sin], axis=-1)


def rope_2d(x, tabs):
    cos_r, sin_r, cos_c, sin_c = tabs
    half = QK_ROPE // 2
    return jnp.concatenate([rope_axis(x[..., :half], cos_r, sin_r),
                            rope_axis(x[..., half:], cos_c, sin_c)], axis=-1)


def wkv7_scan(r, w, k, v, a_vec, b_vec, reverse):
    bsz, _, h, n = r.shape
    xs = tuple(jnp.moveaxis(t.astype(jnp.float32), 1, 0) for t in (r, w, k, v, a_vec, b_vec))

    def step(s, inp):
        r_t, w_t, k_t, v_t, a_t, b_t = inp
        sa = jnp.einsum('bhvk,bhk->bhv', s, a_t)
        s = (s * w_t[:, :, None, :] + sa[..., :, None] * b_t[:, :, None, :]
             + v_t[..., :, None] * k_t[:, :, None, :])
        return s, jnp.einsum('bhvk,bhk->bhv', s, r_t)

    s0 = jnp.zeros((bsz, h, n, n), jnp.float32)
    _, y = lax.scan(step, s0, xs, reverse=reverse)
    return jnp.moveaxis(y, 0, 1)


def rwkv7_group(p_ctx, p_lat, tshift_mu, decay_w0, decay_up, icl_a0, icl_up, gate_up,
                k_k, k_a, r_k, lnx_g, lnx_b):
    n_ctx = p_ctx.shape[1]

    def token_shift(p):
        return p + tshift_mu[0] * (shift_prev(p) - p) + tshift_mu[1] * (shift_next(p) - p)

    p = jnp.concatenate([token_shift(p_ctx), token_shift(p_lat)], axis=1)
    bsz, t_all = p.shape[:2]

    def heads(u):
        return u.reshape(bsz, t_all, A_HEADS, A_HEAD_DIM)

    o1, o2, o3 = A_WIDTH, 2 * A_WIDTH, 3 * A_WIDTH
    o4 = o3 + DECAY_LORA
    o5 = o4 + ICL_LORA
    r, k, v = p[..., :o1], p[..., o1:o2], p[..., o2:o3]
    w_lo, a_lo, g_lo = p[..., o3:o4], p[..., o4:o5], p[..., o5:]
    kk = heads(k * k_k).astype(jnp.float32)
    kk = kk * lax.rsqrt(jnp.sum(kk * kk, axis=-1, keepdims=True) + 1e-12)
    rh, vh = heads(r), heads(v)

    def direction(d):
        w = jnp.exp(-LOG_DECAY_SCALE * jax.nn.sigmoid(
            (decay_w0[d] + jnp.tanh(w_lo) @ decay_up[d]).astype(jnp.float32)))
        a = jax.nn.sigmoid((icl_a0[d] + a_lo @ icl_up[d]).astype(jnp.float32))
        k_d = k * (1 + (a.astype(k.dtype) - 1) * k_a)
        return heads(w), heads(k_d), -kk, kk * heads(a)

    w_f, k_f, a_f, b_f = direction(0)
    y_f = wkv7_scan(rh, w_f, k_f, vh, a_f, b_f, reverse=False)

    def to_bwd(u):
        return jnp.concatenate([u[:, n_ctx:], u[:, :n_ctx]], axis=1)

    def from_bwd(u):
        n_lat = t_all - n_ctx
        return jnp.concatenate([u[:, n_lat:], u[:, :n_lat]], axis=1)

    w_b, k_b, a_b, b_b = direction(1)
    y_b = from_bwd(wkv7_scan(*(to_bwd(u) for u in (rh, w_b, k_b, vh, a_b, b_b)), reverse=True))

    y = y_f + y_b
    mu = jnp.mean(y, axis=-1, keepdims=True)
    var = jnp.mean(jnp.square(y - mu), axis=-1, keepdims=True)
    y = ((y - mu) * lax.rsqrt(var + GN_EPS)).reshape(bsz, t_all, A_WIDTH) * lnx_g + lnx_b
    bonus = (jnp.sum(rh * (k_f + k_b) * r_k, axis=-1, keepdims=True) * vh).reshape(bsz, t_all, A_WIDTH)
    gate = jax.nn.sigmoid(g_lo) @ gate_up
    out = (y.astype(p.dtype) + bonus) * gate
    return out[:, :n_ctx], out[:, n_ctx:]


def mla_attend(q_n, q_r, k_n, k_r, v):
    s = jnp.einsum('bqhd,bkhd->bhqk', q_n, k_n) + jnp.einsum('bqhr,bkr->bhqk', q_r, k_r)
    p = jax.nn.softmax(s.astype(jnp.float32) * ATTN_SCALE, axis=-1).astype(v.dtype)
    return jnp.einsum('bhqk,bkhd->bqhd', p, v)


def mla_group(p_lat, p_ctx, q_norm_g, kv_norm_g, w_uq, w_ukv, q_nope_g, k_nope_g, q_rope_g, k_rope_g,
              rope_tabs, need_ctx):
    def queries(p):
        bsz, t = p.shape[:2]
        q = (rms_norm(p[..., :Q_LORA], q_norm_g) @ w_uq).reshape(bsz, t, B_HEADS, QK_NOPE + QK_ROPE)
        return rms_norm(q[..., :QK_NOPE], q_nope_g), rms_norm(q[..., QK_NOPE:], q_rope_g)

    def keys_values(p):
        bsz, t = p.shape[:2]
        kv = (rms_norm(p[..., Q_LORA:Q_LORA + KV_LORA], kv_norm_g) @ w_ukv).reshape(
            bsz, t, B_HEADS, QK_NOPE + V_HEAD)
        k_r = rms_norm(p[..., Q_LORA + KV_LORA:], k_rope_g)
        return rms_norm(kv[..., :QK_NOPE], k_nope_g), k_r, kv[..., QK_NOPE:]

    bsz, n = p_lat.shape[:2]
    n_ctx = p_ctx.shape[1]
    q_n, q_r = queries(p_lat)
    q_r = rope_2d(q_r, tuple(t[:, None, :] for t in rope_tabs))
    k_n, k_r, v = keys_values(p_lat)
    k_r = rope_2d(k_r, rope_tabs)
    kc_n, kc_r, vc = keys_values(p_ctx)
    k_all_n = jnp.concatenate([k_n, kc_n], axis=1)
    k_all_r = jnp.concatenate([k_r, kc_r], axis=1)
    v_all = jnp.concatenate([v, vc], axis=1)

    nb = n // Q_BLOCK

    def blocks(t):
        return jnp.swapaxes(t.reshape((bsz, nb, Q_BLOCK) + t.shape[2:]), 0, 1)

    out = lax.map(lambda qs: mla_attend(qs[0], qs[1], k_all_n, k_all_r, v_all), (blocks(q_n), blocks(q_r)))
    y_lat = jnp.swapaxes(out, 0, 1).reshape(bsz, n, B_WIDTH)
    y_ctx = None
    if need_ctx:
        qc_n, qc_r = queries(p_ctx)
        y_ctx = mla_attend(qc_n, qc_r, kc_n, kc_r, vc).reshape(bsz, n_ctx, B_WIDTH)
    return y_lat, y_ctx


def short_conv_group(p, conv_w):
    b_gate, c_gate, h = p[..., :C_WIDTH], p[..., C_WIDTH:2 * C_WIDTH], p[..., 2 * C_WIDTH:]
    u = c_gate * h
    conv = conv_w[0] * shift_prev(u) + conv_w[1] * u + conv_w[2] * shift_next(u)
    return b_gate * conv


def swiglu(h, w_in, w_out):
    gate, up = jnp.split(h @ w_in, 2, axis=-1)
    return (jax.nn.silu(gate) * up) @ w_out


def setup_inputs(seed: int = 0) -> dict:
    key = jax.random.key(seed)
    ks = iter(jax.random.split(key, 40))
    f32 = jnp.float32

    def nrm(shape, scale):
        return jax.random.normal(next(ks), shape, f32) * scale

    def gain(shape):
        return 1.0 + nrm(shape, 0.02)

    return {
        'x': nrm((BATCH, SEQ, D_MODEL), 1.0),
        'c': nrm((BATCH, D_MODEL), 1.0),
        'ctx': nrm((BATCH, CTX_LEN, D_MODEL), 1.0),
        'c_ctx': nrm((D_MODEL,), 1.0),
        'ada_w': nrm((DEPTH, D_MODEL, 6 * D_MODEL), 0.5 * D_MODEL ** -0.5),
        'ada_b': nrm((DEPTH, 6 * D_MODEL), 0.02),
        'norm1_g': gain((DEPTH, D_MODEL)),
        'norm2_g': gain((DEPTH, D_MODEL)),
        'w_in': nrm((DEPTH, D_MODEL, P_IN), D_MODEL ** -0.5),
        'tshift_mu': jax.random.uniform(next(ks), (DEPTH, 2, A_IN), f32, 0.0, 0.5),
        'decay_w0': nrm((DEPTH, 2, A_WIDTH), 0.5),
        'decay_up': nrm((DEPTH, 2, DECAY_LORA, A_WIDTH), 0.5 * DECAY_LORA ** -0.5),
        'icl_a0': nrm((DEPTH, 2, A_WIDTH), 0.5),
        'icl_up': nrm((DEPTH, 2, ICL_LORA, A_WIDTH), 0.5 * ICL_LORA ** -0.5),
        'gate_up': nrm((DEPTH, GATE_LORA, A_WIDTH), GATE_LORA ** -0.5),
        'k_k': 0.85 + nrm((DEPTH, A_WIDTH), 0.05),
        'k_a': 1.0 + nrm((DEPTH, A_WIDTH), 0.05),
        'r_k': nrm((DEPTH, A_HEADS, A_HEAD_DIM), 0.1),
        'lnx_g': gain((DEPTH, A_WIDTH)),
        'lnx_b': nrm((DEPTH, A_WIDTH), 0.02),
        'q_norm_g': gain((DEPTH, Q_LORA)),
        'kv_norm_g': gain((DEPTH, KV_LORA)),
        'w_uq': nrm((DEPTH, Q_LORA, B_HEADS * (QK_NOPE + QK_ROPE)), Q_LORA ** -0.5),
        'w_ukv': nrm((DEPTH, KV_LORA, B_HEADS * (QK_NOPE + V_HEAD)), KV_LORA ** -0.5),
        'q_nope_g': gain((DEPTH, QK_NOPE)),
        'k_nope_g': gain((DEPTH, QK_NOPE)),
        'q_rope_g': gain((DEPTH, QK_ROPE)),
        'k_rope_g': gain((DEPTH, QK_ROPE)),
        'conv_w': nrm((DEPTH, CONV_W, C_WIDTH), CONV_W ** -0.5),
        'w_out': nrm((DEPTH, D_MIX, D_MODEL), D_MIX ** -0.5),
        'w_ffn_in': nrm((DEPTH, D_MODEL, 2 * D_FF), D_MODEL ** -0.5),
        'w_ffn_out': nrm((DEPTH, D_FF, D_MODEL), D_FF ** -0.5),
    }


def reference(x, c, ctx, c_ctx, ada_w, ada_b, norm1_g, norm2_g, w_in, tshift_mu, decay_w0, decay_up,
              icl_a0, icl_up, gate_up, k_k, k_a, r_k, lnx_g, lnx_b, q_norm_g, kv_norm_g, w_uq, w_ukv,
              q_nope_g, k_nope_g, q_rope_g, k_rope_g, conv_w, w_out, w_ffn_in, w_ffn_out):
    n = x.shape[1]
    ROWS = n // GRID_W
    rope_tabs = axial_rope_tables(ROWS, x.dtype)
    for l in range(DEPTH):
        need_ctx = l < DEPTH - 1
        sh_a, sc_a, g_a, sh_f, sc_f, g_f = adaln_params(c[:, None, :], ada_w[l], ada_b[l])
        csh_a, csc_a, cg_a, csh_f, csc_f, cg_f = adaln_params(c_ctx[None, None, :], ada_w[l], ada_b[l])
        p_lat = modulated_norm(x, norm1_g[l], sh_a, sc_a) @ w_in[l]
        p_ctx = modulated_norm(ctx, norm1_g[l], csh_a, csc_a) @ w_in[l]
        ya_ctx, ya_lat = rwkv7_group(p_ctx[..., :A_IN], p_lat[..., :A_IN], tshift_mu[l], decay_w0[l],
                                     decay_up[l], icl_a0[l], icl_up[l], gate_up[l], k_k[l], k_a[l], r_k[l],
                                     lnx_g[l], lnx_b[l])
        yb_lat, yb_ctx = mla_group(p_lat[..., A_IN:A_IN + B_IN], p_ctx[..., A_IN:A_IN + B_IN], q_norm_g[l],
                                   kv_norm_g[l], w_uq[l], w_ukv[l], q_nope_g[l], k_nope_g[l], q_rope_g[l],
                                   k_rope_g[l], rope_tabs, need_ctx)
        yc_lat = short_conv_group(p_lat[..., A_IN + B_IN:], conv_w[l])
        x = x + g_a * (jnp.concatenate([ya_lat, yb_lat, yc_lat], axis=-1) @ w_out[l])
        x = x + g_f * swiglu(modulated_norm(x, norm2_g[l], sh_f, sc_f), w_ffn_in[l], w_ffn_out[l])
        if need_ctx:
            yc_ctx = short_conv_group(p_ctx[..., A_IN + B_IN:], conv_w[l])
            ctx = ctx + cg_a * (jnp.concatenate([ya_ctx, yb_ctx, yc_ctx], axis=-1) @ w_out[l])
            ctx = ctx + cg_f * swiglu(modulated_norm(ctx, norm2_g[l], csh_f, csc_f), w_ffn_in[l], w_ffn_out[l])
    return x
```

```python
import numpy as np
from contextlib import ExitStack
import concourse.bass as bass
import concourse.mybir as mybir
from concourse.bass_utils import run_bass_kernel_spmd

F32 = mybir.dt.float32
BF16 = mybir.dt.bfloat16
AF = mybir.ActivationFunctionType
ALU = mybir.AluOpType

D = 1024
KC = 8
T = 2304
NCTX = 256
NLAT = 2048
P_IN = 3232
A_IN = 1408
DFF = 2816
NJ = 22
EPS = 1e-6
BLKS = [(0, 256), (256, 768), (768, 1280), (1280, 1792), (1792, 2304)]


_GROUPS = {}
DBG = {}


class SemGroup:
    def __init__(self, name):
        self.name = name
        self.sem = None
        self.cnt = 0


class Buf:
    __slots__ = ("name", "w", "r", "grp")

    def __init__(self, name, grp=None):
        self.name = name
        self.w = None
        self.r = {}
        if grp is None:
            grp = _GROUPS.get(name)
            if grp is None:
                grp = _GROUPS[name] = SemGroup(name)
        self.grp = grp


class Sched:
    def __init__(self, nc, same_sync=True, waw_sync=True):
        self.nc = nc
        self.waw = waw_sync
        self.E = {"pe": nc.tensor, "act": nc.scalar, "dve": nc.vector, "pool": nc.gpsimd, "sp": nc.sync}
        self.sem = {e: nc.alloc_semaphore("sem_" + e) for e in ("pe", "act", "dve", "pool")}
        self.cnt = {e: 0 for e in self.sem}
        self.known = {e: {} for e in self.E}
        self.clock = {}
        self.semh = dict(self.sem)
        self.same = same_sync
        self.groups = []
        self.n_inst = 0
        self.rr = 0

    def _sync(self, eng, reads, writes, is_dma=False):
        need = {}
        kn = self.known[eng]

        def add(dep):
            k, v = dep
            if k == eng and not is_dma and (eng == "pe" or not self.same):
                return
            if kn.get(k, 0) >= v:
                return
            if need.get(k, 0) < v:
                need[k] = v

        for b in reads:
            if b.w is not None:
                add(b.w)
        for b in writes:
            if b.w is not None and (is_dma or self.waw or b.w[0] != eng):
                add(b.w)
            for k, v in b.r.items():
                if is_dma or self.waw or k != eng:
                    add((k, v))
        for k, v in need.items():
            if kn.get(k, 0) >= v:
                continue
            self.E[eng].wait_ge(self.semh[k], v)
            kn[k] = v
            ck = self.clock.get((k, v))
            if ck:
                for kk, vv in ck.items():
                    if kn.get(kk, 0) < vv:
                        kn[kk] = vv

    def op(self, eng, fn, reads=(), writes=()):
        self._sync(eng, reads, writes)
        inst = fn(self.E[eng])
        self.cnt[eng] += 1
        v = self.cnt[eng]
        inst.then_inc(self.sem[eng], 1)
        dep = (eng, v)
        self.clock[dep] = dict(self.known[eng])
        for b in reads:
            if b.r.get(eng, 0) < v:
                b.r[eng] = v
        for b in writes:
            b.w = dep
            b.r = {}
        self.n_inst += 1

    def dma(self, q, out, in_, reads, write):
        self._sync(q, reads, [write], is_dma=True)
        g = write.grp
        if g.sem is None:
            g.sem = self.nc.alloc_semaphore("ds_" + g.name)
            self.semh[("d", g.name)] = g.sem
            self.groups.append(g)
        g.cnt += 1
        self.E[q].dma_start(out=out, in_=in_).then_inc(g.sem, 16)
        k = ("d", g.name)
        v = 16 * g.cnt
        self.clock[(k, v)] = dict(self.known[q])
        for b in reads:
            if b.r.get(k, 0) < v:
                b.r[k] = v
        write.w = (k, v)
        write.r = {}
        self.n_inst += 1

    def dma_rr(self, queues, out, in_, reads, write):
        q = queues[self.rr % len(queues)]
        self.rr += 1
        self.dma(q, out, in_, reads, write)

    def barrier(self):
        for e in self.E:
            kn = self.known[e]
            for f in self.sem:
                if f == e and e == "sp":
                    continue
                v = self.cnt[f]
                if v > 0 and kn.get(f, 0) < v:
                    self.E[e].wait_ge(self.sem[f], v)
                    kn[f] = v
            for g in self.groups:
                k = ("d", g.name)
                v = 16 * g.cnt
                if v > 0 and kn.get(k, 0) < v:
                    self.E[e].wait_ge(g.sem, v)
                    kn[k] = v


PP = {}
_off = 0
for _n, _w in [("adab", 48), ("n1g", 8), ("n2g", 8), ("mu0", 11), ("mu1", 11), ("qng", 6), ("kvng", 2),
               ("conv", 6), ("lnxg", 3), ("lnxb", 3), ("w0", 6), ("a0", 6), ("kk", 3), ("ka", 3), ("rk", 3)]:
    PP[_n] = (_off, _w)
    _off += _w
NPP = _off


def _cols(v, width=128):
    v = np.asarray(v, np.float32)
    n = v.size // width
    return np.ascontiguousarray(v.reshape(n, width).T)


def pack_pp(inp, l):
    pp = np.zeros((128, NPP), np.float32)

    def put(name, arr):
        o, w = PP[name]
        assert arr.shape[1] == w, (name, arr.shape)
        pp[: arr.shape[0], o:o + w] = arr

    put("adab", _cols(inp["ada_b"][l]))
    put("n1g", _cols(inp["norm1_g"][l]))
    put("n2g", _cols(inp["norm2_g"][l]))
    put("mu0", _cols(inp["tshift_mu"][l, 0]))
    put("mu1", _cols(inp["tshift_mu"][l, 1]))
    put("qng", _cols(inp["q_norm_g"][l]))
    put("kvng", _cols(inp["kv_norm_g"][l]))
    cw = inp["conv_w"][l]
    put("conv", np.concatenate([_cols(cw[t]) for t in range(3)], axis=1))
    put("lnxg", _cols(inp["lnx_g"][l]))
    put("lnxb", _cols(inp["lnx_b"][l]))
    put("w0", np.concatenate([_cols(inp["decay_w0"][l, d]) for d in range(2)], axis=1))
    put("a0", np.concatenate([_cols(inp["icl_a0"][l, d]) for d in range(2)], axis=1))
    put("kk", _cols(inp["k_k"][l]))
    put("ka", _cols(inp["k_a"][l]))
    put("rk", _cols(inp["r_k"][l].reshape(-1)))
    return pp


def build_program(dbg=False, n_layers=2, stages=("mix", "rwkv", "mla", "ffn")):
    nc = bass.Bass("TRN2", target_bir_lowering=False)
    _GROUPS.clear()
    S = Sched(nc)
    skind = "ExternalOutput" if dbg else "Internal"

    def din(name, shape, dt=F32):
        return nc.dram_tensor(name, list(shape), dt, kind="ExternalInput")

    xT_d = din("xT", [D, T])
    cT_d = din("cT", [128, 16])
    pp_d = din("pp", [2, 128, NPP])
    ada_w_d = din("ada_w", [2, D, 6 * D])
    w_in_d = din("w_in", [2, D, P_IN])
    w_out_d = din("w_out", [2, D, D])
    w_fi_d = din("w_ffn_in", [2, D, 2 * DFF])
    w_fo_d = din("w_ffn_out", [2, DFF, D])
    dup_d = din("decay_up", [2, 2, 64, 384])
    iup_d = din("icl_up", [2, 2, 64, 384])
    gup_d = din("gate_up", [2, 128, 384])
    masks_d = din("masks", [64, 2, 192])
    fm_d = nc.dram_tensor("fm_s", [2, 64, 24, T], BF16, kind="Internal")
    tmBK_d = nc.dram_tensor("tmBK_s", [36, 64, 2, 2, 6, 64], BF16, kind="Internal")
    tmV_d = nc.dram_tensor("tmV_s", [36, 64, 6, 64], BF16, kind="Internal")
    gC_d = nc.dram_tensor("gC_s", [64, 2, 6, 36], F32, kind=skind)
    bon_d = nc.dram_tensor("bon_s", [384, T], F32, kind=skind)
    gat_d = nc.dram_tensor("gat_s", [384, T], F32, kind=skind)
    Y_d = nc.dram_tensor("Y_s", [2, T, 384], F32, kind=skind)
    B_fm, B_tm, B_gC, B_bg, B_Y = Buf("fm"), Buf("tm"), Buf("gC"), Buf("bg"), Buf("Ys")
    w_uq_d = din("w_uq", [2, 768, 576])
    w_ukv_d = din("w_ukv", [2, 256, 768])
    gbc_d = din("gbc", [2, 128, 192])
    rope_d = din("rope", [NLAT, 32])
    ident_d = din("ident", [128, 128])
    yT_d = nc.dram_tensor("yT", [D, NLAT], F32, kind="ExternalOutput")

    pA_d = nc.dram_tensor("pA_T", [A_IN, T], F32, kind=skind)
    pB_d = nc.dram_tensor("pB_T", [1024 + 32, T], F32, kind=skind)
    cat_d = nc.dram_tensor("cat_T", [D, T], BF16, kind=skind)
    B_pA = Buf("pA")
    B_pB = Buf("pB")
    B_cat = Buf("cat")
    B_y = Buf("yT")

    es_top = ExitStack()

    uid = [0]

    def sb(es, name, shape, dt):
        uid[0] += 1
        return es.enter_context(nc.sbuf_tensor("s%d_%s" % (uid[0], name), list(shape), dt))

    def ps(es, name, shape, dt=F32):
        uid[0] += 1
        return es.enter_context(nc.psum_tensor("p%d_%s" % (uid[0], name), list(shape), dt))

    xT = sb(es_top, "xT", [128, KC, T], F32)
    XB = [Buf("xb%d" % i) for i in range(len(BLKS))]
    ones_bf = sb(es_top, "ones_bf", [128, 128], BF16)
    B_const = Buf("const")
    ppt = sb(es_top, "ppt", [128, 2, NPP], F32)
    B_pp = Buf("pp")
    mod = sb(es_top, "mod", [128, 48, 2], F32)
    gs1 = sb(es_top, "gs1", [128, KC, 2], F32)
    gs2 = sb(es_top, "gs2", [128, KC, 2], F32)
    cmix = sb(es_top, "cmix", [128, 11], F32)
    B_mod = Buf("mod")
    scT = sb(es_top, "scT", [128, KC, 2], BF16)
    B_scT = Buf("scT")

    S.op("dve", lambda e: e.memset(ones_bf[:], 1.0), writes=[B_const])
    ident_f = sb(es_top, "ident_f", [128, 128], F32)
    ident_b = sb(es_top, "ident_b", [128, 128], BF16)
    B_id = Buf("ident")
    S.dma("sp", ident_f[:], ident_d.ap(), [], B_id)
    S.op("dve", lambda e: e.tensor_copy(out=ident_b[:], in_=ident_f[:]), reads=[B_id], writes=[B_const])
    for i, (t0, t1) in enumerate(BLKS):
        S.dma("sp", xT[:, :, t0:t1], xT_d.ap().rearrange("(kc p) t -> p kc t", p=128)[:, :, t0:t1], [], XB[i])
    S.dma("sp", ppt[:], pp_d.ap().rearrange("l p n -> p l n"), [], B_pp)
    with nc.sbuf_tensor("cTt", [128, 16], F32) as cTt:
        B_c = Buf("cT")
        S.dma("sp", cTt[:], cT_d.ap(), [], B_c)
        S.op("act", lambda e: e.activation(out=scT[:].rearrange("p a b -> p (a b)"), in_=cTt[:], func=AF.Silu),
             reads=[B_c], writes=[B_scT])
        S.barrier()

    def ppc(l, name, j=0, n=1, parts=128):
        o, w = PP[name]
        return ppt[0:parts, l, o + j:o + j + n]

    def norm_mod(es, bi, gs, sh_chunk0, dst, dst_buf, dst_t0, tmp, B_tmp, sq, B_sq, ps_s, B_pss, rstd, B_rstd):
        t0, t1 = BLKS[bi]
        w = t1 - t0
        ci = 1 if bi == 0 else 0
        S.op("act", lambda e: e.activation(out=sq[:, :, 0:w], in_=xT[:, :, t0:t1], func=AF.Square),
             reads=[XB[bi]], writes=[B_sq])
        for kc in range(KC):
            S.op("pe", lambda e, kc=kc: e.matmul(ps_s[:, 0:w], ones_bf[:], sq[:, kc, 0:w], start=(kc == 0), stop=(kc == KC - 1)),
                 reads=[B_sq, B_const], writes=[B_pss])
        S.op("act", lambda e: e.activation(out=rstd[:, 0:w], in_=ps_s[:, 0:w], func=AF.Sqrt, bias=EPS, scale=1.0 / D),
             reads=[B_pss], writes=[B_rstd])
        S.op("dve", lambda e: e.reciprocal(out=rstd[:, 0:w], in_=rstd[:, 0:w]), reads=[B_rstd], writes=[B_rstd])
        for kc in range(KC):
            S.op("dve", lambda e, kc=kc: e.tensor_tensor(out=tmp[:, 0:w], in0=xT[:, kc, t0:t1], in1=rstd[:, 0:w], op=ALU.mult),
                 reads=[XB[bi], B_rstd], writes=[B_tmp])
            S.op("act", lambda e, kc=kc: e.activation(out=dst[:, kc, dst_t0:dst_t0 + w], in_=tmp[:, 0:w], func=AF.Identity,
                                                      scale=gs[:, kc, ci:ci + 1], bias=mod[:, sh_chunk0 + kc, ci:ci + 1]),
                 reads=[B_tmp, B_mod], writes=[dst_buf])

    ATTN_SCALE = 96.0 ** -0.5

    def headnorm(src, H, n, gain, dst, scr, B_src, B_dst, B_scr):
        sqt, ssq = scr
        S.op("dve", lambda e: e.tensor_tensor(out=sqt[:, 0:H, 0:n], in0=src, in1=src, op=ALU.mult), reads=[B_src], writes=[B_scr])
        yield
        S.op("dve", lambda e: e.tensor_reduce(out=ssq[:, 0:H], in_=sqt[:, 0:H, 0:n], axis=mybir.AxisListType.X, op=ALU.add),
             reads=[B_scr], writes=[B_scr])
        yield
        S.op("act", lambda e: e.activation(out=ssq[:, 0:H], in_=ssq[:, 0:H], func=AF.Sqrt, bias=EPS, scale=1.0 / n),
             reads=[B_scr], writes=[B_scr])
        yield
        S.op("dve", lambda e: e.reciprocal(out=ssq[:, 0:H], in_=ssq[:, 0:H]), reads=[B_scr], writes=[B_scr])
        yield
        S.op("dve", lambda e: e.tensor_tensor(out=sqt[:, 0:H, 0:n], in0=src, in1=ssq[:, 0:H].unsqueeze(2).to_broadcast([128, H, n]), op=ALU.mult),
             reads=[B_src, B_scr], writes=[B_scr])
        yield
        S.op("dve", lambda e: e.tensor_tensor(out=dst, in0=sqt[:, 0:H, 0:n], in1=gain.unsqueeze(1).to_broadcast([128, H, n]), op=ALU.mult),
             reads=[B_scr, B_const], writes=[B_dst])
        yield

    CDEC = 0.606531
    GN_EPS = 64e-5
    HT = 1152
    TBH = [(0, 512), (512, 1024), (1024, 1152)]

    def rwkv_prep(l):
        with ExitStack() as es:
            dup = sb(es, "dup", [64, 2, 384], BF16)
            iup = sb(es, "iup", [64, 2, 384], BF16)
            gup = sb(es, "gup", [128, 384], BF16)
            bd = sb(es, "bd", [128, 128], BF16)
            B_w = Buf("rw_w")
            S.dma("pool", dup[:], dup_d.ap()[l].rearrange("d k n -> k d n"), [], B_w)
            S.dma("pool", iup[:], iup_d.ap()[l].rearrange("d k n -> k d n"), [], B_w)
            S.dma("pool", gup[:], gup_d.ap()[l], [], B_w)
            S.op("dve", lambda e: e.memset(bd[:], 0.0), writes=[B_w])
            S.op("dve", lambda e: e.memset(bd[0:64, 0:64], 1.0), writes=[B_w])
            S.op("dve", lambda e: e.memset(bd[64:128, 64:128], 1.0), writes=[B_w])
            tw = sb(es, "tw", [64, T], BF16)
            al = sb(es, "al", [64, T], BF16)
            sgl = sb(es, "sgl", [128, T], BF16)
            B_lo = Buf("lo")
            with ExitStack() as es0:
                lin = sb(es0, "lin", [128, T], F32)
                B_lin = Buf("lin")
                S.dma("sp", lin[0:64, :], pA_d.ap()[1152:1216, :], [B_pA], B_lin)
                S.op("act", lambda e: e.activation(out=tw[:], in_=lin[0:64, :], func=AF.Tanh), reads=[B_lin], writes=[B_lo])
                S.dma("sp", lin[0:64, :], pA_d.ap()[1216:1280, :], [B_pA], B_lin)
                S.op("act", lambda e: e.activation(out=al[:], in_=lin[0:64, :], func=AF.Copy), reads=[B_lin], writes=[B_lo])
                S.dma("sp", lin[:, :], pA_d.ap()[1280:1408, :], [B_pA], B_lin)
                S.op("act", lambda e: e.activation(out=sgl[:], in_=lin[:, :], func=AF.Sigmoid), reads=[B_lin], writes=[B_lo])
                S.barrier()
            names = ("r", "k", "v", "kk", "asum", "xs")
            tl = {n: sb(es, "rp_" + n, [128, HT], F32) for n in names}
            Bt = {n: Buf("rp_" + n) for n in names}
            tld, Btd = [], []
            for d_ in range(2):
                td = {n: sb(es, "rp_%s%d" % (n, d_), [128, HT], F32) for n in ("sw", "ai", "P", "E", "F", "x1", "x2")}
                bd_ = {n: Buf("rp_%s%d" % (n, d_)) for n in ("sw", "ai", "P", "E", "F", "x1", "x2")}
                td["g1"], bd_["g1"] = td["P"], bd_["P"]
                td["g2"], bd_["g2"] = td["sw"], bd_["sw"]
                tld.append(td)
                Btd.append(bd_)
            ob = [sb(es, "rp_ob%d" % i, [128, HT], BF16) for i in range(3)]
            Bob = [Buf("rp_ob%d" % i) for i in range(3)]
            sqb = sb(es, "rp_sqb", [128, HT], BF16)
            B_sqb = Buf("rp_sqb")
            base = [sb(es, "rp_base%d" % i, [128, 18], F32) for i in range(2)]
            gct = [sb(es, "rp_gct%d" % i, [128, 18], F32) for i in range(2)]
            B_base = [Buf("rp_base%d" % i) for i in range(2)]
            B_gct = [Buf("rp_gct%d" % i) for i in range(2)]
            pm = [ps(es, "rp_pm%d" % i, [128, 512]) for i in range(4)]
            Bpm = [Buf("rp_pm%d" % i) for i in range(4)]
            ptrs = [ps(es, "rp_ptr%d" % i, [64, 1024], BF16) for i in range(2)]
            B_ptrs = [Buf("rp_ptr%d" % i) for i in range(2)]
            stgs = [sb(es, "rp_stg%d" % i, [64, 8, 128], BF16) for i in range(3)]
            B_stgs = [Buf("rp_stg%d" % i) for i in range(3)]
            st = {"p": 0, "ob": 0, "c0": 0, "hf": 0, "tr": 0}

            def mm_full(lhsT, rhs_tile, B_rhs, dst, B_dst, func, bias=None, K=64):
                for (t0, t1) in TBH:
                    w = t1 - t0
                    p = st["p"] % 4
                    st["p"] += 1
                    S.op("pe", lambda e, p=p: e.matmul(pm[p][:, 0:w], lhsT, rhs_tile[0:K, st["c0"] + t0:st["c0"] + t1], start=True, stop=True),
                         reads=[B_rhs, B_w], writes=[Bpm[p]])
                    if bias is None:
                        S.op("act", lambda e, p=p: e.activation(out=dst[:, t0:t1], in_=pm[p][:, 0:w], func=func), reads=[Bpm[p]], writes=[B_dst])
                    else:
                        S.op("act", lambda e, p=p: e.activation(out=dst[:, t0:t1], in_=pm[p][:, 0:w], func=func, bias=bias, scale=1.0),
                             reads=[Bpm[p], B_pp], writes=[B_dst])

            def tt(eng, out, a_, b_, op, rd, wr):
                S.op(eng, lambda e: e.tensor_tensor(out=out, in0=a_, in1=b_, op=op), reads=rd, writes=wr)

            def to_tm(src_bf, B_src, dst_fn):
                for g0 in range(0, 18, 8):
                    n = min(8, 18 - g0)
                    ptr, B_ptr = ptrs[st["tr"] % 2], B_ptrs[st["tr"] % 2]
                    stg, B_stg = stgs[st["tr"] % 3], B_stgs[st["tr"] % 3]
                    st["tr"] += 1
                    for i in range(n):
                        c = g0 + i
                        S.op("pe", lambda e, i=i, c=c: e.transpose(ptr[0:64, i * 128:(i + 1) * 128], src_bf[:, c * 64:(c + 1) * 64], ident_b[:]),
                             reads=[B_src, B_const], writes=[B_ptr])
                    S.op("act", lambda e: e.activation(out=stg[:, 0:n, :], in_=ptr[0:64, 0:n * 128].rearrange("p (a b) -> p a b", b=128), func=AF.Copy),
                         reads=[B_ptr], writes=[B_stg])
                    S.dma("sp", dst_fn(st["hf"] * 18 + g0, n), stg[:, 0:n, :], [B_stg], B_tm)

            def out_bf(src_fn):
                i = st["ob"] % 3
                st["ob"] += 1
                src_fn(ob[i][:], Bob[i])
                return ob[i], Bob[i]

            def fm_store(o, Bo, d, hp, q, c0):
                for hl in range(2):
                    S.dma("sp", fm_d.ap()[d, :, (2 * hp + hl) * 4 + q, c0:c0 + HT], o[hl * 64:(hl + 1) * 64, :], [Bo], B_fm)

            def gen_dir(d, hp, hf, c0):
                T_, B_ = tld[d], Btd[d]
                r_, k_, kk_ = tl["r"], tl["k"], tl["kk"]
                sw, ai, P, E, F_, g1, g2, x1, x2 = (T_[n] for n in ("sw", "ai", "P", "E", "F", "g1", "g2", "x1", "x2"))
                mm_full(dup[:, d, hp * 128:(hp + 1) * 128], tw, B_lo, sw, B_["sw"], AF.Sigmoid, ppc(l, "w0", d * 3 + hp))
                yield
                mm_full(iup[:, d, hp * 128:(hp + 1) * 128], al, B_lo, ai, B_["ai"], AF.Sigmoid, ppc(l, "a0", d * 3 + hp))
                yield
                S.op("dve", lambda e: e.tensor_tensor_scan(out=P[:], data0=sw[:], data1=sw[:], initial=0.0, op0=ALU.add, op1=ALU.max),
                     reads=[B_["sw"]], writes=[B_["P"]])
                yield
                pend = P[:].rearrange("p (c t) -> p c t", t=64)[:, :, 63]
                bs, gc, Bb, Bg = base[d], gct[d], B_base[d], B_gct[d]
                if d == 0:
                    S.op("dve", lambda e: e.memset(bs[:, 0:1], 0.0), writes=[Bb])
                    S.op("dve", lambda e: e.tensor_copy(out=bs[:, 1:18], in_=pend[:, 0:17]), reads=[B_["P"]], writes=[Bb])
                    yield
                    S.op("dve", lambda e: e.tensor_tensor(out=gc[:], in0=pend, in1=bs[:], op=ALU.subtract), reads=[B_["P"], Bb], writes=[Bg])
                else:
                    S.op("dve", lambda e: e.tensor_copy(out=gc[:, 0:1], in_=pend[:, 0:1]), reads=[B_["P"]], writes=[Bg])
                    S.op("dve", lambda e: e.tensor_tensor(out=gc[:, 1:18], in0=pend[:, 1:18], in1=pend[:, 0:17], op=ALU.subtract), reads=[B_["P"]], writes=[Bg])
                    yield
                    S.op("dve", lambda e: e.tensor_copy(out=bs[:], in_=pend), reads=[B_["P"]], writes=[Bb])
                yield
                S.op("act", lambda e: e.activation(out=gc[:], in_=gc[:], func=AF.Exp, scale=-CDEC), reads=[Bg], writes=[Bg])
                for hl in range(2):
                    S.dma("sp", gC_d.ap()[:, d, 2 * hp + hl, hf * 18:(hf + 1) * 18], gc[hl * 64:(hl + 1) * 64, :], [Bg], B_gC)
                tt("dve", E[:].rearrange("p (c t) -> p c t", t=64), P[:].rearrange("p (c t) -> p c t", t=64),
                   bs[:].unsqueeze(2).to_broadcast([128, 18, 64]), ALU.subtract, [B_["P"], Bb], [B_["E"]])
                yield
                tt("pool", F_[:], E[:], sw[:], ALU.subtract, [B_["E"], B_["sw"]], [B_["F"]])
                yield
                if d == 0:
                    li, si, le, se = E, -CDEC, F_, -CDEC
                    Bli, Ble = B_["E"], B_["F"]
                else:
                    li, si, le, se = F_, CDEC, E, CDEC
                    Bli, Ble = B_["F"], B_["E"]
                S.op("act", lambda e: e.activation(out=g1[:], in_=li[:], func=AF.Exp, scale=si), reads=[Bli], writes=[B_["g1"]])
                yield
                S.op("act", lambda e: e.activation(out=g2[:], in_=li[:], func=AF.Exp, scale=-si), reads=[Bli], writes=[B_["g2"]])
                yield
                S.op("act", lambda e: e.activation(out=x1[:], in_=le[:], func=AF.Exp, scale=se), reads=[Ble], writes=[B_["x1"]])
                yield
                o, Bo = out_bf(lambda o, B: S.op("dve", lambda e: e.scalar_tensor_tensor(out=o, in0=kk_[:], scalar=-1.0, in1=x1[:], op0=ALU.mult, op1=ALU.mult),
                                                 reads=[Bt["kk"], B_["x1"]], writes=[B]))
                fm_store(o, Bo, d, hp, 0, c0)
                yield
                o, Bo = out_bf(lambda o, B: tt("pool", o, r_[:], g1[:], ALU.mult, [Bt["r"], B_["g1"]], [B]))
                fm_store(o, Bo, d, hp, 1, c0)
                yield
                tt("dve", x2[:], kk_[:], ai[:], ALU.mult, [Bt["kk"], B_["ai"]], [B_["x2"]])
                yield
                o, Bo = out_bf(lambda o, B: tt("dve", o, x2[:], g2[:], ALU.mult, [B_["x2"], B_["g2"]], [B]))
                fm_store(o, Bo, d, hp, 2, c0)
                to_tm(o, Bo, lambda g0, n: tmBK_d.ap()[g0:g0 + n, :, d, 0, 2 * hp:2 * hp + 2, :].rearrange("c t h k -> t c (h k)"))
                yield
                S.op("dve", lambda e: e.tensor_scalar(out=x2[:], in0=ai[:], scalar1=-1.0, scalar2=ppc(l, "ka", hp), op0=ALU.add, op1=ALU.mult),
                     reads=[B_["ai"], B_pp], writes=[B_["x2"]])
                yield
                S.op("dve", lambda e: e.scalar_tensor_tensor(out=x2[:], in0=x2[:], scalar=1.0, in1=k_[:], op0=ALU.add, op1=ALU.mult),
                     reads=[B_["x2"], Bt["k"]], writes=[B_["x2"]])
                yield
                o, Bo = out_bf(lambda o, B: tt("pool", o, x2[:], g2[:], ALU.mult, [B_["x2"], B_["g2"]], [B]))
                fm_store(o, Bo, d, hp, 3, c0)
                to_tm(o, Bo, lambda g0, n: tmBK_d.ap()[g0:g0 + n, :, d, 1, 2 * hp:2 * hp + 2, :].rearrange("c t h k -> t c (h k)"))
                yield

            for it_ in range(6):
                hp, hf = it_ // 2, it_ % 2
                c0 = hf * HT
                st["c0"], st["hf"] = c0, hf
                r_, k_, v_, kk_ = tl["r"], tl["k"], tl["v"], tl["kk"]
                S.dma("sp", r_[:], pA_d.ap()[hp * 128:(hp + 1) * 128, c0:c0 + HT], [B_pA], Bt["r"])
                S.dma("sp", k_[:], pA_d.ap()[384 + hp * 128:384 + (hp + 1) * 128, c0:c0 + HT], [B_pA], Bt["k"])
                S.dma("sp", v_[:], pA_d.ap()[768 + hp * 128:768 + (hp + 1) * 128, c0:c0 + HT], [B_pA], Bt["v"])
                vb, Bvb = out_bf(lambda o, B: S.op("act", lambda e: e.activation(out=o, in_=v_[:], func=AF.Copy), reads=[Bt["v"]], writes=[B]))
                to_tm(vb, Bvb, lambda g0, n: tmV_d.ap()[g0:g0 + n, :, 2 * hp:2 * hp + 2, :].rearrange("c t h k -> t c (h k)"))
                S.op("act", lambda e: e.activation(out=kk_[:], in_=k_[:], func=AF.Identity, scale=ppc(l, "kk", hp), bias=0.0),
                     reads=[Bt["k"], B_pp], writes=[Bt["kk"]])
                S.op("act", lambda e: e.activation(out=sqb[:], in_=kk_[:], func=AF.Square), reads=[Bt["kk"]], writes=[B_sqb])
                for (t0, t1) in TBH:
                    w = t1 - t0
                    p = st["p"] % 4
                    st["p"] += 1
                    S.op("pe", lambda e, p=p: e.matmul(pm[p][:, 0:w], bd[:], sqb[:, t0:t1], start=True, stop=True), reads=[B_sqb, B_w], writes=[Bpm[p]])
                    S.op("act", lambda e, p=p: e.activation(out=tl["xs"][:, t0:t1], in_=pm[p][:, 0:w], func=AF.Sqrt, bias=1e-12, scale=1.0),
                         reads=[Bpm[p]], writes=[Bt["xs"]])
                S.op("dve", lambda e: e.reciprocal(out=tl["xs"][:], in_=tl["xs"][:]), reads=[Bt["xs"]], writes=[Bt["xs"]])
                tt("dve", kk_[:], kk_[:], tl["xs"][:], ALU.mult, [Bt["kk"], Bt["xs"]], [Bt["kk"]])
                run_streams([gen_dir(0, hp, hf, c0), gen_dir(1, hp, hf, c0)])
                tt("dve", tl["asum"][:], tld[0]["x2"][:], tld[1]["x2"][:], ALU.add, [Btd[0]["x2"], Btd[1]["x2"]], [Bt["asum"]])
                S.op("dve", lambda e: e.scalar_tensor_tensor(out=sqb[:], in0=tl["asum"][:], scalar=ppc(l, "rk", hp), in1=r_[:], op0=ALU.mult, op1=ALU.mult),
                     reads=[Bt["asum"], Bt["r"], B_pp], writes=[B_sqb])
                for (t0, t1) in TBH:
                    w = t1 - t0
                    p = st["p"] % 4
                    st["p"] += 1
                    S.op("pe", lambda e, p=p: e.matmul(pm[p][:, 0:w], bd[:], sqb[:, t0:t1], start=True, stop=True), reads=[B_sqb, B_w], writes=[Bpm[p]])
                    S.op("dve", lambda e, p=p: e.tensor_tensor(out=tl["xs"][:, t0:t1], in0=pm[p][:, 0:w], in1=v_[:, t0:t1], op=ALU.mult),
                         reads=[Bpm[p], Bt["v"]], writes=[Bt["xs"]])
                S.dma("sp", bon_d.ap()[hp * 128:(hp + 1) * 128, c0:c0 + HT], tl["xs"][:], [Bt["xs"]], B_bg)
                mm_full(gup[:, hp * 128:(hp + 1) * 128], sgl, B_lo, tl["asum"], Bt["asum"], AF.Copy, None, K=128)
                S.dma("sp", gat_d.ap()[hp * 128:(hp + 1) * 128, c0:c0 + HT], tl["asum"][:], [Bt["asum"]], B_bg)
            S.barrier()

    def run_streams(streams):
        streams = [iter(x) for x in streams]
        while streams:
            for x in list(streams):
                try:
                    next(x)
                except StopIteration:
                    streams.remove(x)

    def rwkv_scan(l):
        with ExitStack() as es:
            masks = sb(es, "masks", [64, 2, 192], F32)
            gC = sb(es, "gC", [64, 2, 6, 36], F32)
            B_ld0 = Buf("sc_ld0")
            S.dma("sp", masks[:], masks_d.ap(), [], B_ld0)
            S.dma("sp", gC[:], gC_d.ap(), [B_gC], B_ld0)
            Hf = sb(es, "Hf", [64, 2, 6, 64], F32)
            Hb = [sb(es, "Hb%d" % i, [64, 2, 6, 64], BF16) for i in range(2)]
            B_Hf = [Buf("Hf%d" % d) for d in range(2)]
            B_Hb = [[Buf("Hb%d_%d" % (i, d)) for d in range(2)] for i in range(2)]
            S.op("dve", lambda e: e.memset(Hf[:], 0.0), writes=B_Hf)
            S.op("dve", lambda e: e.memset(Hb[0][:], 0.0), writes=B_Hb[0])
            S.op("dve", lambda e: e.memset(Hb[1][:], 0.0), writes=B_Hb[1])
            NB = 3
            fmt = [[sb(es, "fmt%d_%d" % (i, d), [64, 6, 4, 64], BF16) for d in range(2)] for i in range(NB)]
            tbk = [[sb(es, "tbk%d_%d" % (i, d), [64, 2, 6, 64], BF16) for d in range(2)] for i in range(NB)]
            tv = [[sb(es, "tv%d_%d" % (i, d), [64, 6, 64], BF16) for d in range(2)] for i in range(NB)]
            B_in = [[Buf("scin%d_%d" % (i, d)) for d in range(2)] for i in range(NB)]
            SC1 = [[sb(es, "SC1_%d_%d" % (p, d), [64, 6, 128], BF16) for d in range(2)] for p in range(2)]
            SC2 = [[sb(es, "SC2_%d_%d" % (p, d), [64, 6, 128], BF16) for d in range(2)] for p in range(2)]
            TTb = [[sb(es, "TTb_%d_%d" % (p, d), [64, 6, 64], BF16) for d in range(2)] for p in range(2)]
            B_S1 = [[[Buf("sc_S1_%d_%d_%d" % (p, d, hf)) for hf in range(2)] for d in range(2)] for p in range(2)]
            B_S2 = [[[Buf("sc_S2_%d_%d_%d" % (p, d, hf)) for hf in range(2)] for d in range(2)] for p in range(2)]
            B_TT = [[Buf("sc_TT_%d_%d" % (p, d)) for d in range(2)] for p in range(2)]
            PM = [sb(es, "PM_%d" % d, [64, 6, 128], BF16) for d in range(2)]
            B_PM = [[Buf("sc_PM_%d_%d" % (d, hf)) for hf in range(2)] for d in range(2)]
            ZXb = [sb(es, "ZXb_%d" % d, [64, 6, 64], BF16) for d in range(2)]
            Ub = [sb(es, "Ub_%d" % d, [64, 6, 64], BF16) for d in range(2)]
            Ysb = [sb(es, "Ysb_%d" % d, [64, 384], F32) for d in range(2)]
            Bc = {n: [Buf("sc_%s_%d" % (n, d)) for d in range(2)] for n in ("ZXb", "Ub", "Ysb")}
            pG1 = [ps(es, "pG1_%d" % d, [64, 1024]) for d in range(2)]
            pG3 = [ps(es, "pG3_%d" % d, [64, 512]) for d in range(2)]
            pZ = ps(es, "pZ", [64, 512])
            pY = ps(es, "pY", [64, 512])
            B_pG1 = [Buf("pG1_%d" % d) for d in range(2)]
            B_pG3 = [Buf("pG3_%d" % d) for d in range(2)]
            B_pZ, B_pY = Buf("pZ"), Buf("pY")

            def chunk_of(i, d):
                if d == 0:
                    return i
                return 3 - i if i < 4 else 39 - i

            def loads(i):
                sl = i % NB
                for d in range(2):
                    c = chunk_of(i, d)
                    cs = c * 64
                    Bi = B_in[sl][d]
                    S.dma("sp", fmt[sl][d][:].rearrange("p h q t -> p (h q) t"), fm_d.ap()[d, :, :, cs:cs + 64], [B_fm], Bi)
                    S.dma("sp", tbk[sl][d][:], tmBK_d.ap()[c, :, d], [B_tm], Bi)
                    S.dma("sp", tv[sl][d][:], tmV_d.ap()[c], [B_tm], Bi)

            def gen_pre(i, d):
                sl, par = i % NB, i % 2
                F_, Bi = fmt[sl][d], B_in[sl][d]
                s1, s2, ttb = SC1[par][d], SC2[par][d], TTb[par][d]
                bS1, bS2, bTT, bPM = B_S1[par][d], B_S2[par][d], B_TT[par][d], B_PM[d]
                g1, g3, bg1, bg3 = pG1[d], pG3[d], B_pG1[d], B_pG3[d]
                m1 = masks[:, d, 0:128].unsqueeze(1).to_broadcast([64, 6, 128])
                m3 = masks[:, d, 128:192].unsqueeze(1).to_broadcast([64, 6, 64])
                g1v = g1[:, 0:768].rearrange("p (a b) -> p a b", b=128)
                g3v = g3[:, 0:384].rearrange("p (a b) -> p a b", b=64)
                for h in range(6):
                    S.op("pe", lambda e, h=h: e.matmul(g1[:, h * 128:(h + 1) * 128], F_[:, h, 2, :], F_[:, h, 0:2, :].rearrange("p a b -> p (a b)"), start=True, stop=True),
                         reads=[Bi], writes=[bg1])
                for (ha, hb) in ((0, 4), (4, 6)):
                    S.op("dve", lambda e, ha=ha, hb=hb: e.tensor_tensor(out=s1[:, ha:hb, :], in0=g1v[:, ha:hb, :], in1=m1[:, ha:hb, :], op=ALU.mult), reads=[bg1, B_ld0], writes=[bS1[0 if ha == 0 else 1]])
                yield
                for h in range(6):
                    S.op("pe", lambda e, h=h: e.matmul(g1[:, h * 128:(h + 1) * 128], F_[:, h, 3, :], F_[:, h, 0:2, :].rearrange("p a b -> p (a b)"), start=True, stop=True),
                         reads=[Bi], writes=[bg1])
                for (ha, hb) in ((0, 4), (4, 6)):
                    S.op("dve", lambda e, ha=ha, hb=hb: e.tensor_tensor(out=s2[:, ha:hb, :], in0=g1v[:, ha:hb, :], in1=m1[:, ha:hb, :], op=ALU.mult), reads=[bg1, B_ld0], writes=[bS2[0 if ha == 0 else 1]])
                yield
                for h in range(6):
                    S.op("pe", lambda e, h=h: e.matmul(g3[:, h * 64:(h + 1) * 64], F_[:, h, 0, :], F_[:, h, 2, :], start=True, stop=True),
                         reads=[Bi], writes=[bg3])
                S.op("dve", lambda e: e.tensor_tensor(out=PM[d][:, :, 0:64], in0=g3v, in1=m3, op=ALU.mult), reads=[bg3, B_ld0], writes=bPM)
                S.op("act", lambda e: e.activation(out=PM[d][:, :, 64:128], in_=s1[:, :, 0:64], func=AF.Copy), reads=bS1, writes=bPM)
                S.op("dve", lambda e: e.tensor_tensor(out=ttb[:], in0=s1[:, :, 0:64], in1=ident_f[0:64, 0:64].unsqueeze(1).to_broadcast([64, 6, 64]), op=ALU.add),
                     reads=bS1 + [B_id], writes=[bTT])
                yield
                for it in range(5):
                    lastit = it == 4
                    for h in range(6):
                        S.op("pe", lambda e, h=h: e.matmul(g1[:, h * 128:h * 128 + 64], PM[d][:, h, 64:128], PM[d][:, h, 0:64], start=True, stop=True),
                             reads=[bPM[h // 4]], writes=[bg1])
                        if not lastit:
                            S.op("pe", lambda e, h=h: e.matmul(g1[:, h * 128 + 64:(h + 1) * 128], PM[d][:, h, 0:64], PM[d][:, h, 64:128], start=True, stop=True),
                                 reads=[bPM[h // 4]], writes=[bg1])
                    wc = 64 if lastit else 128
                    for (ha, hb) in ((0, 4), (4, 6)):
                        S.op("act", lambda e, ha=ha, hb=hb: e.activation(out=PM[d][:, ha:hb, 0:wc], in_=g1v[:, ha:hb, 0:wc], func=AF.Copy), reads=[bg1], writes=[bPM[0 if ha == 0 else 1]])
                    yield
                    for h in range(6):
                        S.op("pe", lambda e, h=h: e.matmul(g3[:, h * 64:(h + 1) * 64], PM[d][:, h, 0:64], ttb[:, h, :], start=True, stop=True),
                             reads=[bPM[h // 4], bTT], writes=[bg3])
                    S.op("dve", lambda e: e.tensor_tensor(out=ttb[:], in0=ttb[:], in1=g3v, op=ALU.add), reads=[bTT, bg3], writes=[bTT])
                    yield

            def gen_chain(i):
                sl, par = i % NB, i % 2
                hb_old, hb_new = Hb[i % 2], Hb[(i + 1) % 2]
                for d in range(2):
                    c = chunk_of(i, d)
                    cs = c * 64
                    F_, BK, V_, Bi = fmt[sl][d], tbk[sl][d], tv[sl][d], B_in[sl][d]
                    s1, s2, ttb = SC1[par][d], SC2[par][d], TTb[par][d]
                    bS1, bS2, bTT = B_S1[par][d], B_S2[par][d], B_TT[par][d]
                    bHo, bHn = B_Hb[i % 2][d], B_Hb[(i + 1) % 2][d]
                    for h in range(6):
                        S.op("pe", lambda e, h=h: e.matmul(pZ[:, h * 64:(h + 1) * 64], s2[:, h, 0:64], V_[:, h, :], start=True, stop=False),
                             reads=[bS2[h // 4], Bi], writes=[B_pZ])
                        S.op("pe", lambda e, h=h: e.matmul(pZ[:, h * 64:(h + 1) * 64], F_[:, h, 0, :], hb_old[:, d, h, :], start=False, stop=True),
                             reads=[Bi, bHo], writes=[B_pZ])
                    for h in range(6):
                        S.op("pe", lambda e, h=h: e.matmul(pY[:, h * 64:(h + 1) * 64], F_[:, h, 1, :], hb_old[:, d, h, :], start=(h == 0), stop=False, skip_group_check=True),
                             reads=[Bi, bHo], writes=[B_pY])
                        S.op("pe", lambda e, h=h: e.matmul(pY[:, h * 64:(h + 1) * 64], s2[:, h, 64:128], V_[:, h, :], start=False, stop=False, skip_group_check=True),
                             reads=[bS2[h // 4], Bi], writes=[B_pY])
                    S.op("act", lambda e: e.activation(out=ZXb[d][:].rearrange("p a b -> p (a b)"), in_=pZ[:, 0:384], func=AF.Copy), reads=[B_pZ], writes=[Bc["ZXb"][d]])
                    yield
                    for h in range(6):
                        S.op("pe", lambda e, h=h: e.matmul(pZ[:, h * 64:(h + 1) * 64], ttb[:, h, :], ZXb[d][:, h, :], start=True, stop=True),
                             reads=[bTT, Bc["ZXb"][d]], writes=[B_pZ])
                    S.op("act", lambda e: e.activation(out=Ub[d][:].rearrange("p a b -> p (a b)"), in_=pZ[:, 0:384], func=AF.Copy), reads=[B_pZ], writes=[Bc["Ub"][d]])
                    yield
                    for h in range(6):
                        S.op("pe", lambda e, h=h: e.matmul(pZ[:, h * 64:(h + 1) * 64], BK[:, 0, h, :], Ub[d][:, h, :], start=True, stop=False),
                             reads=[Bi, Bc["Ub"][d]], writes=[B_pZ])
                        S.op("pe", lambda e, h=h: e.matmul(pZ[:, h * 64:(h + 1) * 64], BK[:, 1, h, :], V_[:, h, :], start=False, stop=True),
                             reads=[Bi], writes=[B_pZ])
                    for h in range(6):
                        S.op("pe", lambda e, h=h: e.matmul(pY[:, h * 64:(h + 1) * 64], s1[:, h, 64:128], Ub[d][:, h, :], start=False, stop=True, skip_group_check=True),
                             reads=[bS1[h // 4], Bc["Ub"][d]], writes=[B_pY])
                    S.op("dve", lambda e: e.tensor_tensor(out=Hf[:, d].rearrange("p a b -> p (a b)"), in0=Hf[:, d].rearrange("p a b -> p (a b)"), in1=pZ[:, 0:384], op=ALU.add),
                         reads=[B_Hf[d], B_pZ], writes=[B_Hf[d]])
                    S.op("dve", lambda e: e.tensor_tensor(out=Hf[:, d], in0=Hf[:, d], in1=gC[:, d, :, c:c + 1].to_broadcast([64, 6, 64]), op=ALU.mult),
                         reads=[B_Hf[d], B_ld0], writes=[B_Hf[d]])
                    S.op("act", lambda e: e.activation(out=hb_new[:, d], in_=Hf[:, d], func=AF.Copy), reads=[B_Hf[d]], writes=[bHn])
                    S.op("act", lambda e: e.activation(out=Ysb[d][:], in_=pY[:, 0:384], func=AF.Copy), reads=[B_pY], writes=[Bc["Ysb"][d]])
                    S.dma("sp", Y_d.ap()[d, cs:cs + 64, :], Ysb[d][:], [Bc["Ysb"][d]], B_Y)
                    yield

            loads(0)
            loads(1)
            run_streams([gen_pre(0, 0), gen_pre(0, 1)])
            for i in range(36 if not DBG.get("skip_scan") else 0):
                if i + 2 < 36:
                    loads(i + 2)
                streams = [gen_chain(i)]
                if i + 1 < 36:
                    streams += [gen_pre(i + 1, 0), gen_pre(i + 1, 1)]
                run_streams(streams)
            S.barrier()

    def rwkv_post(l):
        with ExitStack() as es:
            yf = [sb(es, "yf%d" % i, [128, 384], F32) for i in range(2)]
            yb = [sb(es, "yb%d" % i, [128, 384], F32) for i in range(2)]
            Byl = [Buf("po_yl%d" % i) for i in range(2)]
            ysq = [sb(es, "ysq%d" % i, [128, 384], F32) for i in range(2)]
            ynb = [sb(es, "ynb%d" % i, [128, 384], BF16) for i in range(2)]
            st6 = [sb(es, "st6%d" % i, [128, 6], F32) for i in range(2)]
            Bw_ = [Buf("po_w%d" % i) for i in range(2)]
            ylT = sb(es, "ylT", [128, 3, T], F32)
            B_ylT = Buf("po_ylT")
            ptt = [ps(es, "po_ptt%d" % i, [128, 1024], BF16) for i in range(2)]
            B_ptt = [Buf("po_ptt%d" % i) for i in range(2)]

            def gen_tile(tt_, s_):
                t0 = tt_ * 128
                B_w = Bw_[s_]
                S.dma("sp", yf[s_][:], Y_d.ap()[0, t0:t0 + 128, :], [B_Y], Byl[s_])
                S.dma("sp", yb[s_][:], Y_d.ap()[1, t0:t0 + 128, :], [B_Y], Byl[s_])
                y3 = yf[s_][:].rearrange("p (h c) -> p h c", c=64)
                q3 = ysq[s_][:].rearrange("p (h c) -> p h c", c=64)
                sx = st6[s_]
                S.op("dve", lambda e: e.tensor_tensor(out=yf[s_][:], in0=yf[s_][:], in1=yb[s_][:], op=ALU.add), reads=[Byl[s_]], writes=[Byl[s_]])
                yield
                S.op("dve", lambda e: e.tensor_reduce(out=sx[:], in_=y3, axis=mybir.AxisListType.X, op=ALU.add), reads=[Byl[s_]], writes=[B_w])
                yield
                S.op("dve", lambda e: e.tensor_scalar(out=sx[:], in0=sx[:], scalar1=1.0 / 64, scalar2=None, op0=ALU.mult), reads=[B_w], writes=[B_w])
                yield
                S.op("dve", lambda e: e.tensor_tensor(out=y3, in0=y3, in1=sx[:].unsqueeze(2).to_broadcast([128, 6, 64]), op=ALU.subtract), reads=[Byl[s_], B_w], writes=[Byl[s_]])
                yield
                S.op("dve", lambda e: e.tensor_tensor(out=ysq[s_][:], in0=yf[s_][:], in1=yf[s_][:], op=ALU.mult), reads=[Byl[s_]], writes=[B_w])
                yield
                S.op("dve", lambda e: e.tensor_reduce(out=sx[:], in_=q3, axis=mybir.AxisListType.X, op=ALU.add), reads=[B_w], writes=[B_w])
                yield
                S.op("act", lambda e: e.activation(out=sx[:], in_=sx[:], func=AF.Sqrt, bias=GN_EPS, scale=1.0 / 64), reads=[B_w], writes=[B_w])
                yield
                S.op("dve", lambda e: e.reciprocal(out=sx[:], in_=sx[:]), reads=[B_w], writes=[B_w])
                yield
                S.op("dve", lambda e: e.tensor_tensor(out=ynb[s_][:].rearrange("p (h c) -> p h c", c=64), in0=y3, in1=sx[:].unsqueeze(2).to_broadcast([128, 6, 64]), op=ALU.mult),
                     reads=[Byl[s_], B_w], writes=[B_w])
                yield
                for j in range(3):
                    S.op("pe", lambda e, j=j: e.transpose(ptt[s_][:, j * 128:(j + 1) * 128], ynb[s_][:, j * 128:(j + 1) * 128], ident_b[:]), reads=[B_w, B_const], writes=[B_ptt[s_]])
                for j in range(3):
                    S.op("act", lambda e, j=j: e.activation(out=ylT[:, j, t0:t0 + 128], in_=ptt[s_][:, j * 128:(j + 1) * 128], func=AF.Identity,
                                                            scale=ppc(l, "lnxg", j), bias=ppc(l, "lnxb", j)), reads=[B_ptt[s_], B_pp], writes=[B_ylT])
                yield

            for pr in range(9):
                run_streams([gen_tile(2 * pr, 0), gen_tile(2 * pr + 1, 1)])
            bo = sb(es, "po_bo", [128, T], F32)
            ga = sb(es, "po_ga", [128, T], F32)
            ob_ = sb(es, "po_ob", [128, T], BF16)
            B_bo, B_ob = Buf("po_bo"), Buf("po_ob")
            for j in range(3):
                S.dma("sp", bo[:], bon_d.ap()[j * 128:(j + 1) * 128, :], [B_bg], B_bo)
                S.dma("sp", ga[:], gat_d.ap()[j * 128:(j + 1) * 128, :], [B_bg], B_bo)
                S.op("dve", lambda e: e.tensor_tensor(out=bo[:], in0=bo[:], in1=ylT[:, j, :], op=ALU.add), reads=[B_bo, B_ylT], writes=[B_bo])
                S.op("dve", lambda e: e.tensor_tensor(out=ob_[:], in0=bo[:], in1=ga[:], op=ALU.mult), reads=[B_bo], writes=[B_ob])
                S.dma("sp", cat_d.ap()[j * 128:(j + 1) * 128, :], ob_[:], [B_ob], B_cat)
            S.barrier()

    def mla_stage(l, last):
        with ExitStack() as es:
            B_qn = Buf("qnb")
            wuq = sb(es, "wuq", [128, 6, 576], BF16)
            wukv = sb(es, "wukv", [128, 2, 768], BF16)
            B_wu = Buf("wu")
            S.dma("pool", wuq[:], w_uq_d.ap()[l].rearrange("(kc p) n -> p kc n", p=128), [], B_wu)
            S.dma("pool", wukv[:], w_ukv_d.ap()[l].rearrange("(kc p) n -> p kc n", p=128), [], B_wu)
            QT = sb(es, "QT", [96, 6, T], BF16)
            KT = sb(es, "KT", [96, 6, T], BF16)
            Vt = sb(es, "Vt", [128, 18, 6, 65], BF16)
            B_QT, B_KT, B_V = Buf("QT"), Buf("KT"), Buf("Vt")
            gbc = sb(es, "gbc", [128, 192], F32)
            ropet = sb(es, "ropet", [128, 16, 32], F32)
            S.dma("sp", gbc[:], gbc_d.ap()[l], [], B_const)
            S.dma("sp", ropet[:], rope_d.ap().rearrange("(n p) c -> p n c", p=128), [], B_const)
            S.op("dve", lambda e: e.memset(Vt[:], 1.0), writes=[B_V])
            with ExitStack() as es2:
                pq0 = ps(es2, "pq0", [128, 512])
                pq1 = ps(es2, "pq1", [128, 512])
                pk0 = ps(es2, "pk0", [128, 512])
                pk1 = ps(es2, "pk1", [128, 512])
                pkr = ps(es2, "pkr", [128, 512])
                ptq = ps(es2, "ptq", [128, 1024], BF16)
                ptk = ps(es2, "ptk", [128, 1024], BF16)
                Bp = {n: Buf(n) for n in ("pq0", "pq1", "pk0", "pk1", "pkr", "ptq", "ptk")}
                sets = []
                for k_ in range(2):
                    Z = {}
                    for (n_, shp, dt_) in (("q_sb", [128, 576], F32), ("kv_sb", [128, 768], F32), ("kr_in", [32, 128], F32), ("kr_sb", [128, 32], F32),
                                           ("qr", [128, 6, 32], F32), ("kr", [128, 1, 32], F32), ("krr", [128, 32], F32), ("rt1", [128, 6, 2, 8], F32),
                                           ("rt2", [128, 6, 2, 8], F32), ("Qtok", [128, 6, 96], BF16), ("Ktok", [128, 6, 96], BF16),
                                           ("sqt", [128, 6, 64], F32), ("ssq", [128, 6], F32)):
                        Z[n_] = sb(es2, "%s_%d" % (n_, k_), shp, dt_)
                    Z["B"] = {n: Buf("%s_%d" % (n, k_)) for n in ("q_sb", "kv_sb", "kr_in", "kr_sb", "qr", "kr", "krr", "rt", "Qtok", "Ktok", "scr")}
                    sets.append(Z)
                qin = sb(es2, "qin", [128, 8, 256], F32)
                B_qin = Buf("qin")
                sq = sb(es2, "sq", [128, 8, 256], BF16)
                tmp = sb(es2, "tmp", [128, 256], F32)
                rsq = sb(es2, "rsq", [128, 256], F32)
                rskv = sb(es2, "rskv", [128, 256], F32)
                qnb = sb(es2, "qnb", [128, 8, 256], BF16)
                B_sq, B_tmp, B_rsq, B_rskv = Buf("sq"), Buf("tmp"), Buf("rsq"), Buf("rskv")
                pbv = pB_d.ap()[0:1024, :].rearrange("(kc p) t -> p kc t", p=128)

                def latent_norm(t0):
                    w = 256
                    t1 = t0 + w
                    S.dma("sp", qin[:, :, 0:w], pbv[:, :, t0:t1], [B_pB], B_qin)
                    S.op("act", lambda e: e.activation(out=sq[:, :, 0:w], in_=qin[:, :, 0:w], func=AF.Square), reads=[B_qin], writes=[B_sq])
                    for kc in range(6):
                        S.op("pe", lambda e, kc=kc: e.matmul(pq0[:, 0:w], ones_bf[:], sq[:, kc, 0:w], start=(kc == 0), stop=(kc == 5)),
                             reads=[B_sq, B_const], writes=[Bp["pq0"]])
                    for kc in range(6, 8):
                        S.op("pe", lambda e, kc=kc: e.matmul(pk0[:, 0:w], ones_bf[:], sq[:, kc, 0:w], start=(kc == 6), stop=(kc == 7)),
                             reads=[B_sq, B_const], writes=[Bp["pk0"]])
                    S.op("act", lambda e: e.activation(out=rsq[:, 0:w], in_=pq0[:, 0:w], func=AF.Sqrt, bias=EPS, scale=1.0 / 768), reads=[Bp["pq0"]], writes=[B_rsq])
                    S.op("dve", lambda e: e.reciprocal(out=rsq[:, 0:w], in_=rsq[:, 0:w]), reads=[B_rsq], writes=[B_rsq])
                    S.op("act", lambda e: e.activation(out=rskv[:, 0:w], in_=pk0[:, 0:w], func=AF.Sqrt, bias=EPS, scale=1.0 / 256), reads=[Bp["pk0"]], writes=[B_rskv])
                    S.op("dve", lambda e: e.reciprocal(out=rskv[:, 0:w], in_=rskv[:, 0:w]), reads=[B_rskv], writes=[B_rskv])
                    for kc in range(8):
                        rs, B_rs = (rsq, B_rsq) if kc < 6 else (rskv, B_rskv)
                        g_ap = ppc(l, "qng", kc) if kc < 6 else ppc(l, "kvng", kc - 6)
                        S.op("dve", lambda e, kc=kc: e.tensor_tensor(out=tmp[:, 0:w], in0=qin[:, kc, 0:w], in1=rs[:, 0:w], op=ALU.mult),
                             reads=[B_qin, B_rs], writes=[B_tmp])
                        S.op("act", lambda e, kc=kc: e.activation(out=qnb[:, kc, 0:w], in_=tmp[:, 0:w], func=AF.Identity, scale=g_ap, bias=0.0),
                             reads=[B_tmp, B_pp], writes=[B_qn])

                def rope(Z, src3, H, dst3, tt, B_src, B_dst):
                    Bn = Z["B"]
                    sv = src3.rearrange("p h (a f e) -> p h a f e", a=2, f=2, e=8)
                    dv = dst3.rearrange("p h (a f e) -> p h a f e", a=2, f=2, e=8)
                    cos = ropet[:, tt - 2, 0:16].rearrange("p (a e) -> p a e", a=2).unsqueeze(1).to_broadcast([128, H, 2, 8])
                    sin = ropet[:, tt - 2, 16:32].rearrange("p (a e) -> p a e", a=2).unsqueeze(1).to_broadcast([128, H, 2, 8])
                    x1, x2 = sv[:, :, :, 0, :], sv[:, :, :, 1, :]
                    a_, b_ = Z["rt1"][:, 0:H], Z["rt2"][:, 0:H]
                    S.op("dve", lambda e: e.tensor_tensor(out=a_, in0=x1, in1=cos, op=ALU.mult), reads=[B_src, B_const], writes=[Bn["rt"]])
                    yield
                    S.op("dve", lambda e: e.tensor_tensor(out=b_, in0=x2, in1=sin, op=ALU.mult), reads=[B_src, B_const], writes=[Bn["rt"]])
                    yield
                    S.op("dve", lambda e: e.tensor_tensor(out=dv[:, :, :, 0, :], in0=a_, in1=b_, op=ALU.subtract), reads=[Bn["rt"]], writes=[B_dst])
                    yield
                    S.op("dve", lambda e: e.tensor_tensor(out=a_, in0=x2, in1=cos, op=ALU.mult), reads=[B_src, B_const, B_dst], writes=[Bn["rt"]])
                    yield
                    S.op("dve", lambda e: e.tensor_tensor(out=b_, in0=x1, in1=sin, op=ALU.mult), reads=[B_src, B_const], writes=[Bn["rt"]])
                    yield
                    S.op("dve", lambda e: e.tensor_tensor(out=dv[:, :, :, 1, :], in0=a_, in1=b_, op=ALU.add), reads=[Bn["rt"]], writes=[B_dst])
                    yield

                def gen_tile(tt, Z):
                    Bn = Z["B"]
                    q_sb, kv_sb, kr_in, kr_sb, qr, kr, krr, Qtok, Ktok = (Z[n] for n in ("q_sb", "kv_sb", "kr_in", "kr_sb", "qr", "kr", "krr", "Qtok", "Ktok"))
                    q3 = q_sb[:].rearrange("p (h c) -> p h c", h=6)
                    kv3 = kv_sb[:].rearrange("p (h c) -> p h c", h=6)
                    scr = (Z["sqt"], Z["ssq"])
                    t0 = tt * 128
                    lo = (tt % 2) * 128
                    need_q = (tt >= 2) or (not last)
                    if need_q:
                        for kc in range(6):
                            S.op("pe", lambda e, kc=kc: e.matmul(pq0[:, 0:512], qnb[:, kc, lo:lo + 128], wuq[:, kc, 0:512], start=(kc == 0), stop=(kc == 5)),
                                 reads=[B_qn, B_wu], writes=[Bp["pq0"]])
                        for kc in range(6):
                            S.op("pe", lambda e, kc=kc: e.matmul(pq1[:, 0:64], qnb[:, kc, lo:lo + 128], wuq[:, kc, 512:576], start=(kc == 0), stop=(kc == 5)),
                                 reads=[B_qn, B_wu], writes=[Bp["pq1"]])
                        S.op("act", lambda e: e.activation(out=q_sb[:, 0:512], in_=pq0[:, 0:512], func=AF.Copy), reads=[Bp["pq0"]], writes=[Bn["q_sb"]])
                        S.op("act", lambda e: e.activation(out=q_sb[:, 512:576], in_=pq1[:, 0:64], func=AF.Copy), reads=[Bp["pq1"]], writes=[Bn["q_sb"]])
                        yield
                    for kc in range(2):
                        S.op("pe", lambda e, kc=kc: e.matmul(pk0[:, 0:512], qnb[:, 6 + kc, lo:lo + 128], wukv[:, kc, 0:512], start=(kc == 0), stop=(kc == 1)),
                             reads=[B_qn, B_wu], writes=[Bp["pk0"]])
                    for kc in range(2):
                        S.op("pe", lambda e, kc=kc: e.matmul(pk1[:, 0:256], qnb[:, 6 + kc, lo:lo + 128], wukv[:, kc, 512:768], start=(kc == 0), stop=(kc == 1)),
                             reads=[B_qn, B_wu], writes=[Bp["pk1"]])
                    S.op("act", lambda e: e.activation(out=kv_sb[:, 0:512], in_=pk0[:, 0:512], func=AF.Copy), reads=[Bp["pk0"]], writes=[Bn["kv_sb"]])
                    S.op("act", lambda e: e.activation(out=kv_sb[:, 512:768], in_=pk1[:, 0:256], func=AF.Copy), reads=[Bp["pk1"]], writes=[Bn["kv_sb"]])
                    S.dma("sp", kr_in[:], pB_d.ap()[1024:1056, t0:t0 + 128], [B_pB], Bn["kr_in"])
                    S.op("pe", lambda e: e.transpose(pkr[:, 0:32], kr_in[:], ident_f[0:32, 0:32]), reads=[Bn["kr_in"], B_id], writes=[Bp["pkr"]])
                    S.op("act", lambda e: e.activation(out=kr_sb[:], in_=pkr[:, 0:32], func=AF.Copy), reads=[Bp["pkr"]], writes=[Bn["kr_sb"]])
                    yield
                    if need_q:
                        yield from headnorm(q3[:, :, 0:64], 6, 64, gbc[:, 0:64], Qtok[:, :, 0:64], scr, Bn["q_sb"], Bn["Qtok"], Bn["scr"])
                        if tt >= 2:
                            yield from headnorm(q3[:, :, 64:96], 6, 32, gbc[:, 128:160], qr[:], scr, Bn["q_sb"], Bn["qr"], Bn["scr"])
                            yield from rope(Z, qr[:], 6, Qtok[:, :, 64:96], tt, Bn["qr"], Bn["Qtok"])
                        else:
                            yield from headnorm(q3[:, :, 64:96], 6, 32, gbc[:, 128:160], Qtok[:, :, 64:96], scr, Bn["q_sb"], Bn["Qtok"], Bn["scr"])
                    yield from headnorm(kv3[:, :, 0:64], 6, 64, gbc[:, 64:128], Ktok[:, :, 0:64], scr, Bn["kv_sb"], Bn["Ktok"], Bn["scr"])
                    S.op("act", lambda e: e.activation(out=Vt[:, tt, :, 0:64], in_=kv3[:, :, 64:128], func=AF.Copy), reads=[Bn["kv_sb"]], writes=[B_V])
                    if tt >= 2:
                        yield from headnorm(kr_sb[:].unsqueeze(1), 1, 32, gbc[:, 160:192], kr[:], scr, Bn["kr_sb"], Bn["kr"], Bn["scr"])
                        yield from rope(Z, kr[:], 1, krr[:].unsqueeze(1), tt, Bn["kr"], Bn["krr"])
                    else:
                        yield from headnorm(kr_sb[:].unsqueeze(1), 1, 32, gbc[:, 160:192], krr[:].unsqueeze(1), scr, Bn["kr_sb"], Bn["krr"], Bn["scr"])
                    S.op("dve", lambda e: e.tensor_copy(out=Ktok[:, :, 64:96], in_=krr[:].unsqueeze(1).to_broadcast([128, 6, 32])),
                         reads=[Bn["krr"]], writes=[Bn["Ktok"]])
                    yield
                    if need_q:
                        for h in range(6):
                            S.op("pe", lambda e, h=h: e.transpose(ptq[0:96, h * 128:(h + 1) * 128], Qtok[:, h, :], ident_b[:]),
                                 reads=[Bn["Qtok"], B_const], writes=[Bp["ptq"]])
                        S.op("act", lambda e: e.activation(out=QT[:, :, t0:t0 + 128], in_=ptq[0:96, 0:768].rearrange("p (h t) -> p h t", h=6), func=AF.Copy),
                             reads=[Bp["ptq"]], writes=[B_QT])
                        yield
                    for h in range(6):
                        S.op("pe", lambda e, h=h: e.transpose(ptk[0:96, h * 128:(h + 1) * 128], Ktok[:, h, :], ident_b[:]),
                             reads=[Bn["Ktok"], B_const], writes=[Bp["ptk"]])
                    S.op("act", lambda e: e.activation(out=KT[:, :, t0:t0 + 128], in_=ptk[0:96, 0:768].rearrange("p (h t) -> p h t", h=6), func=AF.Copy),
                         reads=[Bp["ptk"]], writes=[B_KT])
                    yield

                for pr in range(9 if not DBG.get("skip_mla_b") else 0):
                    latent_norm(pr * 256)
                    run_streams([gen_tile(2 * pr, sets[0]), gen_tile(2 * pr + 1, sets[1])])
                S.barrier()
            with ExitStack() as es2:
                pss = [ps(es2, "pss%d" % i, [128, 512]) for i in range(2)]
                pso = [ps(es2, "pso%d" % i, [128, 512]) for i in range(4)]
                Bpss = [Buf("pss%d" % i) for i in range(2)]
                Bpso = [Buf("pso%d" % i) for i in range(4)]
                pts = [sb(es2, "pt%d" % i, [128, 512], BF16) for i in range(2)]
                Bpt = [Buf("pt%d" % i) for i in range(2)]
                ytok = sb(es2, "ytok", [128, 18, 384], BF16)
                B_yt = Buf("ytok")
                rec = sb(es2, "rec", [128, 4], F32)
                B_rec = Buf("rec")
                ycT = sb(es2, "ycT", [128, 3, T], BF16)
                B_yc = Buf("ycT")
                ptt = ps(es2, "ptt", [128, 1024], BF16)
                B_ptt = Buf("ptt")
                qblocks = [(256 + i * 512, 512, list(range(18))) for i in range(4)]
                if not last:
                    qblocks = [(0, 256, [0, 1])] + qblocks
                its = []
                for h in range(6 if not DBG.get("skip_mla_c") else 0):
                    for (q0, qw, kts) in qblocks:
                        for ki, kt in enumerate(kts):
                            its.append((h, q0, qw, kt, ki, len(kts)))

                def score(i):
                    h, q0, qw, kt, ki, nk = its[i]
                    p = i % 2
                    S.op("pe", lambda e: e.matmul(pss[p][:, 0:qw], KT[:, h, kt * 128:(kt + 1) * 128], QT[:, h, q0:q0 + qw], start=True, stop=True),
                         reads=[B_KT, B_QT], writes=[Bpss[p]])
                    S.op("act", lambda e: e.activation(out=pts[p][:, 0:qw], in_=pss[p][:, 0:qw], func=AF.Exp, scale=ATTN_SCALE),
                         reads=[Bpss[p]], writes=[Bpt[p]])

                if its:
                    score(0)
                for i in range(len(its)):
                    h, q0, qw, kt, ki, nk = its[i]
                    p = i % 2
                    nqs = qw // 128
                    if i + 1 < len(its):
                        score(i + 1)
                    for qs in range(nqs):
                        S.op("pe", lambda e, qs=qs: e.matmul(pso[qs][:, 0:65], pts[p][:, qs * 128:(qs + 1) * 128], Vt[:, kt, h, :],
                                                            start=(ki == 0), stop=(ki == nk - 1)),
                             reads=[Bpt[p], B_V], writes=[Bpso[qs]])
                    if ki == nk - 1:
                        for qs in range(nqs):
                            tq = (q0 + qs * 128) // 128
                            S.op("dve", lambda e, qs=qs: e.reciprocal(out=rec[:, qs:qs + 1], in_=pso[qs][:, 64:65]), reads=[Bpso[qs]], writes=[B_rec])
                            S.op("dve", lambda e, qs=qs, tq=tq: e.tensor_scalar(out=ytok[:, tq, h * 64:(h + 1) * 64], in0=pso[qs][:, 0:64], scalar1=rec[:, qs:qs + 1],
                                                                                scalar2=None, op0=ALU.mult), reads=[Bpso[qs], B_rec], writes=[B_yt])
                tts = range(18) if not last else range(2, 18)
                for tt in tts:
                    for j in range(3):
                        S.op("pe", lambda e, j=j: e.transpose(ptt[:, j * 128:(j + 1) * 128], ytok[:, tt, j * 128:(j + 1) * 128], ident_b[:]),
                             reads=[B_yt, B_const], writes=[B_ptt])
                    S.op("act", lambda e: e.activation(out=ycT[:, :, tt * 128:(tt + 1) * 128], in_=ptt[:, 0:384].rearrange("p (j t) -> p j t", j=3), func=AF.Copy),
                         reads=[B_ptt], writes=[B_yc])
                c_lo = 0 if not last else 256
                for j in range(3):
                    S.dma("sp", cat_d.ap()[384 + j * 128:384 + (j + 1) * 128, c_lo:T], ycT[:, j, c_lo:T], [B_yc], B_cat)
                S.barrier()

    for l in range(n_layers):
        last = l == 1
        with ExitStack() as es:
            wts = [sb(es, "adw%d" % i, [128, KC, 512], BF16) for i in range(2)]
            Bw = [Buf("adw%d" % i) for i in range(2)]
            psm = ps(es, "psm", [128, 96])
            B_psm = Buf("psm")
            prow = [ps(es, "prow%d" % i, [2, 512]) for i in range(2)]
            B_prow = [Buf("prow%d" % i) for i in range(2)]
            mrow = sb(es, "mrow", [2, 6 * D], F32)
            B_mrow = Buf("mrow")
            wv = ada_w_d.ap()[l].rearrange("(kc p) n -> p kc n", p=128)
            for og in range(12):
                s = og % 2
                S.dma("pool", wts[s][:], wv[:, :, og * 512:(og + 1) * 512], [], Bw[s])
                for kc in range(KC):
                    S.op("pe", lambda e, s=s, kc=kc: e.matmul(prow[s][:, 0:512], scT[:, kc, :], wts[s][:, kc, :], start=(kc == 0), stop=(kc == KC - 1)),
                         reads=[Bw[s], B_scT], writes=[B_prow[s]])
                S.op("act", lambda e, s=s, og=og: e.activation(out=mrow[:, og * 512:(og + 1) * 512], in_=prow[s][:, 0:512], func=AF.Copy),
                     reads=[B_prow[s]], writes=[B_mrow])
            for ch in range(48):
                S.op("pe", lambda e, ch=ch: e.transpose(psm[:, 2 * ch:2 * ch + 2], mrow[:, ch * 128:(ch + 1) * 128], ident_f[0:2, 0:2]),
                     reads=[B_mrow, B_id], writes=[B_psm])
            o, _ = PP["adab"]
            S.op("dve", lambda e: e.tensor_tensor(out=mod[:], in0=psm[:].rearrange("p (a b) -> p a b", b=2),
                                                  in1=ppt[:, l, o:o + 48].unsqueeze(2).to_broadcast([128, 48, 2]), op=ALU.add),
                 reads=[B_psm, B_pp], writes=[B_mod])
            for (gs, sc0, gname) in ((gs1, 8, "n1g"), (gs2, 32, "n2g")):
                og_, _ = PP[gname]
                S.op("dve", lambda e, gs=gs, sc0=sc0: e.tensor_scalar(out=gs[:], in0=mod[:, sc0:sc0 + 8, :], scalar1=1.0, scalar2=None, op0=ALU.add),
                     reads=[B_mod], writes=[B_mod])
                S.op("dve", lambda e, gs=gs, og_=og_: e.tensor_tensor(out=gs[:], in0=gs[:], in1=ppt[:, l, og_:og_ + 8].unsqueeze(2).to_broadcast([128, 8, 2]), op=ALU.mult),
                     reads=[B_mod, B_pp], writes=[B_mod])
            o0, _ = PP["mu0"]
            o1, _ = PP["mu1"]
            S.op("dve", lambda e: e.tensor_tensor(out=cmix[:], in0=ppt[:, l, o0:o0 + 11], in1=ppt[:, l, o1:o1 + 11], op=ALU.add),
                 reads=[B_pp], writes=[B_mod])
            S.op("dve", lambda e: e.tensor_scalar(out=cmix[:], in0=cmix[:], scalar1=-1.0, scalar2=1.0, op0=ALU.mult, op1=ALU.add),
                 reads=[B_mod], writes=[B_mod])
            S.barrier()

        if "mix" in stages:
            with ExitStack() as es:
                xnT = sb(es, "xnT", [128, KC, T], BF16)
                B_xn = Buf("xnT")
                tmp = sb(es, "tmp", [128, 512], F32)
                sq = sb(es, "sq", [128, KC, 512], BF16)
                rstd = sb(es, "rstd", [128, 512], F32)
                ps_s = ps(es, "ps_s", [128, 512])
                B_tmp, B_sq, B_pss, B_rstd = Buf("tmp"), Buf("sq"), Buf("pss"), Buf("rstd")
                for bi in range(len(BLKS)):
                    norm_mod(es, bi, gs1, 0, xnT, B_xn, BLKS[bi][0], tmp, B_tmp, sq, B_sq, ps_s, B_pss, rstd, B_rstd)

                wts = [sb(es, "wi%d" % i, [128, KC, 512], BF16) for i in range(2)]
                Bw = [Buf("wi%d" % i) for i in range(2)]
                feat = [sb(es, "feat%d" % i, [128, T], F32) for i in range(5)]
                Bf = [Buf("feat%d" % i) for i in range(5)]
                ybf = sb(es, "ybf", [128, T], BF16)
                B_ybf = Buf("ybf")
                pmm = [ps(es, "pmm%d" % i, [128, 512]) for i in range(2)]
                Bpm = [Buf("pmm%d" % i) for i in range(2)]
                wv = w_in_d.ap()[l].rearrange("(kc p) n -> p kc n", p=128)
                state = {"w": 0, "p": 0, "f": 0}

                def load_w(pieces):
                    s = state["w"] % 2
                    state["w"] += 1
                    o = 0
                    for (c0, wd) in pieces:
                        S.dma("pool", wts[s][:, :, o:o + wd], wv[:, :, c0:c0 + wd], [], Bw[s])
                        o += wd
                    return s

                def proj_unit(s, o, M, fi):
                    for (t0, t1) in BLKS:
                        w = t1 - t0
                        p = state["p"] % 2
                        state["p"] += 1
                        for kc in range(KC):
                            S.op("pe", lambda e, kc=kc, p=p: e.matmul(pmm[p][0:M, 0:w], wts[s][:, kc, o:o + M], xnT[:, kc, t0:t1],
                                                                    start=(kc == 0), stop=(kc == KC - 1)),
                                 reads=[Bw[s], B_xn], writes=[Bpm[p]])
                        S.op("act", lambda e, p=p: e.activation(out=feat[fi][0:M, t0:t1], in_=pmm[p][0:M, 0:w], func=AF.Copy),
                             reads=[Bpm[p]], writes=[Bf[fi]])

                def shift3(src, dst, Bs, Bd, M, c_ap, m0_ap, m1_ap):
                    S.op("act", lambda e: e.activation(out=dst[0:M, :], in_=src[0:M, :], func=AF.Identity, scale=c_ap, bias=0.0),
                         reads=[Bs, B_mod, B_pp], writes=[Bd])
                    for (a0, a1, sh, sc) in ((1, NCTX, -1, m0_ap), (NCTX + 1, T, -1, m0_ap), (0, NCTX - 1, 1, m1_ap), (NCTX, T - 1, 1, m1_ap)):
                        S.op("dve", lambda e, a0=a0, a1=a1, sh=sh, sc=sc: e.scalar_tensor_tensor(
                            out=dst[0:M, a0:a1], in0=src[0:M, a0 + sh:a1 + sh], scalar=sc, in1=dst[0:M, a0:a1],
                            op0=ALU.mult, op1=ALU.add), reads=[Bs, Bd, B_pp], writes=[Bd])

                a_segs = [[(0, 512)], [(512, 512)], [(1024, 384)]]
                ch = 0
                for pieces in a_segs:
                    s = load_w(pieces)
                    for o in range(0, pieces[0][1], 128):
                        fi = state["f"] % 2
                        state["f"] += 1
                        proj_unit(s, o, 128, fi)
                        shift3(feat[fi], feat[2 + fi], Bf[fi], Bf[2 + fi], 128, cmix[:, ch:ch + 1], ppc(l, "mu0", ch), ppc(l, "mu1", ch))
                        S.dma("sp", pA_d.ap()[ch * 128:(ch + 1) * 128, :], feat[2 + fi][:, :], [Bf[2 + fi]], B_pA)
                        ch += 1
                b_segs = [[(1408, 512)], [(1920, 512)], [(2432, 32)]]
                ch = 0
                for pieces in b_segs:
                    s = load_w(pieces)
                    for o in range(0, pieces[0][1], 128):
                        M = min(128, pieces[0][1] - o)
                        fi = state["f"] % 2
                        state["f"] += 1
                        proj_unit(s, o, M, fi)
                        S.dma("sp", pB_d.ap()[ch * 128:ch * 128 + M, :], feat[fi][0:M, :], [Bf[fi]], B_pB)
                        ch += 1
                c0 = A_IN + 1056
                for j in range(2):
                    s = load_w([(c0 + j * 128, 128), (c0 + 256 + j * 128, 128), (c0 + 512 + j * 128, 128)])
                    proj_unit(s, 0, 128, 0)
                    proj_unit(s, 128, 128, 1)
                    proj_unit(s, 256, 128, 2)
                    S.op("dve", lambda e: e.tensor_tensor(out=feat[1][:], in0=feat[1][:], in1=feat[2][:], op=ALU.mult),
                         reads=[Bf[1], Bf[2]], writes=[Bf[1]])
                    shift3(feat[1], feat[3], Bf[1], Bf[3], 128, ppc(l, "conv", 2 + j), ppc(l, "conv", 0 + j), ppc(l, "conv", 4 + j))
                    S.op("dve", lambda e: e.tensor_tensor(out=ybf[:], in0=feat[0][:], in1=feat[3][:], op=ALU.mult),
                         reads=[Bf[0], Bf[3]], writes=[B_ybf])
                    S.dma("sp", cat_d.ap()[768 + j * 128:768 + (j + 1) * 128, :], ybf[:], [B_ybf], B_cat)
                zrows = ([] if "rwkv" in stages else [0, 1, 2]) + ([] if "mla" in stages else [3, 4, 5])
                if zrows:
                    S.op("dve", lambda e: e.memset(ybf[:], 0.0), writes=[B_ybf])
                    for zr in zrows:
                        S.dma("sp", cat_d.ap()[zr * 128:(zr + 1) * 128, :], ybf[:], [B_ybf], B_cat)
                S.barrier()

            if "rwkv" in stages:
                rwkv_prep(l)
                rwkv_scan(l)
                rwkv_post(l)
            if "mla" in stages:
                mla_stage(l, last)

            with ExitStack() as es:
                wo = sb(es, "wo", [128, KC, D], BF16)
                B_wo = Buf("wo")
                S.dma("pool", wo[:, :, 0:512], w_out_d.ap()[l].rearrange("(kc p) n -> p kc n", p=128)[:, :, 0:512], [], B_wo)
                S.dma("pool", wo[:, :, 512:1024], w_out_d.ap()[l].rearrange("(kc p) n -> p kc n", p=128)[:, :, 512:1024], [], B_wo)
                cats = [sb(es, "catb%d" % i, [128, KC, 512], BF16) for i in range(2)]
                Bc = [Buf("catb%d" % i) for i in range(2)]
                pmm = [ps(es, "pmo%d" % i, [128, 512]) for i in range(2)]
                Bpm = [Buf("pmo%d" % i) for i in range(2)]
                pc = 0
                for bi, (t0, t1) in enumerate(BLKS):
                    if last and bi == 0:
                        continue
                    w = t1 - t0
                    ci = 1 if bi == 0 else 0
                    s = bi % 2
                    S.dma("sp", cats[s][:, :, 0:w], cat_d.ap().rearrange("(kc p) t -> p kc t", p=128)[:, :, t0:t1], [B_cat], Bc[s])
                    for fo in range(KC):
                        p = pc % 2
                        pc += 1
                        for kc in range(KC):
                            S.op("pe", lambda e, kc=kc, p=p, fo=fo: e.matmul(pmm[p][:, 0:w], wo[:, kc, fo * 128:(fo + 1) * 128], cats[s][:, kc, 0:w],
                                                                           start=(kc == 0), stop=(kc == KC - 1)),
                                 reads=[B_wo, Bc[s]], writes=[Bpm[p]])
                        S.op("dve", lambda e, p=p, fo=fo: e.scalar_tensor_tensor(
                            out=xT[:, fo, t0:t1], in0=pmm[p][:, 0:w], scalar=mod[:, 16 + fo, ci:ci + 1], in1=xT[:, fo, t0:t1],
                            op0=ALU.mult, op1=ALU.add), reads=[Bpm[p], B_mod, XB[bi]], writes=[XB[bi]])
                S.barrier()

        if "ffn" in stages:
            sbs = [[0, 1, 2], [3, 4]] if not last else [[1, 2], [3, 4]]
            for sbl in sbs:
                with ExitStack() as es:
                    ntok = sum(BLKS[b][1] - BLKS[b][0] for b in sbl)
                    hT = sb(es, "hT", [128, KC, ntok], BF16)
                    B_h = Buf("hT")
                    actT = sb(es, "actT", [128, NJ, ntok], BF16)
                    B_act = [Buf("actT%d" % j) for j in range(NJ)]
                    tmp = sb(es, "tmp", [128, 512], F32)
                    sq = sb(es, "sq", [128, KC, 512], BF16)
                    rstd = sb(es, "rstd", [128, 512], F32)
                    sgs = [sb(es, "sg%d" % i, [128, 512], F32) for i in range(2)]
                    B_sgs = [Buf("sg%d" % i) for i in range(2)]
                    ps_s = ps(es, "ps_s", [128, 512])
                    B_tmp, B_sq, B_pss, B_rstd = Buf("tmp"), Buf("sq"), Buf("pss"), Buf("rstd")
                    loc = {}
                    o = 0
                    for b in sbl:
                        loc[b] = o
                        norm_mod(es, b, gs2, 24, hT, B_h, o, tmp, B_tmp, sq, B_sq, ps_s, B_pss, rstd, B_rstd)
                        o += BLKS[b][1] - BLKS[b][0]
                    wts = [sb(es, "wf%d" % i, [128, KC, 256], BF16) for i in range(2)]
                    Bw = [Buf("wf%d" % i) for i in range(2)]
                    pg = [ps(es, "pg%d" % i, [128, 512]) for i in range(2)]
                    pu = [ps(es, "pu%d" % i, [128, 512]) for i in range(2)]
                    Bpg = [Buf("pg%d" % i) for i in range(2)]
                    Bpu = [Buf("pu%d" % i) for i in range(2)]
                    wv = w_fi_d.ap()[l].rearrange("(kc p) n -> p kc n", p=128)
                    pc = 0
                    for j in range(NJ):
                        s = j % 2
                        S.dma("pool", wts[s][:, :, 0:128], wv[:, :, j * 128:(j + 1) * 128], [], Bw[s])
                        S.dma("pool", wts[s][:, :, 128:256], wv[:, :, DFF + j * 128:DFF + (j + 1) * 128], [], Bw[s])
                        for b in sbl:
                            w = BLKS[b][1] - BLKS[b][0]
                            lo = loc[b]
                            p = pc % 2
                            pc += 1
                            for kc in range(KC):
                                S.op("pe", lambda e, kc=kc, p=p: e.matmul(pg[p][:, 0:w], wts[s][:, kc, 0:128], hT[:, kc, lo:lo + w],
                                                                        start=(kc == 0), stop=(kc == KC - 1)),
                                     reads=[Bw[s], B_h], writes=[Bpg[p]])
                            for kc in range(KC):
                                S.op("pe", lambda e, kc=kc, p=p: e.matmul(pu[p][:, 0:w], wts[s][:, kc, 128:256], hT[:, kc, lo:lo + w],
                                                                        start=(kc == 0), stop=(kc == KC - 1)),
                                     reads=[Bw[s], B_h], writes=[Bpu[p]])
                            sg, B_sg = sgs[p], B_sgs[p]
                            S.op("act", lambda e, p=p: e.activation(out=sg[:, 0:w], in_=pg[p][:, 0:w], func=AF.Silu),
                                 reads=[Bpg[p]], writes=[B_sg])
                            S.op("dve", lambda e, p=p, j=j: e.tensor_tensor(out=actT[:, j, lo:lo + w], in0=sg[:, 0:w], in1=pu[p][:, 0:w], op=ALU.mult),
                                 reads=[B_sg, Bpu[p]], writes=[B_act[j]])
                    wos = [sb(es, "wfo%d" % i, [128, NJ, 128], BF16) for i in range(2)]
                    Bwo = [Buf("wfo%d" % i) for i in range(2)]
                    wov = w_fo_d.ap()[l].rearrange("(j p) n -> p j n", p=128)
                    for fo in range(KC):
                        s = fo % 2
                        S.dma("pool", wos[s][:, 0:11, :], wov[:, 0:11, fo * 128:(fo + 1) * 128], [], Bwo[s])
                        S.dma("pool", wos[s][:, 11:22, :], wov[:, 11:22, fo * 128:(fo + 1) * 128], [], Bwo[s])
                        for b in sbl:
                            t0, t1 = BLKS[b]
                            w = t1 - t0
                            lo = loc[b]
                            ci = 1 if b == 0 else 0
                            p = pc % 2
                            pc += 1
                            for j in range(NJ):
                                S.op("pe", lambda e, j=j, p=p: e.matmul(pg[p][:, 0:w], wos[s][:, j, :], actT[:, j, lo:lo + w],
                                                                      start=(j == 0), stop=(j == NJ - 1)),
                                     reads=[Bwo[s], B_act[j]], writes=[Bpg[p]])
                            S.op("dve", lambda e, p=p, fo=fo: e.scalar_tensor_tensor(
                                out=xT[:, fo, t0:t1], in0=pg[p][:, 0:w], scalar=mod[:, 40 + fo, ci:ci + 1], in1=xT[:, fo, t0:t1],
                                op0=ALU.mult, op1=ALU.add), reads=[Bpg[p], B_mod, XB[b]], writes=[XB[b]])
                    S.barrier()

    yv = yT_d.ap().rearrange("(kc p) t -> p kc t", p=128)
    for bi in range(1, len(BLKS)):
        t0, t1 = BLKS[bi]
        S.dma("sp", yv[:, :, t0 - NCTX:t1 - NCTX], xT[:, :, t0:t1], [XB[bi]], B_y)
    S.E["sp"].wait_ge(B_y.grp.sem, 16 * B_y.grp.cnt)
    es_top.close()
    return nc, S


def rope_table():
    n = np.arange(NLAT)
    r_pos = (n // 64).astype(np.float32)
    c_pos = (n % 64).astype(np.float32)
    inv_freq = (1.0 / (np.float32(10000.0) ** (np.arange(0, 16, 2, dtype=np.float32) / np.float32(16)))).astype(np.float32)
    ang_r = r_pos[:, None] * inv_freq[None, :]
    ang_c = c_pos[:, None] * inv_freq[None, :]
    return np.concatenate([np.cos(ang_r), np.cos(ang_c), np.sin(ang_r), np.sin(ang_c)], axis=1).astype(np.float32)


def make_in_maps(inp):
    pps = np.stack([pack_pp(inp, l) for l in range(2)], axis=0)
    maps = []
    gbc = np.zeros((2, 128, 192), np.float32)
    for l in range(2):
        gbc[l] = np.concatenate([inp["q_nope_g"][l], inp["k_nope_g"][l], inp["q_rope_g"][l], inp["k_rope_g"][l]])[None, :]
    rope = rope_table()
    ii = np.arange(64)
    masks = np.zeros((64, 2, 192), np.float32)
    lt = (ii[:, None] < ii[None, :]).astype(np.float32)
    le = (ii[:, None] <= ii[None, :]).astype(np.float32)
    masks[:, 0, 0:64], masks[:, 0, 64:128], masks[:, 0, 128:192] = lt, le, lt.T
    masks[:, 1, 0:64], masks[:, 1, 64:128], masks[:, 1, 128:192] = lt.T, le.T, lt
    f = lambda a: np.ascontiguousarray(np.asarray(a, np.float32))
    for b in range(8):
        xcat = np.concatenate([inp["ctx"][b], inp["x"][b]], axis=0)
        cT = np.zeros((128, 16), np.float32)
        cT[:, 0::2] = _cols(inp["c"][b])
        cT[:, 1::2] = _cols(inp["c_ctx"])
        maps.append({
            "xT": f(xcat.T), "cT": cT, "pp": pps,
            "ada_w": f(inp["ada_w"]), "w_in": f(inp["w_in"]), "w_out": f(inp["w_out"]),
            "w_ffn_in": f(inp["w_ffn_in"]), "w_ffn_out": f(inp["w_ffn_out"]),
            "decay_up": f(inp["decay_up"]), "icl_up": f(inp["icl_up"]), "gate_up": f(inp["gate_up"]), "masks": masks,
            "w_uq": f(inp["w_uq"]), "w_ukv": f(inp["w_ukv"]), "gbc": gbc, "rope": rope, "ident": np.eye(128, dtype=np.float32),
        })
    return maps


def kernel(**inputs):
    inp = {k: np.asarray(v) for k, v in inputs.items()}
    nc, _ = build_program()
    maps = make_in_maps(inp)
    res = run_bass_kernel_spmd(nc, maps, core_ids=list(range(8)))
    out = np.stack([np.ascontiguousarray(res.results[b]["yT"].T) for b in range(8)], axis=0)
    return out.astype(np.float32)
```

```python
import numpy as np
from contextlib import ExitStack
import concourse.bass as bass
import concourse.mybir as mybir
from concourse.bass_utils import run_bass_kernel_spmd

F32 = mybir.dt.float32
BF16 = mybir.dt.bfloat16
AF = mybir.ActivationFunctionType
ALU = mybir.AluOpType

D = 1024
KC = 8
T = 2304
NCTX = 256
NLAT = 2048
P_IN = 3232
A_IN = 1408
DFF = 2816
NJ = 22
EPS = 1e-6
BLKS = [(0, 256), (256, 768), (768, 1280), (1280, 1792), (1792, 2304)]


_GROUPS = {}
DBG = {}


class SemGroup:
    def __init__(self, name):
        self.name = name
        self.sem = None
        self.cnt = 0


class Buf:
    __slots__ = ("name", "w", "r", "grp")

    def __init__(self, name, grp=None):
        self.name = name
        self.w = None
        self.r = {}
        if grp is None:
            grp = _GROUPS.get(name)
            if grp is None:
                grp = _GROUPS[name] = SemGroup(name)
        self.grp = grp


class Sched:
    def __init__(self, nc, same_sync=True, waw_sync=True):
        self.nc = nc
        self.waw = waw_sync
        self.E = {"pe": nc.tensor, "act": nc.scalar, "dve": nc.vector, "pool": nc.gpsimd, "sp": nc.sync}
        self.sem = {e: nc.alloc_semaphore("sem_" + e) for e in ("pe", "act", "dve", "pool")}
        self.cnt = {e: 0 for e in self.sem}
        self.known = {e: {} for e in self.E}
        self.clock = {}
        self.semh = dict(self.sem)
        self.same = same_sync
        self.groups = []
        self.n_inst = 0
        self.rr = 0

    def _sync(self, eng, reads, writes, is_dma=False):
        need = {}
        kn = self.known[eng]

        def add(dep):
            k, v = dep
            if k == eng and not is_dma and (eng == "pe" or not self.same):
                return
            if kn.get(k, 0) >= v:
                return
            if need.get(k, 0) < v:
                need[k] = v

        for b in reads:
            if b.w is not None:
                add(b.w)
        for b in writes:
            if b.w is not None and (is_dma or self.waw or b.w[0] != eng):
                add(b.w)
            for k, v in b.r.items():
                if is_dma or self.waw or k != eng:
                    add((k, v))
        for k, v in need.items():
            if kn.get(k, 0) >= v:
                continue
            self.E[eng].wait_ge(self.semh[k], v)
            kn[k] = v
            ck = self.clock.get((k, v))
            if ck:
                for kk, vv in ck.items():
                    if kn.get(kk, 0) < vv:
                        kn[kk] = vv

    def op(self, eng, fn, reads=(), writes=()):
        self._sync(eng, reads, writes)
        inst = fn(self.E[eng])
        self.cnt[eng] += 1
        v = self.cnt[eng]
        inst.then_inc(self.sem[eng], 1)
        dep = (eng, v)
        self.clock[dep] = dict(self.known[eng])
        for b in reads:
            if b.r.get(eng, 0) < v:
                b.r[eng] = v
        for b in writes:
            b.w = dep
            b.r = {}
        self.n_inst += 1

    def dma(self, q, out, in_, reads, write):
        self._sync(q, reads, [write], is_dma=True)
        g = write.grp
        if g.sem is None:
            g.sem = self.nc.alloc_semaphore("ds_" + g.name)
            self.semh[("d", g.name)] = g.sem
            self.groups.append(g)
        g.cnt += 1
        self.E[q].dma_start(out=out, in_=in_).then_inc(g.sem, 16)
        k = ("d", g.name)
        v = 16 * g.cnt
        self.clock[(k, v)] = dict(self.known[q])
        for b in reads:
            if b.r.get(k, 0) < v:
                b.r[k] = v
        write.w = (k, v)
        write.r = {}
        self.n_inst += 1

    def dma_rr(self, queues, out, in_, reads, write):
        q = queues[self.rr % len(queues)]
        self.rr += 1
        self.dma(q, out, in_, reads, write)

    def barrier(self):
        for e in self.E:
            kn = self.known[e]
            for f in self.sem:
                if f == e and e == "sp":
                    continue
                v = self.cnt[f]
                if v > 0 and kn.get(f, 0) < v:
                    self.E[e].wait_ge(self.sem[f], v)
                    kn[f] = v
            for g in self.groups:
                k = ("d", g.name)
                v = 16 * g.cnt
                if v > 0 and kn.get(k, 0) < v:
                    self.E[e].wait_ge(g.sem, v)
                    kn[k] = v


PP = {}
_off = 0
for _n, _w in [("adab", 48), ("n1g", 8), ("n2g", 8), ("mu0", 11), ("mu1", 11), ("qng", 6), ("kvng", 2),
               ("conv", 6), ("lnxg", 3), ("lnxb", 3), ("w0", 6), ("a0", 6), ("kk", 3), ("ka", 3), ("rk", 3)]:
    PP[_n] = (_off, _w)
    _off += _w
NPP = _off


def _cols(v, width=128):
    v = np.asarray(v, np.float32)
    n = v.size // width
    return np.ascontiguousarray(v.reshape(n, width).T)


def pack_pp(inp, l):
    pp = np.zeros((128, NPP), np.float32)

    def put(name, arr):
        o, w = PP[name]
        assert arr.shape[1] == w, (name, arr.shape)
        pp[: arr.shape[0], o:o + w] = arr

    put("adab", _cols(inp["ada_b"][l]))
    put("n1g", _cols(inp["norm1_g"][l]))
    put("n2g", _cols(inp["norm2_g"][l]))
    put("mu0", _cols(inp["tshift_mu"][l, 0]))
    put("mu1", _cols(inp["tshift_mu"][l, 1]))
    put("qng", _cols(inp["q_norm_g"][l]))
    put("kvng", _cols(inp["kv_norm_g"][l]))
    cw = inp["conv_w"][l]
    put("conv", np.concatenate([_cols(cw[t]) for t in range(3)], axis=1))
    put("lnxg", _cols(inp["lnx_g"][l]))
    put("lnxb", _cols(inp["lnx_b"][l]))
    put("w0", np.concatenate([_cols(inp["decay_w0"][l, d]) for d in range(2)], axis=1))
    put("a0", np.concatenate([_cols(inp["icl_a0"][l, d]) for d in range(2)], axis=1))
    put("kk", _cols(inp["k_k"][l]))
    put("ka", _cols(inp["k_a"][l]))
    put("rk", _cols(inp["r_k"][l].reshape(-1)))
    return pp


def build_program(dbg=False, n_layers=2, stages=("mix", "rwkv", "mla", "ffn")):
    nc = bass.Bass("TRN2", target_bir_lowering=False)
    _GROUPS.clear()
    S = Sched(nc)
    skind = "ExternalOutput" if dbg else "Internal"

    def din(name, shape, dt=F32):
        return nc.dram_tensor(name, list(shape), dt, kind="ExternalInput")

    xT_d = din("xT", [D, T])
    cT_d = din("cT", [128, 16])
    pp_d = din("pp", [2, 128, NPP])
    ada_w_d = din("ada_w", [2, D, 6 * D])
    w_in_d = din("w_in", [2, D, P_IN])
    w_out_d = din("w_out", [2, D, D])
    w_fi_d = din("w_ffn_in", [2, D, 2 * DFF])
    w_fo_d = din("w_ffn_out", [2, DFF, D])
    dup_d = din("decay_up", [2, 2, 64, 384])
    iup_d = din("icl_up", [2, 2, 64, 384])
    gup_d = din("gate_up", [2, 128, 384])
    masks_d = din("masks", [64, 2, 192])
    fm_d = nc.dram_tensor("fm_s", [2, 64, 24, T], BF16, kind="Internal")
    tmBK_d = nc.dram_tensor("tmBK_s", [36, 64, 2, 2, 6, 64], BF16, kind="Internal")
    tmV_d = nc.dram_tensor("tmV_s", [36, 64, 6, 64], BF16, kind="Internal")
    gC_d = nc.dram_tensor("gC_s", [64, 2, 6, 36], F32, kind=skind)
    bon_d = nc.dram_tensor("bon_s", [384, T], F32, kind=skind)
    gat_d = nc.dram_tensor("gat_s", [384, T], F32, kind=skind)
    Y_d = nc.dram_tensor("Y_s", [2, T, 384], F32, kind=skind)
    B_fm, B_tm, B_gC, B_bg, B_Y = Buf("fm"), Buf("tm"), Buf("gC"), Buf("bg"), Buf("Ys")
    w_uq_d = din("w_uq", [2, 768, 576])
    w_ukv_d = din("w_ukv", [2, 256, 768])
    gbc_d = din("gbc", [2, 128, 192])
    rope_d = din("rope", [NLAT, 32])
    ident_d = din("ident", [128, 128])
    yT_d = nc.dram_tensor("yT", [D, NLAT], F32, kind="ExternalOutput")

    pA_d = nc.dram_tensor("pA_T", [A_IN, T], F32, kind=skind)
    pB_d = nc.dram_tensor("pB_T", [1024 + 32, T], F32, kind=skind)
    cat_d = nc.dram_tensor("cat_T", [D, T], BF16, kind=skind)
    B_pA = Buf("pA")
    B_pB = Buf("pB")
    B_cat = Buf("cat")
    B_y = Buf("yT")

    es_top = ExitStack()

    uid = [0]

    def sb(es, name, shape, dt):
        uid[0] += 1
        return es.enter_context(nc.sbuf_tensor("s%d_%s" % (uid[0], name), list(shape), dt))

    def ps(es, name, shape, dt=F32):
        uid[0] += 1
        return es.enter_context(nc.psum_tensor("p%d_%s" % (uid[0], name), list(shape), dt))

    xT = sb(es_top, "xT", [128, KC, T], F32)
    XB = [Buf("xb%d" % i) for i in range(len(BLKS))]
    ones_bf = sb(es_top, "ones_bf", [128, 128], BF16)
    B_const = Buf("const")
    ppt = sb(es_top, "ppt", [128, 2, NPP], F32)
    B_pp = Buf("pp")
    mod = sb(es_top, "mod", [128, 48, 2], F32)
    gs1 = sb(es_top, "gs1", [128, KC, 2], F32)
    gs2 = sb(es_top, "gs2", [128, KC, 2], F32)
    cmix = sb(es_top, "cmix", [128, 11], F32)
    B_mod = Buf("mod")
    scT = sb(es_top, "scT", [128, KC, 2], BF16)
    B_scT = Buf("scT")

    S.op("dve", lambda e: e.memset(ones_bf[:], 1.0), writes=[B_const])
    ident_f = sb(es_top, "ident_f", [128, 128], F32)
    ident_b = sb(es_top, "ident_b", [128, 128], BF16)
    B_id = Buf("ident")
    S.dma("sp", ident_f[:], ident_d.ap(), [], B_id)
    S.op("dve", lambda e: e.tensor_copy(out=ident_b[:], in_=ident_f[:]), reads=[B_id], writes=[B_const])
    for i, (t0, t1) in enumerate(BLKS):
        S.dma("sp", xT[:, :, t0:t1], xT_d.ap().rearrange("(kc p) t -> p kc t", p=128)[:, :, t0:t1], [], XB[i])
    S.dma("sp", ppt[:], pp_d.ap().rearrange("l p n -> p l n"), [], B_pp)
    with nc.sbuf_tensor("cTt", [128, 16], F32) as cTt:
        B_c = Buf("cT")
        S.dma("sp", cTt[:], cT_d.ap(), [], B_c)
        S.op("act", lambda e: e.activation(out=scT[:].rearrange("p a b -> p (a b)"), in_=cTt[:], func=AF.Silu),
             reads=[B_c], writes=[B_scT])
        S.barrier()

    def ppc(l, name, j=0, n=1, parts=128):
        o, w = PP[name]
        return ppt[0:parts, l, o + j:o + j + n]

    def norm_mod(es, bi, gs, sh_chunk0, dst, dst_buf, dst_t0, tmp, B_tmp, sq, B_sq, ps_s, B_pss, rstd, B_rstd):
        t0, t1 = BLKS[bi]
        w = t1 - t0
        ci = 1 if bi == 0 else 0
        S.op("act", lambda e: e.activation(out=sq[:, :, 0:w], in_=xT[:, :, t0:t1], func=AF.Square),
             reads=[XB[bi]], writes=[B_sq])
        for kc in range(KC):
            S.op("pe", lambda e, kc=kc: e.matmul(ps_s[:, 0:w], ones_bf[:], sq[:, kc, 0:w], start=(kc == 0), stop=(kc == KC - 1)),
                 reads=[B_sq, B_const], writes=[B_pss])
        S.op("act", lambda e: e.activation(out=rstd[:, 0:w], in_=ps_s[:, 0:w], func=AF.Sqrt, bias=EPS, scale=1.0 / D),
             reads=[B_pss], writes=[B_rstd])
        S.op("dve", lambda e: e.reciprocal(out=rstd[:, 0:w], in_=rstd[:, 0:w]), reads=[B_rstd], writes=[B_rstd])
        for kc in range(KC):
            tm_, Btm_ = tmp[kc % 2], B_tmp[kc % 2]
            S.op("dve", lambda e, kc=kc: e.tensor_tensor(out=tm_[:, 0:w], in0=xT[:, kc, t0:t1], in1=rstd[:, 0:w], op=ALU.mult),
                 reads=[XB[bi], B_rstd], writes=[Btm_])
            S.op("act", lambda e, kc=kc: e.activation(out=dst[:, kc, dst_t0:dst_t0 + w], in_=tm_[:, 0:w], func=AF.Identity,
                                                      scale=gs[:, kc, ci:ci + 1], bias=mod[:, sh_chunk0 + kc, ci:ci + 1]),
                 reads=[Btm_, B_mod], writes=[dst_buf])

    ATTN_SCALE = 96.0 ** -0.5

    def headnorm(src, H, n, gain, dst, scr, B_src, B_dst, B_scr):
        sqt, ssq = scr
        S.op("dve", lambda e: e.tensor_tensor(out=sqt[:, 0:H, 0:n], in0=src, in1=src, op=ALU.mult), reads=[B_src], writes=[B_scr])
        yield
        S.op("dve", lambda e: e.tensor_reduce(out=ssq[:, 0:H], in_=sqt[:, 0:H, 0:n], axis=mybir.AxisListType.X, op=ALU.add),
             reads=[B_scr], writes=[B_scr])
        yield
        S.op("act", lambda e: e.activation(out=ssq[:, 0:H], in_=ssq[:, 0:H], func=AF.Sqrt, bias=EPS, scale=1.0 / n),
             reads=[B_scr], writes=[B_scr])
        yield
        S.op("dve", lambda e: e.reciprocal(out=ssq[:, 0:H], in_=ssq[:, 0:H]), reads=[B_scr], writes=[B_scr])
        yield
        S.op("dve", lambda e: e.tensor_tensor(out=sqt[:, 0:H, 0:n], in0=src, in1=ssq[:, 0:H].unsqueeze(2).to_broadcast([128, H, n]), op=ALU.mult),
             reads=[B_src, B_scr], writes=[B_scr])
        yield
        S.op("dve", lambda e: e.tensor_tensor(out=dst, in0=sqt[:, 0:H, 0:n], in1=gain.unsqueeze(1).to_broadcast([128, H, n]), op=ALU.mult),
             reads=[B_scr, B_const], writes=[B_dst])
        yield

    CDEC = 0.606531
    GN_EPS = 64e-5
    HT = 1152
    TBH = [(0, 512), (512, 1024), (1024, 1152)]

    def rwkv_prep(l):
        with ExitStack() as es:
            dup = sb(es, "dup", [64, 2, 384], BF16)
            iup = sb(es, "iup", [64, 2, 384], BF16)
            gup = sb(es, "gup", [128, 384], BF16)
            bd = sb(es, "bd", [128, 128], BF16)
            B_w = Buf("rw_w")
            S.dma("pool", dup[:], dup_d.ap()[l].rearrange("d k n -> k d n"), [], B_w)
            S.dma("pool", iup[:], iup_d.ap()[l].rearrange("d k n -> k d n"), [], B_w)
            S.dma("pool", gup[:], gup_d.ap()[l], [], B_w)
            S.op("dve", lambda e: e.memset(bd[:], 0.0), writes=[B_w])
            S.op("dve", lambda e: e.memset(bd[0:64, 0:64], 1.0), writes=[B_w])
            S.op("dve", lambda e: e.memset(bd[64:128, 64:128], 1.0), writes=[B_w])
            tw = sb(es, "tw", [64, T], BF16)
            al = sb(es, "al", [64, T], BF16)
            sgl = sb(es, "sgl", [128, T], BF16)
            B_lo = Buf("lo")
            with ExitStack() as es0:
                lin = sb(es0, "lin", [128, T], F32)
                B_lin = Buf("lin")
                S.dma("sp", lin[0:64, :], pA_d.ap()[1152:1216, :], [B_pA], B_lin)
                S.op("act", lambda e: e.activation(out=tw[:], in_=lin[0:64, :], func=AF.Tanh), reads=[B_lin], writes=[B_lo])
                S.dma("sp", lin[0:64, :], pA_d.ap()[1216:1280, :], [B_pA], B_lin)
                S.op("act", lambda e: e.activation(out=al[:], in_=lin[0:64, :], func=AF.Copy), reads=[B_lin], writes=[B_lo])
                S.dma("sp", lin[:, :], pA_d.ap()[1280:1408, :], [B_pA], B_lin)
                S.op("act", lambda e: e.activation(out=sgl[:], in_=lin[:, :], func=AF.Sigmoid), reads=[B_lin], writes=[B_lo])
                S.barrier()
            names = ("r", "k", "v", "kk", "asum", "xs")
            tl = {n: sb(es, "rp_" + n, [128, HT], F32) for n in names}
            Bt = {n: Buf("rp_" + n) for n in names}
            tld, Btd = [], []
            for d_ in range(2):
                td = {n: sb(es, "rp_%s%d" % (n, d_), [128, HT], F32) for n in ("sw", "ai", "P", "E", "F", "x1", "x2")}
                bd_ = {n: Buf("rp_%s%d" % (n, d_)) for n in ("sw", "ai", "P", "E", "F", "x1", "x2")}
                td["g1"], bd_["g1"] = td["P"], bd_["P"]
                td["g2"], bd_["g2"] = td["sw"], bd_["sw"]
                tld.append(td)
                Btd.append(bd_)
            ob = [sb(es, "rp_ob%d" % i, [128, HT], BF16) for i in range(3)]
            Bob = [Buf("rp_ob%d" % i) for i in range(3)]
            sqb = sb(es, "rp_sqb", [128, HT], BF16)
            B_sqb = Buf("rp_sqb")
            base = [sb(es, "rp_base%d" % i, [128, 18], F32) for i in range(2)]
            gct = [sb(es, "rp_gct%d" % i, [128, 18], F32) for i in range(2)]
            B_base = [Buf("rp_base%d" % i) for i in range(2)]
            B_gct = [Buf("rp_gct%d" % i) for i in range(2)]
            pm = [ps(es, "rp_pm%d" % i, [128, 512]) for i in range(4)]
            Bpm = [Buf("rp_pm%d" % i) for i in range(4)]
            ptrs = [ps(es, "rp_ptr%d" % i, [64, 1024], BF16) for i in range(2)]
            B_ptrs = [Buf("rp_ptr%d" % i) for i in range(2)]
            stgs = [sb(es, "rp_stg%d" % i, [64, 8, 128], BF16) for i in range(3)]
            B_stgs = [Buf("rp_stg%d" % i) for i in range(3)]
            st = {"p": 0, "ob": 0, "c0": 0, "hf": 0, "tr": 0}

            def mm_full(lhsT, rhs_tile, B_rhs, dst, B_dst, func, bias=None, K=64):
                for (t0, t1) in TBH:
                    w = t1 - t0
                    p = st["p"] % 4
                    st["p"] += 1
                    S.op("pe", lambda e, p=p: e.matmul(pm[p][:, 0:w], lhsT, rhs_tile[0:K, st["c0"] + t0:st["c0"] + t1], start=True, stop=True),
                         reads=[B_rhs, B_w], writes=[Bpm[p]])
                    if bias is None:
                        S.op("act", lambda e, p=p: e.activation(out=dst[:, t0:t1], in_=pm[p][:, 0:w], func=func), reads=[Bpm[p]], writes=[B_dst])
                    else:
                        S.op("act", lambda e, p=p: e.activation(out=dst[:, t0:t1], in_=pm[p][:, 0:w], func=func, bias=bias, scale=1.0),
                             reads=[Bpm[p], B_pp], writes=[B_dst])

            def tt(eng, out, a_, b_, op, rd, wr):
                S.op(eng, lambda e: e.tensor_tensor(out=out, in0=a_, in1=b_, op=op), reads=rd, writes=wr)

            def to_tm(src_bf, B_src, dst_fn):
                for g0 in range(0, 18, 8):
                    n = min(8, 18 - g0)
                    ptr, B_ptr = ptrs[st["tr"] % 2], B_ptrs[st["tr"] % 2]
                    stg, B_stg = stgs[st["tr"] % 3], B_stgs[st["tr"] % 3]
                    st["tr"] += 1
                    for i in range(n):
                        c = g0 + i
                        S.op("pe", lambda e, i=i, c=c: e.transpose(ptr[0:64, i * 128:(i + 1) * 128], src_bf[:, c * 64:(c + 1) * 64], ident_b[:]),
                             reads=[B_src, B_const], writes=[B_ptr])
                    S.op("act", lambda e: e.activation(out=stg[:, 0:n, :], in_=ptr[0:64, 0:n * 128].rearrange("p (a b) -> p a b", b=128), func=AF.Copy),
                         reads=[B_ptr], writes=[B_stg])
                    S.dma("sp", dst_fn(st["hf"] * 18 + g0, n), stg[:, 0:n, :], [B_stg], B_tm)

            def out_bf(src_fn):
                i = st["ob"] % 3
                st["ob"] += 1
                src_fn(ob[i][:], Bob[i])
                return ob[i], Bob[i]

            def fm_store(o, Bo, d, hp, q, c0):
                for hl in range(2):
                    S.dma("sp", fm_d.ap()[d, :, (2 * hp + hl) * 4 + q, c0:c0 + HT], o[hl * 64:(hl + 1) * 64, :], [Bo], B_fm)

            def gen_dir(d, hp, hf, c0):
                T_, B_ = tld[d], Btd[d]
                r_, k_, kk_ = tl["r"], tl["k"], tl["kk"]
                sw, ai, P, E, F_, g1, g2, x1, x2 = (T_[n] for n in ("sw", "ai", "P", "E", "F", "g1", "g2", "x1", "x2"))
                mm_full(dup[:, d, hp * 128:(hp + 1) * 128], tw, B_lo, sw, B_["sw"], AF.Sigmoid, ppc(l, "w0", d * 3 + hp))
                yield
                mm_full(iup[:, d, hp * 128:(hp + 1) * 128], al, B_lo, ai, B_["ai"], AF.Sigmoid, ppc(l, "a0", d * 3 + hp))
                yield
                S.op("dve", lambda e: e.tensor_tensor_scan(out=P[:], data0=sw[:], data1=sw[:], initial=0.0, op0=ALU.add, op1=ALU.max),
                     reads=[B_["sw"]], writes=[B_["P"]])
                yield
                pend = P[:].rearrange("p (c t) -> p c t", t=64)[:, :, 63]
                bs, gc, Bb, Bg = base[d], gct[d], B_base[d], B_gct[d]
                if d == 0:
                    S.op("dve", lambda e: e.memset(bs[:, 0:1], 0.0), writes=[Bb])
                    S.op("dve", lambda e: e.tensor_copy(out=bs[:, 1:18], in_=pend[:, 0:17]), reads=[B_["P"]], writes=[Bb])
                    yield
                    S.op("dve", lambda e: e.tensor_tensor(out=gc[:], in0=pend, in1=bs[:], op=ALU.subtract), reads=[B_["P"], Bb], writes=[Bg])
                else:
                    S.op("dve", lambda e: e.tensor_copy(out=gc[:, 0:1], in_=pend[:, 0:1]), reads=[B_["P"]], writes=[Bg])
                    S.op("dve", lambda e: e.tensor_tensor(out=gc[:, 1:18], in0=pend[:, 1:18], in1=pend[:, 0:17], op=ALU.subtract), reads=[B_["P"]], writes=[Bg])
                    yield
                    S.op("dve", lambda e: e.tensor_copy(out=bs[:], in_=pend), reads=[B_["P"]], writes=[Bb])
                yield
                S.op("act", lambda e: e.activation(out=gc[:], in_=gc[:], func=AF.Exp, scale=-CDEC), reads=[Bg], writes=[Bg])
                for hl in range(2):
                    S.dma("sp", gC_d.ap()[:, d, 2 * hp + hl, hf * 18:(hf + 1) * 18], gc[hl * 64:(hl + 1) * 64, :], [Bg], B_gC)
                tt("dve", E[:].rearrange("p (c t) -> p c t", t=64), P[:].rearrange("p (c t) -> p c t", t=64),
                   bs[:].unsqueeze(2).to_broadcast([128, 18, 64]), ALU.subtract, [B_["P"], Bb], [B_["E"]])
                yield
                tt("pool", F_[:], E[:], sw[:], ALU.subtract, [B_["E"], B_["sw"]], [B_["F"]])
                yield
                if d == 0:
                    li, si, le, se = E, -CDEC, F_, -CDEC
                    Bli, Ble = B_["E"], B_["F"]
                else:
                    li, si, le, se = F_, CDEC, E, CDEC
                    Bli, Ble = B_["F"], B_["E"]
                S.op("act", lambda e: e.activation(out=g1[:], in_=li[:], func=AF.Exp, scale=si), reads=[Bli], writes=[B_["g1"]])
                yield
                S.op("act", lambda e: e.activation(out=g2[:], in_=li[:], func=AF.Exp, scale=-si), reads=[Bli], writes=[B_["g2"]])
                yield
                S.op("act", lambda e: e.activation(out=x1[:], in_=le[:], func=AF.Exp, scale=se), reads=[Ble], writes=[B_["x1"]])
                yield
                o, Bo = out_bf(lambda o, B: S.op("dve", lambda e: e.scalar_tensor_tensor(out=o, in0=kk_[:], scalar=-1.0, in1=x1[:], op0=ALU.mult, op1=ALU.mult),
                                                 reads=[Bt["kk"], B_["x1"]], writes=[B]))
                fm_store(o, Bo, d, hp, 0, c0)
                yield
                o, Bo = out_bf(lambda o, B: tt("pool", o, r_[:], g1[:], ALU.mult, [Bt["r"], B_["g1"]], [B]))
                fm_store(o, Bo, d, hp, 1, c0)
                yield
                tt("dve", x2[:], kk_[:], ai[:], ALU.mult, [Bt["kk"], B_["ai"]], [B_["x2"]])
                yield
                o, Bo = out_bf(lambda o, B: tt("dve", o, x2[:], g2[:], ALU.mult, [B_["x2"], B_["g2"]], [B]))
                fm_store(o, Bo, d, hp, 2, c0)
                to_tm(o, Bo, lambda g0, n: tmBK_d.ap()[g0:g0 + n, :, d, 0, 2 * hp:2 * hp + 2, :].rearrange("c t h k -> t c (h k)"))
                yield
                S.op("dve", lambda e: e.tensor_scalar(out=x2[:], in0=ai[:], scalar1=-1.0, scalar2=ppc(l, "ka", hp), op0=ALU.add, op1=ALU.mult),
                     reads=[B_["ai"], B_pp], writes=[B_["x2"]])
                yield
                S.op("dve", lambda e: e.scalar_tensor_tensor(out=x2[:], in0=x2[:], scalar=1.0, in1=k_[:], op0=ALU.add, op1=ALU.mult),
                     reads=[B_["x2"], Bt["k"]], writes=[B_["x2"]])
                yield
                o, Bo = out_bf(lambda o, B: tt("pool", o, x2[:], g2[:], ALU.mult, [B_["x2"], B_["g2"]], [B]))
                fm_store(o, Bo, d, hp, 3, c0)
                to_tm(o, Bo, lambda g0, n: tmBK_d.ap()[g0:g0 + n, :, d, 1, 2 * hp:2 * hp + 2, :].rearrange("c t h k -> t c (h k)"))
                yield

            for it_ in range(6):
                hp, hf = it_ // 2, it_ % 2
                c0 = hf * HT
                st["c0"], st["hf"] = c0, hf
                r_, k_, v_, kk_ = tl["r"], tl["k"], tl["v"], tl["kk"]
                S.dma("sp", r_[:], pA_d.ap()[hp * 128:(hp + 1) * 128, c0:c0 + HT], [B_pA], Bt["r"])
                S.dma("sp", k_[:], pA_d.ap()[384 + hp * 128:384 + (hp + 1) * 128, c0:c0 + HT], [B_pA], Bt["k"])
                S.dma("sp", v_[:], pA_d.ap()[768 + hp * 128:768 + (hp + 1) * 128, c0:c0 + HT], [B_pA], Bt["v"])
                vb, Bvb = out_bf(lambda o, B: S.op("act", lambda e: e.activation(out=o, in_=v_[:], func=AF.Copy), reads=[Bt["v"]], writes=[B]))
                to_tm(vb, Bvb, lambda g0, n: tmV_d.ap()[g0:g0 + n, :, 2 * hp:2 * hp + 2, :].rearrange("c t h k -> t c (h k)"))
                S.op("act", lambda e: e.activation(out=kk_[:], in_=k_[:], func=AF.Identity, scale=ppc(l, "kk", hp), bias=0.0),
                     reads=[Bt["k"], B_pp], writes=[Bt["kk"]])
                S.op("act", lambda e: e.activation(out=sqb[:], in_=kk_[:], func=AF.Square), reads=[Bt["kk"]], writes=[B_sqb])
                for (t0, t1) in TBH:
                    w = t1 - t0
                    p = st["p"] % 4
                    st["p"] += 1
                    S.op("pe", lambda e, p=p: e.matmul(pm[p][:, 0:w], bd[:], sqb[:, t0:t1], start=True, stop=True), reads=[B_sqb, B_w], writes=[Bpm[p]])
                    S.op("act", lambda e, p=p: e.activation(out=tl["xs"][:, t0:t1], in_=pm[p][:, 0:w], func=AF.Sqrt, bias=1e-12, scale=1.0),
                         reads=[Bpm[p]], writes=[Bt["xs"]])
                S.op("dve", lambda e: e.reciprocal(out=tl["xs"][:], in_=tl["xs"][:]), reads=[Bt["xs"]], writes=[Bt["xs"]])
                tt("dve", kk_[:], kk_[:], tl["xs"][:], ALU.mult, [Bt["kk"], Bt["xs"]], [Bt["kk"]])
                run_streams([gen_dir(0, hp, hf, c0), gen_dir(1, hp, hf, c0)])
                tt("dve", tl["asum"][:], tld[0]["x2"][:], tld[1]["x2"][:], ALU.add, [Btd[0]["x2"], Btd[1]["x2"]], [Bt["asum"]])
                S.op("dve", lambda e: e.scalar_tensor_tensor(out=sqb[:], in0=tl["asum"][:], scalar=ppc(l, "rk", hp), in1=r_[:], op0=ALU.mult, op1=ALU.mult),
                     reads=[Bt["asum"], Bt["r"], B_pp], writes=[B_sqb])
                for (t0, t1) in TBH:
                    w = t1 - t0
                    p = st["p"] % 4
                    st["p"] += 1
                    S.op("pe", lambda e, p=p: e.matmul(pm[p][:, 0:w], bd[:], sqb[:, t0:t1], start=True, stop=True), reads=[B_sqb, B_w], writes=[Bpm[p]])
                    S.op("dve", lambda e, p=p: e.tensor_tensor(out=tl["xs"][:, t0:t1], in0=pm[p][:, 0:w], in1=v_[:, t0:t1], op=ALU.mult),
                         reads=[Bpm[p], Bt["v"]], writes=[Bt["xs"]])
                S.dma("sp", bon_d.ap()[hp * 128:(hp + 1) * 128, c0:c0 + HT], tl["xs"][:], [Bt["xs"]], B_bg)
                mm_full(gup[:, hp * 128:(hp + 1) * 128], sgl, B_lo, tl["asum"], Bt["asum"], AF.Copy, None, K=128)
                S.dma("sp", gat_d.ap()[hp * 128:(hp + 1) * 128, c0:c0 + HT], tl["asum"][:], [Bt["asum"]], B_bg)
            S.barrier()

    def run_streams(streams):
        streams = [iter(x) for x in streams]
        while streams:
            for x in list(streams):
                try:
                    next(x)
                except StopIteration:
                    streams.remove(x)

    def rwkv_scan(l):
        with ExitStack() as es:
            masks = sb(es, "masks", [64, 2, 192], F32)
            gC = sb(es, "gC", [64, 2, 6, 36], F32)
            B_ld0 = Buf("sc_ld0")
            S.dma("sp", masks[:], masks_d.ap(), [], B_ld0)
            S.dma("sp", gC[:], gC_d.ap(), [B_gC], B_ld0)
            Hf = sb(es, "Hf", [64, 2, 6, 64], F32)
            Hb = [sb(es, "Hb%d" % i, [64, 2, 6, 64], BF16) for i in range(2)]
            B_Hf = [Buf("Hf%d" % d) for d in range(2)]
            B_Hb = [[Buf("Hb%d_%d" % (i, d)) for d in range(2)] for i in range(2)]
            S.op("dve", lambda e: e.memset(Hf[:], 0.0), writes=B_Hf)
            S.op("dve", lambda e: e.memset(Hb[0][:], 0.0), writes=B_Hb[0])
            S.op("dve", lambda e: e.memset(Hb[1][:], 0.0), writes=B_Hb[1])
            NB = 3
            fmt = [[sb(es, "fmt%d_%d" % (i, d), [64, 6, 4, 64], BF16) for d in range(2)] for i in range(NB)]
            tbk = [[sb(es, "tbk%d_%d" % (i, d), [64, 2, 6, 64], BF16) for d in range(2)] for i in range(NB)]
            tv = [[sb(es, "tv%d_%d" % (i, d), [64, 6, 64], BF16) for d in range(2)] for i in range(NB)]
            B_in = [[Buf("scin%d_%d" % (i, d)) for d in range(2)] for i in range(NB)]
            SC1 = [[sb(es, "SC1_%d_%d" % (p, d), [64, 6, 128], BF16) for d in range(2)] for p in range(2)]
            SC2 = [[sb(es, "SC2_%d_%d" % (p, d), [64, 6, 128], BF16) for d in range(2)] for p in range(2)]
            TTb = [[sb(es, "TTb_%d_%d" % (p, d), [64, 6, 64], BF16) for d in range(2)] for p in range(2)]
            B_S1 = [[[Buf("sc_S1_%d_%d_%d" % (p, d, hf)) for hf in range(2)] for d in range(2)] for p in range(2)]
            B_S2 = [[[Buf("sc_S2_%d_%d_%d" % (p, d, hf)) for hf in range(2)] for d in range(2)] for p in range(2)]
            B_TT = [[Buf("sc_TT_%d_%d" % (p, d)) for d in range(2)] for p in range(2)]
            PM = [sb(es, "PM_%d" % d, [64, 6, 128], BF16) for d in range(2)]
            B_PM = [[Buf("sc_PM_%d_%d" % (d, hf)) for hf in range(2)] for d in range(2)]
            ZXb = [sb(es, "ZXb_%d" % d, [64, 6, 64], BF16) for d in range(2)]
            Ub = [sb(es, "Ub_%d" % d, [64, 6, 64], BF16) for d in range(2)]
            Ysb = [sb(es, "Ysb_%d" % d, [64, 384], F32) for d in range(2)]
            Bc = {n: [Buf("sc_%s_%d" % (n, d)) for d in range(2)] for n in ("ZXb", "Ub", "Ysb")}
            pG1 = [ps(es, "pG1_%d" % d, [64, 1024]) for d in range(2)]
            pG3 = [ps(es, "pG3_%d" % d, [64, 512]) for d in range(2)]
            pZ = ps(es, "pZ", [64, 512])
            pY = ps(es, "pY", [64, 512])
            B_pG1 = [Buf("pG1_%d" % d) for d in range(2)]
            B_pG3 = [Buf("pG3_%d" % d) for d in range(2)]
            B_pZ, B_pY = Buf("pZ"), Buf("pY")

            def chunk_of(i, d):
                if d == 0:
                    return i
                return 3 - i if i < 4 else 39 - i

            def loads(i):
                sl = i % NB
                for d in range(2):
                    c = chunk_of(i, d)
                    cs = c * 64
                    Bi = B_in[sl][d]
                    S.dma("sp", fmt[sl][d][:].rearrange("p h q t -> p (h q) t"), fm_d.ap()[d, :, :, cs:cs + 64], [B_fm], Bi)
                    S.dma("sp", tbk[sl][d][:], tmBK_d.ap()[c, :, d], [B_tm], Bi)
                    S.dma("sp", tv[sl][d][:], tmV_d.ap()[c], [B_tm], Bi)

            def gen_pre(i, d):
                sl, par = i % NB, i % 2
                F_, Bi = fmt[sl][d], B_in[sl][d]
                s1, s2, ttb = SC1[par][d], SC2[par][d], TTb[par][d]
                bS1, bS2, bTT, bPM = B_S1[par][d], B_S2[par][d], B_TT[par][d], B_PM[d]
                g1, g3, bg1, bg3 = pG1[d], pG3[d], B_pG1[d], B_pG3[d]
                m1 = masks[:, d, 0:128].unsqueeze(1).to_broadcast([64, 6, 128])
                m3 = masks[:, d, 128:192].unsqueeze(1).to_broadcast([64, 6, 64])
                g1v = g1[:, 0:768].rearrange("p (a b) -> p a b", b=128)
                g3v = g3[:, 0:384].rearrange("p (a b) -> p a b", b=64)
                for h in range(6):
                    S.op("pe", lambda e, h=h: e.matmul(g1[:, h * 128:(h + 1) * 128], F_[:, h, 2, :], F_[:, h, 0:2, :].rearrange("p a b -> p (a b)"), start=True, stop=True),
                         reads=[Bi], writes=[bg1])
                for (ha, hb) in ((0, 4), (4, 6)):
                    S.op("dve", lambda e, ha=ha, hb=hb: e.tensor_tensor(out=s1[:, ha:hb, :], in0=g1v[:, ha:hb, :], in1=m1[:, ha:hb, :], op=ALU.mult), reads=[bg1, B_ld0], writes=[bS1[0 if ha == 0 else 1]])
                yield
                for h in range(6):
                    S.op("pe", lambda e, h=h: e.matmul(g1[:, h * 128:(h + 1) * 128], F_[:, h, 3, :], F_[:, h, 0:2, :].rearrange("p a b -> p (a b)"), start=True, stop=True),
                         reads=[Bi], writes=[bg1])
                for (ha, hb) in ((0, 4), (4, 6)):
                    S.op("dve", lambda e, ha=ha, hb=hb: e.tensor_tensor(out=s2[:, ha:hb, :], in0=g1v[:, ha:hb, :], in1=m1[:, ha:hb, :], op=ALU.mult), reads=[bg1, B_ld0], writes=[bS2[0 if ha == 0 else 1]])
                yield
                for h in range(6):
                    S.op("pe", lambda e, h=h: e.matmul(g3[:, h * 64:(h + 1) * 64], F_[:, h, 0, :], F_[:, h, 2, :], start=True, stop=True),
                         reads=[Bi], writes=[bg3])
                S.op("dve", lambda e: e.tensor_tensor(out=PM[d][:, :, 0:64], in0=g3v, in1=m3, op=ALU.mult), reads=[bg3, B_ld0], writes=bPM)
                S.op("act", lambda e: e.activation(out=PM[d][:, :, 64:128], in_=s1[:, :, 0:64], func=AF.Copy), reads=bS1, writes=bPM)
                S.op("dve", lambda e: e.tensor_tensor(out=ttb[:], in0=s1[:, :, 0:64], in1=ident_f[0:64, 0:64].unsqueeze(1).to_broadcast([64, 6, 64]), op=ALU.add),
                     reads=bS1 + [B_id], writes=[bTT])
                yield
                for it in range(5):
                    lastit = it == 4
                    for h in range(6):
                        S.op("pe", lambda e, h=h: e.matmul(g1[:, h * 128:h * 128 + 64], PM[d][:, h, 64:128], PM[d][:, h, 0:64], start=True, stop=True),
                             reads=[bPM[h // 4]], writes=[bg1])
                        if not lastit:
                            S.op("pe", lambda e, h=h: e.matmul(g1[:, h * 128 + 64:(h + 1) * 128], PM[d][:, h, 0:64], PM[d][:, h, 64:128], start=True, stop=True),
                                 reads=[bPM[h // 4]], writes=[bg1])
                    wc = 64 if lastit else 128
                    for (ha, hb) in ((0, 4), (4, 6)):
                        S.op("act", lambda e, ha=ha, hb=hb: e.activation(out=PM[d][:, ha:hb, 0:wc], in_=g1v[:, ha:hb, 0:wc], func=AF.Copy), reads=[bg1], writes=[bPM[0 if ha == 0 else 1]])
                    yield
                    for h in range(6):
                        S.op("pe", lambda e, h=h: e.matmul(g3[:, h * 64:(h + 1) * 64], PM[d][:, h, 0:64], ttb[:, h, :], start=True, stop=True),
                             reads=[bPM[h // 4], bTT], writes=[bg3])
                    S.op("dve", lambda e: e.tensor_tensor(out=ttb[:], in0=ttb[:], in1=g3v, op=ALU.add), reads=[bTT, bg3], writes=[bTT])
                    yield

            def gen_chain(i):
                sl, par = i % NB, i % 2
                hb_old, hb_new = Hb[i % 2], Hb[(i + 1) % 2]
                for d in range(2):
                    c = chunk_of(i, d)
                    cs = c * 64
                    F_, BK, V_, Bi = fmt[sl][d], tbk[sl][d], tv[sl][d], B_in[sl][d]
                    s1, s2, ttb = SC1[par][d], SC2[par][d], TTb[par][d]
                    bS1, bS2, bTT = B_S1[par][d], B_S2[par][d], B_TT[par][d]
                    bHo, bHn = B_Hb[i % 2][d], B_Hb[(i + 1) % 2][d]
                    for h in range(6):
                        S.op("pe", lambda e, h=h: e.matmul(pZ[:, h * 64:(h + 1) * 64], s2[:, h, 0:64], V_[:, h, :], start=True, stop=False),
                             reads=[bS2[h // 4], Bi], writes=[B_pZ])
                        S.op("pe", lambda e, h=h: e.matmul(pZ[:, h * 64:(h + 1) * 64], F_[:, h, 0, :], hb_old[:, d, h, :], start=False, stop=True),
                             reads=[Bi, bHo], writes=[B_pZ])
                    for h in range(6):
                        S.op("pe", lambda e, h=h: e.matmul(pY[:, h * 64:(h + 1) * 64], F_[:, h, 1, :], hb_old[:, d, h, :], start=(h == 0), stop=False, skip_group_check=True),
                             reads=[Bi, bHo], writes=[B_pY])
                        S.op("pe", lambda e, h=h: e.matmul(pY[:, h * 64:(h + 1) * 64], s2[:, h, 64:128], V_[:, h, :], start=False, stop=False, skip_group_check=True),
                             reads=[bS2[h // 4], Bi], writes=[B_pY])
                    S.op("act", lambda e: e.activation(out=ZXb[d][:].rearrange("p a b -> p (a b)"), in_=pZ[:, 0:384], func=AF.Copy), reads=[B_pZ], writes=[Bc["ZXb"][d]])
                    yield
                    for h in range(6):
                        S.op("pe", lambda e, h=h: e.matmul(pZ[:, h * 64:(h + 1) * 64], ttb[:, h, :], ZXb[d][:, h, :], start=True, stop=True),
                             reads=[bTT, Bc["ZXb"][d]], writes=[B_pZ])
                    S.op("act", lambda e: e.activation(out=Ub[d][:].rearrange("p a b -> p (a b)"), in_=pZ[:, 0:384], func=AF.Copy), reads=[B_pZ], writes=[Bc["Ub"][d]])
                    yield
                    for h in range(6):
                        S.op("pe", lambda e, h=h: e.matmul(pZ[:, h * 64:(h + 1) * 64], BK[:, 0, h, :], Ub[d][:, h, :], start=True, stop=False),
                             reads=[Bi, Bc["Ub"][d]], writes=[B_pZ])
                        S.op("pe", lambda e, h=h: e.matmul(pZ[:, h * 64:(h + 1) * 64], BK[:, 1, h, :], V_[:, h, :], start=False, stop=True),
                             reads=[Bi], writes=[B_pZ])
                    for h in range(6):
                        S.op("pe", lambda e, h=h: e.matmul(pY[:, h * 64:(h + 1) * 64], s1[:, h, 64:128], Ub[d][:, h, :], start=False, stop=True, skip_group_check=True),
                             reads=[bS1[h // 4], Bc["Ub"][d]], writes=[B_pY])
                    S.op("dve", lambda e: e.tensor_tensor(out=Hf[:, d].rearrange("p a b -> p (a b)"), in0=Hf[:, d].rearrange("p a b -> p (a b)"), in1=pZ[:, 0:384], op=ALU.add),
                         reads=[B_Hf[d], B_pZ], writes=[B_Hf[d]])
                    S.op("dve", lambda e: e.tensor_tensor(out=Hf[:, d], in0=Hf[:, d], in1=gC[:, d, :, c:c + 1].to_broadcast([64, 6, 64]), op=ALU.mult),
                         reads=[B_Hf[d], B_ld0], writes=[B_Hf[d]])
                    S.op("act", lambda e: e.activation(out=hb_new[:, d], in_=Hf[:, d], func=AF.Copy), reads=[B_Hf[d]], writes=[bHn])
                    S.op("act", lambda e: e.activation(out=Ysb[d][:], in_=pY[:, 0:384], func=AF.Copy), reads=[B_pY], writes=[Bc["Ysb"][d]])
                    S.dma("sp", Y_d.ap()[d, cs:cs + 64, :], Ysb[d][:], [Bc["Ysb"][d]], B_Y)
                    yield

            loads(0)
            loads(1)
            run_streams([gen_pre(0, 0), gen_pre(0, 1)])
            for i in range(36 if not DBG.get("skip_scan") else 0):
                if i + 2 < 36:
                    loads(i + 2)
                streams = [gen_chain(i)]
                if i + 1 < 36:
                    streams += [gen_pre(i + 1, 0), gen_pre(i + 1, 1)]
                run_streams(streams)
            S.barrier()

    def rwkv_post(l):
        with ExitStack() as es:
            yf = [sb(es, "yf%d" % i, [128, 384], F32) for i in range(2)]
            yb = [sb(es, "yb%d" % i, [128, 384], F32) for i in range(2)]
            Byl = [Buf("po_yl%d" % i) for i in range(2)]
            ysq = [sb(es, "ysq%d" % i, [128, 384], F32) for i in range(2)]
            ynb = [sb(es, "ynb%d" % i, [128, 384], BF16) for i in range(2)]
            st6 = [sb(es, "st6%d" % i, [128, 6], F32) for i in range(2)]
            Bw_ = [Buf("po_w%d" % i) for i in range(2)]
            ylT = sb(es, "ylT", [128, 3, T], F32)
            B_ylT = Buf("po_ylT")
            ptt = [ps(es, "po_ptt%d" % i, [128, 1024], BF16) for i in range(2)]
            B_ptt = [Buf("po_ptt%d" % i) for i in range(2)]

            def gen_tile(tt_, s_):
                t0 = tt_ * 128
                B_w = Bw_[s_]
                S.dma("sp", yf[s_][:], Y_d.ap()[0, t0:t0 + 128, :], [B_Y], Byl[s_])
                S.dma("sp", yb[s_][:], Y_d.ap()[1, t0:t0 + 128, :], [B_Y], Byl[s_])
                y3 = yf[s_][:].rearrange("p (h c) -> p h c", c=64)
                q3 = ysq[s_][:].rearrange("p (h c) -> p h c", c=64)
                sx = st6[s_]
                S.op("dve", lambda e: e.tensor_tensor(out=yf[s_][:], in0=yf[s_][:], in1=yb[s_][:], op=ALU.add), reads=[Byl[s_]], writes=[Byl[s_]])
                yield
                S.op("dve", lambda e: e.tensor_reduce(out=sx[:], in_=y3, axis=mybir.AxisListType.X, op=ALU.add), reads=[Byl[s_]], writes=[B_w])
                yield
                S.op("dve", lambda e: e.tensor_scalar(out=sx[:], in0=sx[:], scalar1=1.0 / 64, scalar2=None, op0=ALU.mult), reads=[B_w], writes=[B_w])
                yield
                S.op("dve", lambda e: e.tensor_tensor(out=y3, in0=y3, in1=sx[:].unsqueeze(2).to_broadcast([128, 6, 64]), op=ALU.subtract), reads=[Byl[s_], B_w], writes=[Byl[s_]])
                yield
                S.op("dve", lambda e: e.tensor_tensor(out=ysq[s_][:], in0=yf[s_][:], in1=yf[s_][:], op=ALU.mult), reads=[Byl[s_]], writes=[B_w])
                yield
                S.op("dve", lambda e: e.tensor_reduce(out=sx[:], in_=q3, axis=mybir.AxisListType.X, op=ALU.add), reads=[B_w], writes=[B_w])
                yield
                S.op("act", lambda e: e.activation(out=sx[:], in_=sx[:], func=AF.Sqrt, bias=GN_EPS, scale=1.0 / 64), reads=[B_w], writes=[B_w])
                yield
                S.op("dve", lambda e: e.reciprocal(out=sx[:], in_=sx[:]), reads=[B_w], writes=[B_w])
                yield
                S.op("dve", lambda e: e.tensor_tensor(out=ynb[s_][:].rearrange("p (h c) -> p h c", c=64), in0=y3, in1=sx[:].unsqueeze(2).to_broadcast([128, 6, 64]), op=ALU.mult),
                     reads=[Byl[s_], B_w], writes=[B_w])
                yield
                for j in range(3):
                    S.op("pe", lambda e, j=j: e.transpose(ptt[s_][:, j * 128:(j + 1) * 128], ynb[s_][:, j * 128:(j + 1) * 128], ident_b[:]), reads=[B_w, B_const], writes=[B_ptt[s_]])
                for j in range(3):
                    S.op("act", lambda e, j=j: e.activation(out=ylT[:, j, t0:t0 + 128], in_=ptt[s_][:, j * 128:(j + 1) * 128], func=AF.Identity,
                                                            scale=ppc(l, "lnxg", j), bias=ppc(l, "lnxb", j)), reads=[B_ptt[s_], B_pp], writes=[B_ylT])
                yield

            for pr in range(9):
                run_streams([gen_tile(2 * pr, 0), gen_tile(2 * pr + 1, 1)])
            bo = sb(es, "po_bo", [128, T], F32)
            ga = sb(es, "po_ga", [128, T], F32)
            ob_ = sb(es, "po_ob", [128, T], BF16)
            B_bo, B_ob = Buf("po_bo"), Buf("po_ob")
            for j in range(3):
                S.dma("sp", bo[:], bon_d.ap()[j * 128:(j + 1) * 128, :], [B_bg], B_bo)
                S.dma("sp", ga[:], gat_d.ap()[j * 128:(j + 1) * 128, :], [B_bg], B_bo)
                S.op("dve", lambda e: e.tensor_tensor(out=bo[:], in0=bo[:], in1=ylT[:, j, :], op=ALU.add), reads=[B_bo, B_ylT], writes=[B_bo])
                S.op("dve", lambda e: e.tensor_tensor(out=ob_[:], in0=bo[:], in1=ga[:], op=ALU.mult), reads=[B_bo], writes=[B_ob])
                S.dma("sp", cat_d.ap()[j * 128:(j + 1) * 128, :], ob_[:], [B_ob], B_cat)
            S.barrier()

    def mla_stage(l, last):
        with ExitStack() as es:
            B_qn = Buf("qnb")
            wuq = sb(es, "wuq", [128, 6, 576], BF16)
            wukv = sb(es, "wukv", [128, 2, 768], BF16)
            B_wu = Buf("wu")
            S.dma("pool", wuq[:], w_uq_d.ap()[l].rearrange("(kc p) n -> p kc n", p=128), [], B_wu)
            S.dma("pool", wukv[:], w_ukv_d.ap()[l].rearrange("(kc p) n -> p kc n", p=128), [], B_wu)
            QT = sb(es, "QT", [96, 6, T], BF16)
            KT = sb(es, "KT", [96, 6, T], BF16)
            Vt = sb(es, "Vt", [128, 18, 6, 65], BF16)
            B_QT, B_KT, B_V = Buf("QT"), Buf("KT"), Buf("Vt")
            gbc = sb(es, "gbc", [128, 192], F32)
            ropet = sb(es, "ropet", [128, 16, 32], F32)
            S.dma("sp", gbc[:], gbc_d.ap()[l], [], B_const)
            S.dma("sp", ropet[:], rope_d.ap().rearrange("(n p) c -> p n c", p=128), [], B_const)
            S.op("dve", lambda e: e.memset(Vt[:], 1.0), writes=[B_V])
            with ExitStack() as es2:
                pq0 = ps(es2, "pq0", [128, 512])
                pq1 = ps(es2, "pq1", [128, 512])
                pk0 = ps(es2, "pk0", [128, 512])
                pk1 = ps(es2, "pk1", [128, 512])
                pkr = ps(es2, "pkr", [128, 512])
                ptq = ps(es2, "ptq", [128, 1024], BF16)
                ptk = ps(es2, "ptk", [128, 1024], BF16)
                Bp = {n: Buf(n) for n in ("pq0", "pq1", "pk0", "pk1", "pkr", "ptq", "ptk")}
                sets = []
                for k_ in range(2):
                    Z = {}
                    for (n_, shp, dt_) in (("q_sb", [128, 576], F32), ("kv_sb", [128, 768], F32), ("kr_in", [32, 128], F32), ("kr_sb", [128, 32], F32),
                                           ("qr", [128, 6, 32], F32), ("kr", [128, 1, 32], F32), ("krr", [128, 32], F32), ("rt1", [128, 6, 2, 8], F32),
                                           ("rt2", [128, 6, 2, 8], F32), ("Qtok", [128, 6, 96], BF16), ("Ktok", [128, 6, 96], BF16),
                                           ("sqt", [128, 6, 64], F32), ("ssq", [128, 6], F32)):
                        Z[n_] = sb(es2, "%s_%d" % (n_, k_), shp, dt_)
                    Z["B"] = {n: Buf("%s_%d" % (n, k_)) for n in ("q_sb", "kv_sb", "kr_in", "kr_sb", "qr", "kr", "krr", "rt", "Qtok", "Ktok", "scr")}
                    sets.append(Z)
                qin = sb(es2, "qin", [128, 8, 256], F32)
                B_qin = Buf("qin")
                sq = sb(es2, "sq", [128, 8, 256], BF16)
                tmp = sb(es2, "tmp", [128, 256], F32)
                rsq = sb(es2, "rsq", [128, 256], F32)
                rskv = sb(es2, "rskv", [128, 256], F32)
                qnb = sb(es2, "qnb", [128, 8, 256], BF16)
                B_sq, B_tmp, B_rsq, B_rskv = Buf("sq"), Buf("tmp"), Buf("rsq"), Buf("rskv")
                pbv = pB_d.ap()[0:1024, :].rearrange("(kc p) t -> p kc t", p=128)

                def latent_norm(t0):
                    w = 256
                    t1 = t0 + w
                    S.dma("sp", qin[:, :, 0:w], pbv[:, :, t0:t1], [B_pB], B_qin)
                    S.op("act", lambda e: e.activation(out=sq[:, :, 0:w], in_=qin[:, :, 0:w], func=AF.Square), reads=[B_qin], writes=[B_sq])
                    for kc in range(6):
                        S.op("pe", lambda e, kc=kc: e.matmul(pq0[:, 0:w], ones_bf[:], sq[:, kc, 0:w], start=(kc == 0), stop=(kc == 5)),
                             reads=[B_sq, B_const], writes=[Bp["pq0"]])
                    for kc in range(6, 8):
                        S.op("pe", lambda e, kc=kc: e.matmul(pk0[:, 0:w], ones_bf[:], sq[:, kc, 0:w], start=(kc == 6), stop=(kc == 7)),
                             reads=[B_sq, B_const], writes=[Bp["pk0"]])
                    S.op("act", lambda e: e.activation(out=rsq[:, 0:w], in_=pq0[:, 0:w], func=AF.Sqrt, bias=EPS, scale=1.0 / 768), reads=[Bp["pq0"]], writes=[B_rsq])
                    S.op("dve", lambda e: e.reciprocal(out=rsq[:, 0:w], in_=rsq[:, 0:w]), reads=[B_rsq], writes=[B_rsq])
                    S.op("act", lambda e: e.activation(out=rskv[:, 0:w], in_=pk0[:, 0:w], func=AF.Sqrt, bias=EPS, scale=1.0 / 256), reads=[Bp["pk0"]], writes=[B_rskv])
                    S.op("dve", lambda e: e.reciprocal(out=rskv[:, 0:w], in_=rskv[:, 0:w]), reads=[B_rskv], writes=[B_rskv])
                    for kc in range(8):
                        rs, B_rs = (rsq, B_rsq) if kc < 6 else (rskv, B_rskv)
                        g_ap = ppc(l, "qng", kc) if kc < 6 else ppc(l, "kvng", kc - 6)
                        S.op("dve", lambda e, kc=kc: e.tensor_tensor(out=tmp[:, 0:w], in0=qin[:, kc, 0:w], in1=rs[:, 0:w], op=ALU.mult),
                             reads=[B_qin, B_rs], writes=[B_tmp])
                        S.op("act", lambda e, kc=kc: e.activation(out=qnb[:, kc, 0:w], in_=tmp[:, 0:w], func=AF.Identity, scale=g_ap, bias=0.0),
                             reads=[B_tmp, B_pp], writes=[B_qn])

                def rope(Z, src3, H, dst3, tt, B_src, B_dst):
                    Bn = Z["B"]
                    sv = src3.rearrange("p h (a f e) -> p h a f e", a=2, f=2, e=8)
                    dv = dst3.rearrange("p h (a f e) -> p h a f e", a=2, f=2, e=8)
                    cos = ropet[:, tt - 2, 0:16].rearrange("p (a e) -> p a e", a=2).unsqueeze(1).to_broadcast([128, H, 2, 8])
                    sin = ropet[:, tt - 2, 16:32].rearrange("p (a e) -> p a e", a=2).unsqueeze(1).to_broadcast([128, H, 2, 8])
                    x1, x2 = sv[:, :, :, 0, :], sv[:, :, :, 1, :]
                    a_, b_ = Z["rt1"][:, 0:H], Z["rt2"][:, 0:H]
                    S.op("dve", lambda e: e.tensor_tensor(out=a_, in0=x1, in1=cos, op=ALU.mult), reads=[B_src, B_const], writes=[Bn["rt"]])
                    yield
                    S.op("dve", lambda e: e.tensor_tensor(out=b_, in0=x2, in1=sin, op=ALU.mult), reads=[B_src, B_const], writes=[Bn["rt"]])
                    yield
                    S.op("dve", lambda e: e.tensor_tensor(out=dv[:, :, :, 0, :], in0=a_, in1=b_, op=ALU.subtract), reads=[Bn["rt"]], writes=[B_dst])
                    yield
                    S.op("dve", lambda e: e.tensor_tensor(out=a_, in0=x2, in1=cos, op=ALU.mult), reads=[B_src, B_const, B_dst], writes=[Bn["rt"]])
                    yield
                    S.op("dve", lambda e: e.tensor_tensor(out=b_, in0=x1, in1=sin, op=ALU.mult), reads=[B_src, B_const], writes=[Bn["rt"]])
                    yield
                    S.op("dve", lambda e: e.tensor_tensor(out=dv[:, :, :, 1, :], in0=a_, in1=b_, op=ALU.add), reads=[Bn["rt"]], writes=[B_dst])
                    yield

                def gen_tile(tt, Z):
                    Bn = Z["B"]
                    q_sb, kv_sb, kr_in, kr_sb, qr, kr, krr, Qtok, Ktok = (Z[n] for n in ("q_sb", "kv_sb", "kr_in", "kr_sb", "qr", "kr", "krr", "Qtok", "Ktok"))
                    q3 = q_sb[:].rearrange("p (h c) -> p h c", h=6)
                    kv3 = kv_sb[:].rearrange("p (h c) -> p h c", h=6)
                    scr = (Z["sqt"], Z["ssq"])
                    t0 = tt * 128
                    lo = (tt % 2) * 128
                    need_q = (tt >= 2) or (not last)
                    if need_q:
                        for kc in range(6):
                            S.op("pe", lambda e, kc=kc: e.matmul(pq0[:, 0:512], qnb[:, kc, lo:lo + 128], wuq[:, kc, 0:512], start=(kc == 0), stop=(kc == 5)),
                                 reads=[B_qn, B_wu], writes=[Bp["pq0"]])
                        for kc in range(6):
                            S.op("pe", lambda e, kc=kc: e.matmul(pq1[:, 0:64], qnb[:, kc, lo:lo + 128], wuq[:, kc, 512:576], start=(kc == 0), stop=(kc == 5)),
                                 reads=[B_qn, B_wu], writes=[Bp["pq1"]])
                        S.op("act", lambda e: e.activation(out=q_sb[:, 0:512], in_=pq0[:, 0:512], func=AF.Copy), reads=[Bp["pq0"]], writes=[Bn["q_sb"]])
                        S.op("act", lambda e: e.activation(out=q_sb[:, 512:576], in_=pq1[:, 0:64], func=AF.Copy), reads=[Bp["pq1"]], writes=[Bn["q_sb"]])
                        yield
                    for kc in range(2):
                        S.op("pe", lambda e, kc=kc: e.matmul(pk0[:, 0:512], qnb[:, 6 + kc, lo:lo + 128], wukv[:, kc, 0:512], start=(kc == 0), stop=(kc == 1)),
                             reads=[B_qn, B_wu], writes=[Bp["pk0"]])
                    for kc in range(2):
                        S.op("pe", lambda e, kc=kc: e.matmul(pk1[:, 0:256], qnb[:, 6 + kc, lo:lo + 128], wukv[:, kc, 512:768], start=(kc == 0), stop=(kc == 1)),
                             reads=[B_qn, B_wu], writes=[Bp["pk1"]])
                    S.op("act", lambda e: e.activation(out=kv_sb[:, 0:512], in_=pk0[:, 0:512], func=AF.Copy), reads=[Bp["pk0"]], writes=[Bn["kv_sb"]])
                    S.op("act", lambda e: e.activation(out=kv_sb[:, 512:768], in_=pk1[:, 0:256], func=AF.Copy), reads=[Bp["pk1"]], writes=[Bn["kv_sb"]])
                    S.dma("sp", kr_in[:], pB_d.ap()[1024:1056, t0:t0 + 128], [B_pB], Bn["kr_in"])
                    S.op("pe", lambda e: e.transpose(pkr[:, 0:32], kr_in[:], ident_f[0:32, 0:32]), reads=[Bn["kr_in"], B_id], writes=[Bp["pkr"]])
                    S.op("act", lambda e: e.activation(out=kr_sb[:], in_=pkr[:, 0:32], func=AF.Copy), reads=[Bp["pkr"]], writes=[Bn["kr_sb"]])
                    yield
                    if need_q:
                        yield from headnorm(q3[:, :, 0:64], 6, 64, gbc[:, 0:64], Qtok[:, :, 0:64], scr, Bn["q_sb"], Bn["Qtok"], Bn["scr"])
                        if tt >= 2:
                            yield from headnorm(q3[:, :, 64:96], 6, 32, gbc[:, 128:160], qr[:], scr, Bn["q_sb"], Bn["qr"], Bn["scr"])
                            yield from rope(Z, qr[:], 6, Qtok[:, :, 64:96], tt, Bn["qr"], Bn["Qtok"])
                        else:
                            yield from headnorm(q3[:, :, 64:96], 6, 32, gbc[:, 128:160], Qtok[:, :, 64:96], scr, Bn["q_sb"], Bn["Qtok"], Bn["scr"])
                    yield from headnorm(kv3[:, :, 0:64], 6, 64, gbc[:, 64:128], Ktok[:, :, 0:64], scr, Bn["kv_sb"], Bn["Ktok"], Bn["scr"])
                    S.op("act", lambda e: e.activation(out=Vt[:, tt, :, 0:64], in_=kv3[:, :, 64:128], func=AF.Copy), reads=[Bn["kv_sb"]], writes=[B_V])
                    if tt >= 2:
                        yield from headnorm(kr_sb[:].unsqueeze(1), 1, 32, gbc[:, 160:192], kr[:], scr, Bn["kr_sb"], Bn["kr"], Bn["scr"])
                        yield from rope(Z, kr[:], 1, krr[:].unsqueeze(1), tt, Bn["kr"], Bn["krr"])
                    else:
                        yield from headnorm(kr_sb[:].unsqueeze(1), 1, 32, gbc[:, 160:192], krr[:].unsqueeze(1), scr, Bn["kr_sb"], Bn["krr"], Bn["scr"])
                    S.op("dve", lambda e: e.tensor_copy(out=Ktok[:, :, 64:96], in_=krr[:].unsqueeze(1).to_broadcast([128, 6, 32])),
                         reads=[Bn["krr"]], writes=[Bn["Ktok"]])
                    yield
                    if need_q:
                        for h in range(6):
                            S.op("pe", lambda e, h=h: e.transpose(ptq[0:96, h * 128:(h + 1) * 128], Qtok[:, h, :], ident_b[:]),
                                 reads=[Bn["Qtok"], B_const], writes=[Bp["ptq"]])
                        S.op("act", lambda e: e.activation(out=QT[:, :, t0:t0 + 128], in_=ptq[0:96, 0:768].rearrange("p (h t) -> p h t", h=6), func=AF.Copy),
                             reads=[Bp["ptq"]], writes=[B_QT])
                        yield
                    for h in range(6):
                        S.op("pe", lambda e, h=h: e.transpose(ptk[0:96, h * 128:(h + 1) * 128], Ktok[:, h, :], ident_b[:]),
                             reads=[Bn["Ktok"], B_const], writes=[Bp["ptk"]])
                    S.op("act", lambda e: e.activation(out=KT[:, :, t0:t0 + 128], in_=ptk[0:96, 0:768].rearrange("p (h t) -> p h t", h=6), func=AF.Copy),
                         reads=[Bp["ptk"]], writes=[B_KT])
                    yield

                for pr in range(9 if not DBG.get("skip_mla_b") else 0):
                    latent_norm(pr * 256)
                    run_streams([gen_tile(2 * pr, sets[0]), gen_tile(2 * pr + 1, sets[1])])
                S.barrier()
            with ExitStack() as es2:
                pss = [ps(es2, "pss%d" % i, [128, 512]) for i in range(2)]
                pso = [ps(es2, "pso%d" % i, [128, 512]) for i in range(4)]
                Bpss = [Buf("pss%d" % i) for i in range(2)]
                Bpso = [Buf("pso%d" % i) for i in range(4)]
                pts = [sb(es2, "pt%d" % i, [128, 512], BF16) for i in range(2)]
                Bpt = [Buf("pt%d" % i) for i in range(2)]
                ytok = sb(es2, "ytok", [128, 18, 384], BF16)
                B_yt = Buf("ytok")
                rec = sb(es2, "rec", [128, 4], F32)
                B_rec = Buf("rec")
                ycT = sb(es2, "ycT", [128, 3, T], BF16)
                B_yc = Buf("ycT")
                ptt = ps(es2, "ptt", [128, 1024], BF16)
                B_ptt = Buf("ptt")
                qblocks = [(256 + i * 512, 512, list(range(18))) for i in range(4)]
                if not last:
                    qblocks = [(0, 256, [0, 1])] + qblocks
                its = []
                for h in range(6 if not DBG.get("skip_mla_c") else 0):
                    for (q0, qw, kts) in qblocks:
                        for ki, kt in enumerate(kts):
                            its.append((h, q0, qw, kt, ki, len(kts)))

                def score(i):
                    h, q0, qw, kt, ki, nk = its[i]
                    p = i % 2
                    S.op("pe", lambda e: e.matmul(pss[p][:, 0:qw], KT[:, h, kt * 128:(kt + 1) * 128], QT[:, h, q0:q0 + qw], start=True, stop=True),
                         reads=[B_KT, B_QT], writes=[Bpss[p]])
                    S.op("act", lambda e: e.activation(out=pts[p][:, 0:qw], in_=pss[p][:, 0:qw], func=AF.Exp, scale=ATTN_SCALE),
                         reads=[Bpss[p]], writes=[Bpt[p]])

                if its:
                    score(0)
                for i in range(len(its)):
                    h, q0, qw, kt, ki, nk = its[i]
                    p = i % 2
                    nqs = qw // 128
                    if i + 1 < len(its):
                        score(i + 1)
                    for qs in range(nqs):
                        S.op("pe", lambda e, qs=qs: e.matmul(pso[qs][:, 0:65], pts[p][:, qs * 128:(qs + 1) * 128], Vt[:, kt, h, :],
                                                            start=(ki == 0), stop=(ki == nk - 1)),
                             reads=[Bpt[p], B_V], writes=[Bpso[qs]])
                    if ki == nk - 1:
                        for qs in range(nqs):
                            tq = (q0 + qs * 128) // 128
                            S.op("dve", lambda e, qs=qs: e.reciprocal(out=rec[:, qs:qs + 1], in_=pso[qs][:, 64:65]), reads=[Bpso[qs]], writes=[B_rec])
                            S.op("dve", lambda e, qs=qs, tq=tq: e.tensor_scalar(out=ytok[:, tq, h * 64:(h + 1) * 64], in0=pso[qs][:, 0:64], scalar1=rec[:, qs:qs + 1],
                                                                                scalar2=None, op0=ALU.mult), reads=[Bpso[qs], B_rec], writes=[B_yt])
                tts = range(18) if not last else range(2, 18)
                for tt in tts:
                    for j in range(3):
                        S.op("pe", lambda e, j=j: e.transpose(ptt[:, j * 128:(j + 1) * 128], ytok[:, tt, j * 128:(j + 1) * 128], ident_b[:]),
                             reads=[B_yt, B_const], writes=[B_ptt])
                    S.op("act", lambda e: e.activation(out=ycT[:, :, tt * 128:(tt + 1) * 128], in_=ptt[:, 0:384].rearrange("p (j t) -> p j t", j=3), func=AF.Copy),
                         reads=[B_ptt], writes=[B_yc])
                c_lo = 0 if not last else 256
                for j in range(3):
                    S.dma("sp", cat_d.ap()[384 + j * 128:384 + (j + 1) * 128, c_lo:T], ycT[:, j, c_lo:T], [B_yc], B_cat)
                S.barrier()

    for l in range(n_layers):
        last = l == 1
        with ExitStack() as es:
            wts = [sb(es, "adw%d" % i, [128, KC, 512], BF16) for i in range(2)]
            Bw = [Buf("adw%d" % i) for i in range(2)]
            psm = ps(es, "psm", [128, 96])
            B_psm = Buf("psm")
            prow = [ps(es, "prow%d" % i, [2, 512]) for i in range(2)]
            B_prow = [Buf("prow%d" % i) for i in range(2)]
            mrow = sb(es, "mrow", [2, 6 * D], F32)
            B_mrow = Buf("mrow")
            wv = ada_w_d.ap()[l].rearrange("(kc p) n -> p kc n", p=128)
            for og in range(12):
                s = og % 2
                S.dma("pool", wts[s][:], wv[:, :, og * 512:(og + 1) * 512], [], Bw[s])
                for kc in range(KC):
                    S.op("pe", lambda e, s=s, kc=kc: e.matmul(prow[s][:, 0:512], scT[:, kc, :], wts[s][:, kc, :], start=(kc == 0), stop=(kc == KC - 1)),
                         reads=[Bw[s], B_scT], writes=[B_prow[s]])
                S.op("act", lambda e, s=s, og=og: e.activation(out=mrow[:, og * 512:(og + 1) * 512], in_=prow[s][:, 0:512], func=AF.Copy),
                     reads=[B_prow[s]], writes=[B_mrow])
            for ch in range(48):
                S.op("pe", lambda e, ch=ch: e.transpose(psm[:, 2 * ch:2 * ch + 2], mrow[:, ch * 128:(ch + 1) * 128], ident_f[0:2, 0:2]),
                     reads=[B_mrow, B_id], writes=[B_psm])
            o, _ = PP["adab"]
            S.op("dve", lambda e: e.tensor_tensor(out=mod[:], in0=psm[:].rearrange("p (a b) -> p a b", b=2),
                                                  in1=ppt[:, l, o:o + 48].unsqueeze(2).to_broadcast([128, 48, 2]), op=ALU.add),
                 reads=[B_psm, B_pp], writes=[B_mod])
            for (gs, sc0, gname) in ((gs1, 8, "n1g"), (gs2, 32, "n2g")):
                og_, _ = PP[gname]
                S.op("dve", lambda e, gs=gs, sc0=sc0: e.tensor_scalar(out=gs[:], in0=mod[:, sc0:sc0 + 8, :], scalar1=1.0, scalar2=None, op0=ALU.add),
                     reads=[B_mod], writes=[B_mod])
                S.op("dve", lambda e, gs=gs, og_=og_: e.tensor_tensor(out=gs[:], in0=gs[:], in1=ppt[:, l, og_:og_ + 8].unsqueeze(2).to_broadcast([128, 8, 2]), op=ALU.mult),
                     reads=[B_mod, B_pp], writes=[B_mod])
            o0, _ = PP["mu0"]
            o1, _ = PP["mu1"]
            S.op("dve", lambda e: e.tensor_tensor(out=cmix[:], in0=ppt[:, l, o0:o0 + 11], in1=ppt[:, l, o1:o1 + 11], op=ALU.add),
                 reads=[B_pp], writes=[B_mod])
            S.op("dve", lambda e: e.tensor_scalar(out=cmix[:], in0=cmix[:], scalar1=-1.0, scalar2=1.0, op0=ALU.mult, op1=ALU.add),
                 reads=[B_mod], writes=[B_mod])
            S.barrier()

        if "mix" in stages:
            with ExitStack() as es:
                xnT = sb(es, "xnT", [128, KC, T], BF16)
                B_xn = Buf("xnT")
                tmp = [sb(es, "tmp%d" % i, [128, 512], F32) for i in range(2)]
                sq = sb(es, "sq", [128, KC, 512], BF16)
                rstd = sb(es, "rstd", [128, 512], F32)
                ps_s = ps(es, "ps_s", [128, 512])
                B_tmp, B_sq, B_pss, B_rstd = [Buf("tmpa"), Buf("tmpb")], Buf("sq"), Buf("pss"), Buf("rstd")
                for bi in range(len(BLKS)):
                    norm_mod(es, bi, gs1, 0, xnT, B_xn, BLKS[bi][0], tmp, B_tmp, sq, B_sq, ps_s, B_pss, rstd, B_rstd)

                wts = [sb(es, "wi%d" % i, [128, KC, 512], BF16) for i in range(2)]
                Bw = [Buf("wi%d" % i) for i in range(2)]
                feat = [sb(es, "feat%d" % i, [128, T], F32) for i in range(5)]
                Bf = [Buf("feat%d" % i) for i in range(5)]
                ybf = sb(es, "ybf", [128, T], BF16)
                B_ybf = Buf("ybf")
                pmm = [ps(es, "pmm%d" % i, [128, 512]) for i in range(2)]
                Bpm = [Buf("pmm%d" % i) for i in range(2)]
                wv = w_in_d.ap()[l].rearrange("(kc p) n -> p kc n", p=128)
                state = {"w": 0, "p": 0, "f": 0}

                def load_w(pieces):
                    s = state["w"] % 2
                    state["w"] += 1
                    o = 0
                    for (c0, wd) in pieces:
                        S.dma("pool", wts[s][:, :, o:o + wd], wv[:, :, c0:c0 + wd], [], Bw[s])
                        o += wd
                    return s

                def proj_unit(s, o, M, fi):
                    for (t0, t1) in BLKS:
                        w = t1 - t0
                        p = state["p"] % 2
                        state["p"] += 1
                        for kc in range(KC):
                            S.op("pe", lambda e, kc=kc, p=p: e.matmul(pmm[p][0:M, 0:w], wts[s][:, kc, o:o + M], xnT[:, kc, t0:t1],
                                                                    start=(kc == 0), stop=(kc == KC - 1)),
                                 reads=[Bw[s], B_xn], writes=[Bpm[p]])
                        S.op("act", lambda e, p=p: e.activation(out=feat[fi][0:M, t0:t1], in_=pmm[p][0:M, 0:w], func=AF.Copy),
                             reads=[Bpm[p]], writes=[Bf[fi]])

                def shift3(src, dst, Bs, Bd, M, c_ap, m0_ap, m1_ap):
                    S.op("act", lambda e: e.activation(out=dst[0:M, :], in_=src[0:M, :], func=AF.Identity, scale=c_ap, bias=0.0),
                         reads=[Bs, B_mod, B_pp], writes=[Bd])
                    for (a0, a1, sh, sc) in ((1, NCTX, -1, m0_ap), (NCTX + 1, T, -1, m0_ap), (0, NCTX - 1, 1, m1_ap), (NCTX, T - 1, 1, m1_ap)):
                        S.op("dve", lambda e, a0=a0, a1=a1, sh=sh, sc=sc: e.scalar_tensor_tensor(
                            out=dst[0:M, a0:a1], in0=src[0:M, a0 + sh:a1 + sh], scalar=sc, in1=dst[0:M, a0:a1],
                            op0=ALU.mult, op1=ALU.add), reads=[Bs, Bd, B_pp], writes=[Bd])

                a_segs = [[(0, 512)], [(512, 512)], [(1024, 384)]]
                ch = 0
                for pieces in a_segs:
                    s = load_w(pieces)
                    for o in range(0, pieces[0][1], 128):
                        fi = state["f"] % 2
                        state["f"] += 1
                        proj_unit(s, o, 128, fi)
                        shift3(feat[fi], feat[2 + fi], Bf[fi], Bf[2 + fi], 128, cmix[:, ch:ch + 1], ppc(l, "mu0", ch), ppc(l, "mu1", ch))
                        S.dma("sp", pA_d.ap()[ch * 128:(ch + 1) * 128, :], feat[2 + fi][:, :], [Bf[2 + fi]], B_pA)
                        ch += 1
                b_segs = [[(1408, 512)], [(1920, 512)], [(2432, 32)]]
                ch = 0
                for pieces in b_segs:
                    s = load_w(pieces)
                    for o in range(0, pieces[0][1], 128):
                        M = min(128, pieces[0][1] - o)
                        fi = state["f"] % 2
                        state["f"] += 1
                        proj_unit(s, o, M, fi)
                        S.dma("sp", pB_d.ap()[ch * 128:ch * 128 + M, :], feat[fi][0:M, :], [Bf[fi]], B_pB)
                        ch += 1
                c0 = A_IN + 1056
                for j in range(2):
                    s = load_w([(c0 + j * 128, 128), (c0 + 256 + j * 128, 128), (c0 + 512 + j * 128, 128)])
                    proj_unit(s, 0, 128, 0)
                    proj_unit(s, 128, 128, 1)
                    proj_unit(s, 256, 128, 2)
                    S.op("dve", lambda e: e.tensor_tensor(out=feat[1][:], in0=feat[1][:], in1=feat[2][:], op=ALU.mult),
                         reads=[Bf[1], Bf[2]], writes=[Bf[1]])
                    shift3(feat[1], feat[3], Bf[1], Bf[3], 128, ppc(l, "conv", 2 + j), ppc(l, "conv", 0 + j), ppc(l, "conv", 4 + j))
                    S.op("dve", lambda e: e.tensor_tensor(out=ybf[:], in0=feat[0][:], in1=feat[3][:], op=ALU.mult),
                         reads=[Bf[0], Bf[3]], writes=[B_ybf])
                    S.dma("sp", cat_d.ap()[768 + j * 128:768 + (j + 1) * 128, :], ybf[:], [B_ybf], B_cat)
                zrows = ([] if "rwkv" in stages else [0, 1, 2]) + ([] if "mla" in stages else [3, 4, 5])
                if zrows:
                    S.op("dve", lambda e: e.memset(ybf[:], 0.0), writes=[B_ybf])
                    for zr in zrows:
                        S.dma("sp", cat_d.ap()[zr * 128:(zr + 1) * 128, :], ybf[:], [B_ybf], B_cat)
                S.barrier()

            if "rwkv" in stages:
                rwkv_prep(l)
                rwkv_scan(l)
                rwkv_post(l)
            if "mla" in stages:
                mla_stage(l, last)

            with ExitStack() as es:
                wo = sb(es, "wo", [128, KC, D], BF16)
                B_wo = Buf("wo")
                S.dma("pool", wo[:, :, 0:512], w_out_d.ap()[l].rearrange("(kc p) n -> p kc n", p=128)[:, :, 0:512], [], B_wo)
                S.dma("pool", wo[:, :, 512:1024], w_out_d.ap()[l].rearrange("(kc p) n -> p kc n", p=128)[:, :, 512:1024], [], B_wo)
                cats = [sb(es, "catb%d" % i, [128, KC, 512], BF16) for i in range(2)]
                Bc = [Buf("catb%d" % i) for i in range(2)]
                pmm = [ps(es, "pmo%d" % i, [128, 512]) for i in range(2)]
                Bpm = [Buf("pmo%d" % i) for i in range(2)]
                pc = 0
                for bi, (t0, t1) in enumerate(BLKS):
                    if last and bi == 0:
                        continue
                    w = t1 - t0
                    ci = 1 if bi == 0 else 0
                    s = bi % 2
                    S.dma("sp", cats[s][:, :, 0:w], cat_d.ap().rearrange("(kc p) t -> p kc t", p=128)[:, :, t0:t1], [B_cat], Bc[s])
                    for fo in range(KC):
                        p = pc % 2
                        pc += 1
                        for kc in range(KC):
                            S.op("pe", lambda e, kc=kc, p=p, fo=fo: e.matmul(pmm[p][:, 0:w], wo[:, kc, fo * 128:(fo + 1) * 128], cats[s][:, kc, 0:w],
                                                                           start=(kc == 0), stop=(kc == KC - 1)),
                                 reads=[B_wo, Bc[s]], writes=[Bpm[p]])
                        S.op("dve", lambda e, p=p, fo=fo: e.scalar_tensor_tensor(
                            out=xT[:, fo, t0:t1], in0=pmm[p][:, 0:w], scalar=mod[:, 16 + fo, ci:ci + 1], in1=xT[:, fo, t0:t1],
                            op0=ALU.mult, op1=ALU.add), reads=[Bpm[p], B_mod, XB[bi]], writes=[XB[bi]])
                S.barrier()

        if "ffn" in stages:
            sbs = [[0, 1, 2], [3, 4]] if not last else [[1, 2], [3, 4]]
            for sbl in sbs:
                with ExitStack() as es:
                    ntok = sum(BLKS[b][1] - BLKS[b][0] for b in sbl)
                    hT = sb(es, "hT", [128, KC, ntok], BF16)
                    B_h = Buf("hT")
                    actT = sb(es, "actT", [128, NJ, ntok], BF16)
                    B_act = [Buf("actT%d" % j) for j in range(NJ)]
                    tmp = [sb(es, "tmp%d" % i, [128, 512], F32) for i in range(2)]
                    sq = sb(es, "sq", [128, KC, 512], BF16)
                    rstd = sb(es, "rstd", [128, 512], F32)
                    sgs = [sb(es, "sg%d" % i, [128, 512], F32) for i in range(2)]
                    B_sgs = [Buf("sg%d" % i) for i in range(2)]
                    ps_s = ps(es, "ps_s", [128, 512])
                    B_tmp, B_sq, B_pss, B_rstd = [Buf("tmpa"), Buf("tmpb")], Buf("sq"), Buf("pss"), Buf("rstd")
                    loc = {}
                    o = 0
                    for b in sbl:
                        loc[b] = o
                        norm_mod(es, b, gs2, 24, hT, B_h, o, tmp, B_tmp, sq, B_sq, ps_s, B_pss, rstd, B_rstd)
                        o += BLKS[b][1] - BLKS[b][0]
                    wts = [sb(es, "wf%d" % i, [128, KC, 256], BF16) for i in range(2)]
                    Bw = [Buf("wf%d" % i) for i in range(2)]
                    pg = [ps(es, "pg%d" % i, [128, 512]) for i in range(2)]
                    pu = [ps(es, "pu%d" % i, [128, 512]) for i in range(2)]
                    Bpg = [Buf("pg%d" % i) for i in range(2)]
                    Bpu = [Buf("pu%d" % i) for i in range(2)]
                    wv = w_fi_d.ap()[l].rearrange("(kc p) n -> p kc n", p=128)
                    pc = 0
                    for j in range(NJ):
                        s = j % 2
                        S.dma("pool", wts[s][:, :, 0:128], wv[:, :, j * 128:(j + 1) * 128], [], Bw[s])
                        S.dma("pool", wts[s][:, :, 128:256], wv[:, :, DFF + j * 128:DFF + (j + 1) * 128], [], Bw[s])
                        for b in sbl:
                            w = BLKS[b][1] - BLKS[b][0]
                            lo = loc[b]
                            p = pc % 2
                            pc += 1
                            for kc in range(KC):
                                S.op("pe", lambda e, kc=kc, p=p: e.matmul(pg[p][:, 0:w], wts[s][:, kc, 0:128], hT[:, kc, lo:lo + w],
                                                                        start=(kc == 0), stop=(kc == KC - 1)),
                                     reads=[Bw[s], B_h], writes=[Bpg[p]])
                            for kc in range(KC):
                                S.op("pe", lambda e, kc=kc, p=p: e.matmul(pu[p][:, 0:w], wts[s][:, kc, 128:256], hT[:, kc, lo:lo + w],
                                                                        start=(kc == 0), stop=(kc == KC - 1)),
                                     reads=[Bw[s], B_h], writes=[Bpu[p]])
                            sg, B_sg = sgs[p], B_sgs[p]
                            S.op("act", lambda e, p=p: e.activation(out=sg[:, 0:w], in_=pg[p][:, 0:w], func=AF.Silu),
                                 reads=[Bpg[p]], writes=[B_sg])
                            S.op("dve", lambda e, p=p, j=j: e.tensor_tensor(out=actT[:, j, lo:lo + w], in0=sg[:, 0:w], in1=pu[p][:, 0:w], op=ALU.mult),
                                 reads=[B_sg, Bpu[p]], writes=[B_act[j]])
                    wos = [sb(es, "wfo%d" % i, [128, NJ, 128], BF16) for i in range(2)]
                    Bwo = [Buf("wfo%d" % i) for i in range(2)]
                    wov = w_fo_d.ap()[l].rearrange("(j p) n -> p j n", p=128)
                    for fo in range(KC):
                        s = fo % 2
                        S.dma("pool", wos[s][:, 0:11, :], wov[:, 0:11, fo * 128:(fo + 1) * 128], [], Bwo[s])
                        S.dma("pool", wos[s][:, 11:22, :], wov[:, 11:22, fo * 128:(fo + 1) * 128], [], Bwo[s])
                        for b in sbl:
                            t0, t1 = BLKS[b]
                            w = t1 - t0
                            lo = loc[b]
                            ci = 1 if b == 0 else 0
                            p = pc % 2
                            pc += 1
                            for j in range(NJ):
                                S.op("pe", lambda e, j=j, p=p: e.matmul(pg[p][:, 0:w], wos[s][:, j, :], actT[:, j, lo:lo + w],
                                                                      start=(j == 0), stop=(j == NJ - 1)),
                                     reads=[Bwo[s], B_act[j]], writes=[Bpg[p]])
                            S.op("dve", lambda e, p=p, fo=fo: e.scalar_tensor_tensor(
                                out=xT[:, fo, t0:t1], in0=pg[p][:, 0:w], scalar=mod[:, 40 + fo, ci:ci + 1], in1=xT[:, fo, t0:t1],
                                op0=ALU.mult, op1=ALU.add), reads=[Bpg[p], B_mod, XB[b]], writes=[XB[b]])
                    S.barrier()

    yv = yT_d.ap().rearrange("(kc p) t -> p kc t", p=128)
    for bi in range(1, len(BLKS)):
        t0, t1 = BLKS[bi]
        S.dma("sp", yv[:, :, t0 - NCTX:t1 - NCTX], xT[:, :, t0:t1], [XB[bi]], B_y)
    S.E["sp"].wait_ge(B_y.grp.sem, 16 * B_y.grp.cnt)
    es_top.close()
    return nc, S


def rope_table():
    n = np.arange(NLAT)
    r_pos = (n // 64).astype(np.float32)
    c_pos = (n % 64).astype(np.float32)
    inv_freq = (1.0 / (np.float32(10000.0) ** (np.arange(0, 16, 2, dtype=np.float32) / np.float32(16)))).astype(np.float32)
    ang_r = r_pos[:, None] * inv_freq[None, :]
    ang_c = c_pos[:, None] * inv_freq[None, :]
    return np.concatenate([np.cos(ang_r), np.cos(ang_c), np.sin(ang_r), np.sin(ang_c)], axis=1).astype(np.float32)


def make_in_maps(inp):
    pps = np.stack([pack_pp(inp, l) for l in range(2)], axis=0)
    maps = []
    gbc = np.zeros((2, 128, 192), np.float32)
    for l in range(2):
        gbc[l] = np.concatenate([inp["q_nope_g"][l], inp["k_nope_g"][l], inp["q_rope_g"][l], inp["k_rope_g"][l]])[None, :]
    rope = rope_table()
    ii = np.arange(64)
    masks = np.zeros((64, 2, 192), np.float32)
    lt = (ii[:, None] < ii[None, :]).astype(np.float32)
    le = (ii[:, None] <= ii[None, :]).astype(np.float32)
    masks[:, 0, 0:64], masks[:, 0, 64:128], masks[:, 0, 128:192] = lt, le, lt.T
    masks[:, 1, 0:64], masks[:, 1, 64:128], masks[:, 1, 128:192] = lt.T, le.T, lt
    f = lambda a: np.ascontiguousarray(np.asarray(a, np.float32))
    for b in range(8):
        xcat = np.concatenate([inp["ctx"][b], inp["x"][b]], axis=0)
        cT = np.zeros((128, 16), np.float32)
        cT[:, 0::2] = _cols(inp["c"][b])
        cT[:, 1::2] = _cols(inp["c_ctx"])
        maps.append({
            "xT": f(xcat.T), "cT": cT, "pp": pps,
            "ada_w": f(inp["ada_w"]), "w_in": f(inp["w_in"]), "w_out": f(inp["w_out"]),
            "w_ffn_in": f(inp["w_ffn_in"]), "w_ffn_out": f(inp["w_ffn_out"]),
            "decay_up": f(inp["decay_up"]), "icl_up": f(inp["icl_up"]), "gate_up": f(inp["gate_up"]), "masks": masks,
            "w_uq": f(inp["w_uq"]), "w_ukv": f(inp["w_ukv"]), "gbc": gbc, "rope": rope, "ident": np.eye(128, dtype=np.float32),
        })
    return maps


def kernel(**inputs):
    inp = {k: np.asarray(v) for k, v in inputs.items()}
    nc, _ = build_program()
    maps = make_in_maps(inp)
    res = run_bass_kernel_spmd(nc, maps, core_ids=list(range(8)))
    out = np.stack([np.ascontiguousarray(res.results[b]["yT"].T) for b in range(8)], axis=0)
    return out.astype(np.float32)
```

```python
import numpy as np
from contextlib import ExitStack
import concourse.bass as bass
import concourse.mybir as mybir
from concourse.bass_utils import run_bass_kernel_spmd

F32 = mybir.dt.float32
BF16 = mybir.dt.bfloat16
AF = mybir.ActivationFunctionType
ALU = mybir.AluOpType

D = 1024
KC = 8
T = 2304
NCTX = 256
NLAT = 2048
P_IN = 3232
A_IN = 1408
DFF = 2816
NJ = 22
EPS = 1e-6
BLKS = [(0, 256), (256, 768), (768, 1280), (1280, 1792), (1792, 2304)]


_GROUPS = {}
DBG = {}


class SemGroup:
    def __init__(self, name):
        self.name = name
        self.sem = None
        self.cnt = 0


class Buf:
    __slots__ = ("name", "w", "r", "grp")

    def __init__(self, name, grp=None):
        self.name = name
        self.w = None
        self.r = {}
        if grp is None:
            grp = _GROUPS.get(name)
            if grp is None:
                grp = _GROUPS[name] = SemGroup(name)
        self.grp = grp


class Sched:
    def __init__(self, nc, same_sync=True, waw_sync=True):
        self.nc = nc
        self.waw = waw_sync
        self.E = {"pe": nc.tensor, "act": nc.scalar, "dve": nc.vector, "pool": nc.gpsimd, "sp": nc.sync}
        self.sem = {e: nc.alloc_semaphore("sem_" + e) for e in ("pe", "act", "dve", "pool")}
        self.cnt = {e: 0 for e in self.sem}
        self.known = {e: {} for e in self.E}
        self.clock = {}
        self.semh = dict(self.sem)
        self.same = same_sync
        self.groups = []
        self.n_inst = 0
        self.rr = 0

    def _sync(self, eng, reads, writes, is_dma=False):
        need = {}
        kn = self.known[eng]

        def add(dep):
            k, v = dep
            if k == eng and not is_dma and (eng == "pe" or not self.same):
                return
            if kn.get(k, 0) >= v:
                return
            if need.get(k, 0) < v:
                need[k] = v

        for b in reads:
            if b.w is not None:
                add(b.w)
        for b in writes:
            if b.w is not None and (is_dma or self.waw or b.w[0] != eng):
                add(b.w)
            for k, v in b.r.items():
                if is_dma or self.waw or k != eng:
                    add((k, v))
        for k, v in need.items():
            if kn.get(k, 0) >= v:
                continue
            self.E[eng].wait_ge(self.semh[k], v)
            kn[k] = v
            ck = self.clock.get((k, v))
            if ck:
                for kk, vv in ck.items():
                    if kn.get(kk, 0) < vv:
                        kn[kk] = vv

    def op(self, eng, fn, reads=(), writes=()):
        self._sync(eng, reads, writes)
        inst = fn(self.E[eng])
        self.cnt[eng] += 1
        v = self.cnt[eng]
        inst.then_inc(self.sem[eng], 1)
        dep = (eng, v)
        self.clock[dep] = dict(self.known[eng])
        for b in reads:
            if b.r.get(eng, 0) < v:
                b.r[eng] = v
        for b in writes:
            b.w = dep
            b.r = {}
        self.n_inst += 1

    def dma(self, q, out, in_, reads, write):
        self._sync(q, reads, [write], is_dma=True)
        g = write.grp
        if g.sem is None:
            g.sem = self.nc.alloc_semaphore("ds_" + g.name)
            self.semh[("d", g.name)] = g.sem
            self.groups.append(g)
        g.cnt += 1
        self.E[q].dma_start(out=out, in_=in_).then_inc(g.sem, 16)
        k = ("d", g.name)
        v = 16 * g.cnt
        self.clock[(k, v)] = dict(self.known[q])
        for b in reads:
            if b.r.get(k, 0) < v:
                b.r[k] = v
        write.w = (k, v)
        write.r = {}
        self.n_inst += 1

    def dma_rr(self, queues, out, in_, reads, write):
        q = queues[self.rr % len(queues)]
        self.rr += 1
        self.dma(q, out, in_, reads, write)

    def barrier(self):
        for e in self.E:
            kn = self.known[e]
            for f in self.sem:
                if f == e and e == "sp":
                    continue
                v = self.cnt[f]
                if v > 0 and kn.get(f, 0) < v:
                    self.E[e].wait_ge(self.sem[f], v)
                    kn[f] = v
            for g in self.groups:
                k = ("d", g.name)
                v = 16 * g.cnt
                if v > 0 and kn.get(k, 0) < v:
                    self.E[e].wait_ge(g.sem, v)
                    kn[k] = v


PP = {}
_off = 0
for _n, _w in [("adab", 48), ("n1g", 8), ("n2g", 8), ("mu0", 11), ("mu1", 11), ("qng", 6), ("kvng", 2),
               ("conv", 6), ("lnxg", 3), ("lnxb", 3), ("w0", 6), ("a0", 6), ("kk", 3), ("ka", 3), ("rk", 3)]:
    PP[_n] = (_off, _w)
    _off += _w
NPP = _off


def _cols(v, width=128):
    v = np.asarray(v, np.float32)
    n = v.size // width
    return np.ascontiguousarray(v.reshape(n, width).T)


def pack_pp(inp, l):
    pp = np.zeros((128, NPP), np.float32)

    def put(name, arr):
        o, w = PP[name]
        assert arr.shape[1] == w, (name, arr.shape)
        pp[: arr.shape[0], o:o + w] = arr

    put("adab", _cols(inp["ada_b"][l]))
    put("n1g", _cols(inp["norm1_g"][l]))
    put("n2g", _cols(inp["norm2_g"][l]))
    put("mu0", _cols(inp["tshift_mu"][l, 0]))
    put("mu1", _cols(inp["tshift_mu"][l, 1]))
    put("qng", _cols(inp["q_norm_g"][l]))
    put("kvng", _cols(inp["kv_norm_g"][l]))
    cw = inp["conv_w"][l]
    put("conv", np.concatenate([_cols(cw[t]) for t in range(3)], axis=1))
    put("lnxg", _cols(inp["lnx_g"][l]))
    put("lnxb", _cols(inp["lnx_b"][l]))
    put("w0", np.concatenate([_cols(inp["decay_w0"][l, d]) for d in range(2)], axis=1))
    put("a0", np.concatenate([_cols(inp["icl_a0"][l, d]) for d in range(2)], axis=1))
    put("kk", _cols(inp["k_k"][l]))
    put("ka", _cols(inp["k_a"][l]))
    put("rk", _cols(inp["r_k"][l].reshape(-1)))
    return pp


def build_program(dbg=False, n_layers=2, stages=("mix", "rwkv", "mla", "ffn")):
    nc = bass.Bass("TRN2", target_bir_lowering=False)
    _GROUPS.clear()
    S = Sched(nc)
    skind = "ExternalOutput" if dbg else "Internal"

    def din(name, shape, dt=F32):
        return nc.dram_tensor(name, list(shape), dt, kind="ExternalInput")

    xT_d = din("xT", [D, T])
    cT_d = din("cT", [128, 16])
    pp_d = din("pp", [2, 128, NPP])
    ada_w_d = din("ada_w", [2, D, 6 * D])
    w_in_d = din("w_in", [2, D, P_IN])
    w_out_d = din("w_out", [2, D, D])
    w_fi_d = din("w_ffn_in", [2, D, 2 * DFF])
    w_fo_d = din("w_ffn_out", [2, DFF, D])
    dup_d = din("decay_up", [2, 2, 64, 384])
    iup_d = din("icl_up", [2, 2, 64, 384])
    gup_d = din("gate_up", [2, 128, 384])
    masks_d = din("masks", [64, 2, 192])
    fm_d = nc.dram_tensor("fm_s", [2, 64, 24, T], BF16, kind="Internal")
    tmBK_d = nc.dram_tensor("tmBK_s", [36, 64, 2, 2, 6, 64], BF16, kind="Internal")
    tmV_d = nc.dram_tensor("tmV_s", [36, 64, 6, 64], BF16, kind="Internal")
    gC_d = nc.dram_tensor("gC_s", [64, 2, 6, 36], F32, kind=skind)
    bon_d = nc.dram_tensor("bon_s", [384, T], F32, kind=skind)
    gat_d = nc.dram_tensor("gat_s", [384, T], F32, kind=skind)
    Y_d = nc.dram_tensor("Y_s", [2, T, 384], F32, kind=skind)
    B_fm, B_tm, B_gC, B_bg, B_Y = Buf("fm"), Buf("tm"), Buf("gC"), Buf("bg"), Buf("Ys")
    w_uq_d = din("w_uq", [2, 768, 576])
    w_ukv_d = din("w_ukv", [2, 256, 768])
    gbc_d = din("gbc", [2, 128, 192])
    rope_d = din("rope", [NLAT, 32])
    ident_d = din("ident", [128, 128])
    yT_d = nc.dram_tensor("yT", [D, NLAT], F32, kind="ExternalOutput")

    pA_d = nc.dram_tensor("pA_T", [A_IN, T], F32, kind=skind)
    pB_d = nc.dram_tensor("pB_T", [1024 + 32, T], F32, kind=skind)
    cat_d = nc.dram_tensor("cat_T", [D, T], BF16, kind=skind)
    B_pA = Buf("pA")
    B_pB = Buf("pB")
    B_cat = Buf("cat")
    B_y = Buf("yT")

    es_top = ExitStack()

    uid = [0]

    def sb(es, name, shape, dt):
        uid[0] += 1
        return es.enter_context(nc.sbuf_tensor("s%d_%s" % (uid[0], name), list(shape), dt))

    def ps(es, name, shape, dt=F32):
        uid[0] += 1
        return es.enter_context(nc.psum_tensor("p%d_%s" % (uid[0], name), list(shape), dt))

    xT = sb(es_top, "xT", [128, KC, T], F32)
    XB = [Buf("xb%d" % i) for i in range(len(BLKS))]
    ones_bf = sb(es_top, "ones_bf", [128, 128], BF16)
    B_const = Buf("const")
    ppt = sb(es_top, "ppt", [128, 2, NPP], F32)
    B_pp = Buf("pp")
    mod = sb(es_top, "mod", [128, 48, 2], F32)
    gs1 = sb(es_top, "gs1", [128, KC, 2], F32)
    gs2 = sb(es_top, "gs2", [128, KC, 2], F32)
    cmix = sb(es_top, "cmix", [128, 11], F32)
    B_mod = Buf("mod")
    scT = sb(es_top, "scT", [128, KC, 2], BF16)
    B_scT = Buf("scT")

    S.op("dve", lambda e: e.memset(ones_bf[:], 1.0), writes=[B_const])
    ident_f = sb(es_top, "ident_f", [128, 128], F32)
    ident_b = sb(es_top, "ident_b", [128, 128], BF16)
    B_id = Buf("ident")
    S.dma("sp", ident_f[:], ident_d.ap(), [], B_id)
    S.op("dve", lambda e: e.tensor_copy(out=ident_b[:], in_=ident_f[:]), reads=[B_id], writes=[B_const])
    for i, (t0, t1) in enumerate(BLKS):
        S.dma("sp", xT[:, :, t0:t1], xT_d.ap().rearrange("(kc p) t -> p kc t", p=128)[:, :, t0:t1], [], XB[i])
    S.dma("sp", ppt[:], pp_d.ap().rearrange("l p n -> p l n"), [], B_pp)
    with nc.sbuf_tensor("cTt", [128, 16], F32) as cTt:
        B_c = Buf("cT")
        S.dma("sp", cTt[:], cT_d.ap(), [], B_c)
        S.op("act", lambda e: e.activation(out=scT[:].rearrange("p a b -> p (a b)"), in_=cTt[:], func=AF.Silu),
             reads=[B_c], writes=[B_scT])
        S.barrier()

    def ppc(l, name, j=0, n=1, parts=128):
        o, w = PP[name]
        return ppt[0:parts, l, o + j:o + j + n]

    def norm_mod(es, bi, gs, sh_chunk0, dst, dst_buf, dst_t0, tmp, B_tmp, sq, B_sq, ps_s, B_pss, rstd, B_rstd):
        t0, t1 = BLKS[bi]
        w = t1 - t0
        ci = 1 if bi == 0 else 0
        S.op("act", lambda e: e.activation(out=sq[:, :, 0:w], in_=xT[:, :, t0:t1], func=AF.Square),
             reads=[XB[bi]], writes=[B_sq])
        for kc in range(KC):
            S.op("pe", lambda e, kc=kc: e.matmul(ps_s[:, 0:w], ones_bf[:], sq[:, kc, 0:w], start=(kc == 0), stop=(kc == KC - 1)),
                 reads=[B_sq, B_const], writes=[B_pss])
        S.op("act", lambda e: e.activation(out=rstd[:, 0:w], in_=ps_s[:, 0:w], func=AF.Sqrt, bias=EPS, scale=1.0 / D),
             reads=[B_pss], writes=[B_rstd])
        S.op("dve", lambda e: e.reciprocal(out=rstd[:, 0:w], in_=rstd[:, 0:w]), reads=[B_rstd], writes=[B_rstd])
        for kc in range(KC):
            tm_, Btm_ = tmp[kc % 2], B_tmp[kc % 2]
            S.op("dve", lambda e, kc=kc: e.tensor_tensor(out=tm_[:, 0:w], in0=xT[:, kc, t0:t1], in1=rstd[:, 0:w], op=ALU.mult),
                 reads=[XB[bi], B_rstd], writes=[Btm_])
            S.op("act", lambda e, kc=kc: e.activation(out=dst[:, kc, dst_t0:dst_t0 + w], in_=tm_[:, 0:w], func=AF.Identity,
                                                      scale=gs[:, kc, ci:ci + 1], bias=mod[:, sh_chunk0 + kc, ci:ci + 1]),
                 reads=[Btm_, B_mod], writes=[dst_buf])

    ATTN_SCALE = 96.0 ** -0.5

    def headnorm(src, H, n, gain, dst, scr, B_src, B_dst, B_scr):
        sqt, ssq = scr
        S.op("dve", lambda e: e.tensor_tensor(out=sqt[:, 0:H, 0:n], in0=src, in1=src, op=ALU.mult), reads=[B_src], writes=[B_scr])
        yield
        S.op("dve", lambda e: e.tensor_reduce(out=ssq[:, 0:H], in_=sqt[:, 0:H, 0:n], axis=mybir.AxisListType.X, op=ALU.add),
             reads=[B_scr], writes=[B_scr])
        yield
        S.op("act", lambda e: e.activation(out=ssq[:, 0:H], in_=ssq[:, 0:H], func=AF.Sqrt, bias=EPS, scale=1.0 / n),
             reads=[B_scr], writes=[B_scr])
        yield
        S.op("dve", lambda e: e.reciprocal(out=ssq[:, 0:H], in_=ssq[:, 0:H]), reads=[B_scr], writes=[B_scr])
        yield
        S.op("dve", lambda e: e.tensor_tensor(out=sqt[:, 0:H, 0:n], in0=src, in1=ssq[:, 0:H].unsqueeze(2).to_broadcast([128, H, n]), op=ALU.mult),
             reads=[B_src, B_scr], writes=[B_scr])
        yield
        S.op("dve", lambda e: e.tensor_tensor(out=dst, in0=sqt[:, 0:H, 0:n], in1=gain.unsqueeze(1).to_broadcast([128, H, n]), op=ALU.mult),
             reads=[B_scr, B_const], writes=[B_dst])
        yield

    CDEC = 0.606531
    GN_EPS = 64e-5
    HT = 1152
    TBH = [(0, 512), (512, 1024), (1024, 1152)]

    def rwkv_prep(l):
        with ExitStack() as es:
            dup = sb(es, "dup", [64, 2, 384], BF16)
            iup = sb(es, "iup", [64, 2, 384], BF16)
            gup = sb(es, "gup", [128, 384], BF16)
            bd = sb(es, "bd", [128, 128], BF16)
            B_w = Buf("rw_w")
            S.dma("pool", dup[:], dup_d.ap()[l].rearrange("d k n -> k d n"), [], B_w)
            S.dma("pool", iup[:], iup_d.ap()[l].rearrange("d k n -> k d n"), [], B_w)
            S.dma("pool", gup[:], gup_d.ap()[l], [], B_w)
            S.op("dve", lambda e: e.memset(bd[:], 0.0), writes=[B_w])
            S.op("dve", lambda e: e.memset(bd[0:64, 0:64], 1.0), writes=[B_w])
            S.op("dve", lambda e: e.memset(bd[64:128, 64:128], 1.0), writes=[B_w])
            tw = sb(es, "tw", [64, T], BF16)
            al = sb(es, "al", [64, T], BF16)
            sgl = sb(es, "sgl", [128, T], BF16)
            B_lo = Buf("lo")
            with ExitStack() as es0:
                lin = sb(es0, "lin", [128, T], F32)
                B_lin = Buf("lin")
                S.dma("sp", lin[0:64, :], pA_d.ap()[1152:1216, :], [B_pA], B_lin)
                S.op("act", lambda e: e.activation(out=tw[:], in_=lin[0:64, :], func=AF.Tanh), reads=[B_lin], writes=[B_lo])
                S.dma("sp", lin[0:64, :], pA_d.ap()[1216:1280, :], [B_pA], B_lin)
                S.op("act", lambda e: e.activation(out=al[:], in_=lin[0:64, :], func=AF.Copy), reads=[B_lin], writes=[B_lo])
                S.dma("sp", lin[:, :], pA_d.ap()[1280:1408, :], [B_pA], B_lin)
                S.op("act", lambda e: e.activation(out=sgl[:], in_=lin[:, :], func=AF.Sigmoid), reads=[B_lin], writes=[B_lo])
                S.barrier()
            names = ("r", "k", "v", "kk", "asum", "xs")
            tl = {n: sb(es, "rp_" + n, [128, HT], F32) for n in names}
            Bt = {n: Buf("rp_" + n) for n in names}
            tld, Btd = [], []
            for d_ in range(2):
                td = {n: sb(es, "rp_%s%d" % (n, d_), [128, HT], F32) for n in ("sw", "ai", "P", "E", "F", "x1", "x2")}
                bd_ = {n: Buf("rp_%s%d" % (n, d_)) for n in ("sw", "ai", "P", "E", "F", "x1", "x2")}
                td["g1"], bd_["g1"] = td["P"], bd_["P"]
                td["g2"], bd_["g2"] = td["sw"], bd_["sw"]
                tld.append(td)
                Btd.append(bd_)
            ob = [sb(es, "rp_ob%d" % i, [128, HT], BF16) for i in range(3)]
            Bob = [Buf("rp_ob%d" % i) for i in range(3)]
            sqb = sb(es, "rp_sqb", [128, HT], BF16)
            B_sqb = Buf("rp_sqb")
            base = [sb(es, "rp_base%d" % i, [128, 18], F32) for i in range(2)]
            gct = [sb(es, "rp_gct%d" % i, [128, 18], F32) for i in range(2)]
            B_base = [Buf("rp_base%d" % i) for i in range(2)]
            B_gct = [Buf("rp_gct%d" % i) for i in range(2)]
            pm = [ps(es, "rp_pm%d" % i, [128, 512]) for i in range(4)]
            Bpm = [Buf("rp_pm%d" % i) for i in range(4)]
            ptrs = [ps(es, "rp_ptr%d" % i, [64, 1024], BF16) for i in range(2)]
            B_ptrs = [Buf("rp_ptr%d" % i) for i in range(2)]
            stgs = [sb(es, "rp_stg%d" % i, [64, 8, 128], BF16) for i in range(3)]
            B_stgs = [Buf("rp_stg%d" % i) for i in range(3)]
            st = {"p": 0, "ob": 0, "c0": 0, "hf": 0, "tr": 0}

            def mm_full(lhsT, rhs_tile, B_rhs, dst, B_dst, func, bias=None, K=64):
                for (t0, t1) in TBH:
                    w = t1 - t0
                    p = st["p"] % 4
                    st["p"] += 1
                    S.op("pe", lambda e, p=p: e.matmul(pm[p][:, 0:w], lhsT, rhs_tile[0:K, st["c0"] + t0:st["c0"] + t1], start=True, stop=True),
                         reads=[B_rhs, B_w], writes=[Bpm[p]])
                    if bias is None:
                        S.op("act", lambda e, p=p: e.activation(out=dst[:, t0:t1], in_=pm[p][:, 0:w], func=func), reads=[Bpm[p]], writes=[B_dst])
                    else:
                        S.op("act", lambda e, p=p: e.activation(out=dst[:, t0:t1], in_=pm[p][:, 0:w], func=func, bias=bias, scale=1.0),
                             reads=[Bpm[p], B_pp], writes=[B_dst])

            def tt(eng, out, a_, b_, op, rd, wr):
                S.op(eng, lambda e: e.tensor_tensor(out=out, in0=a_, in1=b_, op=op), reads=rd, writes=wr)

            def to_tm(src_bf, B_src, dst_fn):
                for g0 in range(0, 18, 8):
                    n = min(8, 18 - g0)
                    ptr, B_ptr = ptrs[st["tr"] % 2], B_ptrs[st["tr"] % 2]
                    stg, B_stg = stgs[st["tr"] % 3], B_stgs[st["tr"] % 3]
                    st["tr"] += 1
                    for i in range(n):
                        c = g0 + i
                        S.op("pe", lambda e, i=i, c=c: e.transpose(ptr[0:64, i * 128:(i + 1) * 128], src_bf[:, c * 64:(c + 1) * 64], ident_b[:]),
                             reads=[B_src, B_const], writes=[B_ptr])
                    S.op("act", lambda e: e.activation(out=stg[:, 0:n, :], in_=ptr[0:64, 0:n * 128].rearrange("p (a b) -> p a b", b=128), func=AF.Copy),
                         reads=[B_ptr], writes=[B_stg])
                    S.dma("sp", dst_fn(st["hf"] * 18 + g0, n), stg[:, 0:n, :], [B_stg], B_tm)

            def out_bf(src_fn):
                i = st["ob"] % 3
                st["ob"] += 1
                src_fn(ob[i][:], Bob[i])
                return ob[i], Bob[i]

            def fm_store(o, Bo, d, hp, q, c0):
                for hl in range(2):
                    S.dma("sp", fm_d.ap()[d, :, (2 * hp + hl) * 4 + q, c0:c0 + HT], o[hl * 64:(hl + 1) * 64, :], [Bo], B_fm)

            def gen_dir(d, hp, hf, c0):
                T_, B_ = tld[d], Btd[d]
                r_, k_, kk_ = tl["r"], tl["k"], tl["kk"]
                sw, ai, P, E, F_, g1, g2, x1, x2 = (T_[n] for n in ("sw", "ai", "P", "E", "F", "g1", "g2", "x1", "x2"))
                mm_full(dup[:, d, hp * 128:(hp + 1) * 128], tw, B_lo, sw, B_["sw"], AF.Sigmoid, ppc(l, "w0", d * 3 + hp))
                yield
                mm_full(iup[:, d, hp * 128:(hp + 1) * 128], al, B_lo, ai, B_["ai"], AF.Sigmoid, ppc(l, "a0", d * 3 + hp))
                yield
                S.op("dve", lambda e: e.tensor_tensor_scan(out=P[:], data0=sw[:], data1=sw[:], initial=0.0, op0=ALU.add, op1=ALU.max),
                     reads=[B_["sw"]], writes=[B_["P"]])
                yield
                pend = P[:].rearrange("p (c t) -> p c t", t=64)[:, :, 63]
                bs, gc, Bb, Bg = base[d], gct[d], B_base[d], B_gct[d]
                if d == 0:
                    S.op("dve", lambda e: e.memset(bs[:, 0:1], 0.0), writes=[Bb])
                    S.op("dve", lambda e: e.tensor_copy(out=bs[:, 1:18], in_=pend[:, 0:17]), reads=[B_["P"]], writes=[Bb])
                    yield
                    S.op("dve", lambda e: e.tensor_tensor(out=gc[:], in0=pend, in1=bs[:], op=ALU.subtract), reads=[B_["P"], Bb], writes=[Bg])
                else:
                    S.op("dve", lambda e: e.tensor_copy(out=gc[:, 0:1], in_=pend[:, 0:1]), reads=[B_["P"]], writes=[Bg])
                    S.op("dve", lambda e: e.tensor_tensor(out=gc[:, 1:18], in0=pend[:, 1:18], in1=pend[:, 0:17], op=ALU.subtract), reads=[B_["P"]], writes=[Bg])
                    yield
                    S.op("dve", lambda e: e.tensor_copy(out=bs[:], in_=pend), reads=[B_["P"]], writes=[Bb])
                yield
                S.op("act", lambda e: e.activation(out=gc[:], in_=gc[:], func=AF.Exp, scale=-CDEC), reads=[Bg], writes=[Bg])
                for hl in range(2):
                    S.dma("sp", gC_d.ap()[:, d, 2 * hp + hl, hf * 18:(hf + 1) * 18], gc[hl * 64:(hl + 1) * 64, :], [Bg], B_gC)
                tt("dve", E[:].rearrange("p (c t) -> p c t", t=64), P[:].rearrange("p (c t) -> p c t", t=64),
                   bs[:].unsqueeze(2).to_broadcast([128, 18, 64]), ALU.subtract, [B_["P"], Bb], [B_["E"]])
                yield
                tt("pool", F_[:], E[:], sw[:], ALU.subtract, [B_["E"], B_["sw"]], [B_["F"]])
                yield
                if d == 0:
                    li, si, le, se = E, -CDEC, F_, -CDEC
                    Bli, Ble = B_["E"], B_["F"]
                else:
                    li, si, le, se = F_, CDEC, E, CDEC
                    Bli, Ble = B_["F"], B_["E"]
                S.op("act", lambda e: e.activation(out=g1[:], in_=li[:], func=AF.Exp, scale=si), reads=[Bli], writes=[B_["g1"]])
                yield
                S.op("act", lambda e: e.activation(out=g2[:], in_=li[:], func=AF.Exp, scale=-si), reads=[Bli], writes=[B_["g2"]])
                yield
                S.op("act", lambda e: e.activation(out=x1[:], in_=le[:], func=AF.Exp, scale=se), reads=[Ble], writes=[B_["x1"]])
                yield
                o, Bo = out_bf(lambda o, B: S.op("dve", lambda e: e.scalar_tensor_tensor(out=o, in0=kk_[:], scalar=-1.0, in1=x1[:], op0=ALU.mult, op1=ALU.mult),
                                                 reads=[Bt["kk"], B_["x1"]], writes=[B]))
                fm_store(o, Bo, d, hp, 0, c0)
                yield
                o, Bo = out_bf(lambda o, B: tt("pool", o, r_[:], g1[:], ALU.mult, [Bt["r"], B_["g1"]], [B]))
                fm_store(o, Bo, d, hp, 1, c0)
                yield
                tt("dve", x2[:], kk_[:], ai[:], ALU.mult, [Bt["kk"], B_["ai"]], [B_["x2"]])
                yield
                o, Bo = out_bf(lambda o, B: tt("dve", o, x2[:], g2[:], ALU.mult, [B_["x2"], B_["g2"]], [B]))
                fm_store(o, Bo, d, hp, 2, c0)
                to_tm(o, Bo, lambda g0, n: tmBK_d.ap()[g0:g0 + n, :, d, 0, 2 * hp:2 * hp + 2, :].rearrange("c t h k -> t c (h k)"))
                yield
                S.op("dve", lambda e: e.tensor_scalar(out=x2[:], in0=ai[:], scalar1=-1.0, scalar2=ppc(l, "ka", hp), op0=ALU.add, op1=ALU.mult),
                     reads=[B_["ai"], B_pp], writes=[B_["x2"]])
                yield
                S.op("dve", lambda e: e.scalar_tensor_tensor(out=x2[:], in0=x2[:], scalar=1.0, in1=k_[:], op0=ALU.add, op1=ALU.mult),
                     reads=[B_["x2"], Bt["k"]], writes=[B_["x2"]])
                yield
                o, Bo = out_bf(lambda o, B: tt("pool", o, x2[:], g2[:], ALU.mult, [B_["x2"], B_["g2"]], [B]))
                fm_store(o, Bo, d, hp, 3, c0)
                to_tm(o, Bo, lambda g0, n: tmBK_d.ap()[g0:g0 + n, :, d, 1, 2 * hp:2 * hp + 2, :].rearrange("c t h k -> t c (h k)"))
                yield

            for it_ in range(6):
                hp, hf = it_ // 2, it_ % 2
                c0 = hf * HT
                st["c0"], st["hf"] = c0, hf
                r_, k_, v_, kk_ = tl["r"], tl["k"], tl["v"], tl["kk"]
                S.dma("sp", r_[:], pA_d.ap()[hp * 128:(hp + 1) * 128, c0:c0 + HT], [B_pA], Bt["r"])
                S.dma("sp", k_[:], pA_d.ap()[384 + hp * 128:384 + (hp + 1) * 128, c0:c0 + HT], [B_pA], Bt["k"])
                S.dma("sp", v_[:], pA_d.ap()[768 + hp * 128:768 + (hp + 1) * 128, c0:c0 + HT], [B_pA], Bt["v"])
                vb, Bvb = out_bf(lambda o, B: S.op("act", lambda e: e.activation(out=o, in_=v_[:], func=AF.Copy), reads=[Bt["v"]], writes=[B]))
                to_tm(vb, Bvb, lambda g0, n: tmV_d.ap()[g0:g0 + n, :, 2 * hp:2 * hp + 2, :].rearrange("c t h k -> t c (h k)"))
                S.op("act", lambda e: e.activation(out=kk_[:], in_=k_[:], func=AF.Identity, scale=ppc(l, "kk", hp), bias=0.0),
                     reads=[Bt["k"], B_pp], writes=[Bt["kk"]])
                S.op("act", lambda e: e.activation(out=sqb[:], in_=kk_[:], func=AF.Square), reads=[Bt["kk"]], writes=[B_sqb])
                for (t0, t1) in TBH:
                    w = t1 - t0
                    p = st["p"] % 4
                    st["p"] += 1
                    S.op("pe", lambda e, p=p: e.matmul(pm[p][:, 0:w], bd[:], sqb[:, t0:t1], start=True, stop=True), reads=[B_sqb, B_w], writes=[Bpm[p]])
                    S.op("act", lambda e, p=p: e.activation(out=tl["xs"][:, t0:t1], in_=pm[p][:, 0:w], func=AF.Sqrt, bias=1e-12, scale=1.0),
                         reads=[Bpm[p]], writes=[Bt["xs"]])
                S.op("dve", lambda e: e.reciprocal(out=tl["xs"][:], in_=tl["xs"][:]), reads=[Bt["xs"]], writes=[Bt["xs"]])
                tt("dve", kk_[:], kk_[:], tl["xs"][:], ALU.mult, [Bt["kk"], Bt["xs"]], [Bt["kk"]])
                run_streams([gen_dir(0, hp, hf, c0), gen_dir(1, hp, hf, c0)])
                tt("dve", tl["asum"][:], tld[0]["x2"][:], tld[1]["x2"][:], ALU.add, [Btd[0]["x2"], Btd[1]["x2"]], [Bt["asum"]])
                S.op("dve", lambda e: e.scalar_tensor_tensor(out=sqb[:], in0=tl["asum"][:], scalar=ppc(l, "rk", hp), in1=r_[:], op0=ALU.mult, op1=ALU.mult),
                     reads=[Bt["asum"], Bt["r"], B_pp], writes=[B_sqb])
                for (t0, t1) in TBH:
                    w = t1 - t0
                    p = st["p"] % 4
                    st["p"] += 1
                    S.op("pe", lambda e, p=p: e.matmul(pm[p][:, 0:w], bd[:], sqb[:, t0:t1], start=True, stop=True), reads=[B_sqb, B_w], writes=[Bpm[p]])
                    S.op("dve", lambda e, p=p: e.tensor_tensor(out=tl["xs"][:, t0:t1], in0=pm[p][:, 0:w], in1=v_[:, t0:t1], op=ALU.mult),
                         reads=[Bpm[p], Bt["v"]], writes=[Bt["xs"]])
                S.dma("sp", bon_d.ap()[hp * 128:(hp + 1) * 128, c0:c0 + HT], tl["xs"][:], [Bt["xs"]], B_bg)
                mm_full(gup[:, hp * 128:(hp + 1) * 128], sgl, B_lo, tl["asum"], Bt["asum"], AF.Copy, None, K=128)
                S.dma("sp", gat_d.ap()[hp * 128:(hp + 1) * 128, c0:c0 + HT], tl["asum"][:], [Bt["asum"]], B_bg)
            S.barrier()

    def run_streams(streams):
        streams = [iter(x) for x in streams]
        while streams:
            for x in list(streams):
                try:
                    next(x)
                except StopIteration:
                    streams.remove(x)

    def rwkv_scan(l):
        with ExitStack() as es:
            masks = sb(es, "masks", [64, 2, 192], F32)
            gC = sb(es, "gC", [64, 2, 6, 36], F32)
            B_ld0 = Buf("sc_ld0")
            S.dma("sp", masks[:], masks_d.ap(), [], B_ld0)
            S.dma("sp", gC[:], gC_d.ap(), [B_gC], B_ld0)
            Hf = sb(es, "Hf", [64, 2, 6, 64], F32)
            Hb = [sb(es, "Hb%d" % i, [64, 2, 6, 64], BF16) for i in range(2)]
            B_Hf = [Buf("Hf%d" % d) for d in range(2)]
            B_Hb = [[Buf("Hb%d_%d" % (i, d)) for d in range(2)] for i in range(2)]
            S.op("dve", lambda e: e.memset(Hf[:], 0.0), writes=B_Hf)
            S.op("dve", lambda e: e.memset(Hb[0][:], 0.0), writes=B_Hb[0])
            S.op("dve", lambda e: e.memset(Hb[1][:], 0.0), writes=B_Hb[1])
            NB = 3
            fmt = [[sb(es, "fmt%d_%d" % (i, d), [64, 6, 4, 64], BF16) for d in range(2)] for i in range(NB)]
            tbk = [[sb(es, "tbk%d_%d" % (i, d), [64, 2, 6, 64], BF16) for d in range(2)] for i in range(NB)]
            tv = [[sb(es, "tv%d_%d" % (i, d), [64, 6, 64], BF16) for d in range(2)] for i in range(NB)]
            B_in = [[Buf("scin%d_%d" % (i, d)) for d in range(2)] for i in range(NB)]
            SC1 = [[sb(es, "SC1_%d_%d" % (p, d), [64, 6, 128], BF16) for d in range(2)] for p in range(2)]
            SC2 = [[sb(es, "SC2_%d_%d" % (p, d), [64, 6, 128], BF16) for d in range(2)] for p in range(2)]
            TTb = [[sb(es, "TTb_%d_%d" % (p, d), [64, 6, 64], BF16) for d in range(2)] for p in range(2)]
            B_S1 = [[[Buf("sc_S1_%d_%d_%d" % (p, d, hf)) for hf in range(2)] for d in range(2)] for p in range(2)]
            B_S2 = [[[Buf("sc_S2_%d_%d_%d" % (p, d, hf)) for hf in range(2)] for d in range(2)] for p in range(2)]
            B_TT = [[Buf("sc_TT_%d_%d" % (p, d)) for d in range(2)] for p in range(2)]
            PM = [sb(es, "PM_%d" % d, [64, 6, 128], BF16) for d in range(2)]
            B_PM = [[Buf("sc_PM_%d_%d" % (d, hf)) for hf in range(2)] for d in range(2)]
            ZXb = [sb(es, "ZXb_%d" % d, [64, 6, 64], BF16) for d in range(2)]
            Ub = [sb(es, "Ub_%d" % d, [64, 6, 64], BF16) for d in range(2)]
            Ysb = [sb(es, "Ysb_%d" % d, [64, 384], F32) for d in range(2)]
            Bc = {n: [Buf("sc_%s_%d" % (n, d)) for d in range(2)] for n in ("ZXb", "Ub", "Ysb")}
            pG1 = [ps(es, "pG1_%d" % d, [64, 1024]) for d in range(2)]
            pG3 = [ps(es, "pG3_%d" % d, [64, 512]) for d in range(2)]
            pZ = ps(es, "pZ", [64, 512])
            pY = ps(es, "pY", [64, 512])
            B_pG1 = [Buf("pG1_%d" % d) for d in range(2)]
            B_pG3 = [Buf("pG3_%d" % d) for d in range(2)]
            B_pZ, B_pY = Buf("pZ"), Buf("pY")

            def chunk_of(i, d):
                if d == 0:
                    return i
                return 3 - i if i < 4 else 39 - i

            def loads(i):
                sl = i % NB
                for d in range(2):
                    c = chunk_of(i, d)
                    cs = c * 64
                    Bi = B_in[sl][d]
                    S.dma("sp", fmt[sl][d][:].rearrange("p h q t -> p (h q) t"), fm_d.ap()[d, :, :, cs:cs + 64], [B_fm], Bi)
                    S.dma("sp", tbk[sl][d][:], tmBK_d.ap()[c, :, d], [B_tm], Bi)
                    S.dma("sp", tv[sl][d][:], tmV_d.ap()[c], [B_tm], Bi)

            def gen_pre(i, d):
                sl, par = i % NB, i % 2
                F_, Bi = fmt[sl][d], B_in[sl][d]
                s1, s2, ttb = SC1[par][d], SC2[par][d], TTb[par][d]
                bS1, bS2, bTT, bPM = B_S1[par][d], B_S2[par][d], B_TT[par][d], B_PM[d]
                g1, g3, bg1, bg3 = pG1[d], pG3[d], B_pG1[d], B_pG3[d]
                m1 = masks[:, d, 0:128].unsqueeze(1).to_broadcast([64, 6, 128])
                m3 = masks[:, d, 128:192].unsqueeze(1).to_broadcast([64, 6, 64])
                g1v = g1[:, 0:768].rearrange("p (a b) -> p a b", b=128)
                g3v = g3[:, 0:384].rearrange("p (a b) -> p a b", b=64)
                for h in range(6):
                    S.op("pe", lambda e, h=h: e.matmul(g1[:, h * 128:(h + 1) * 128], F_[:, h, 2, :], F_[:, h, 0:2, :].rearrange("p a b -> p (a b)"), start=True, stop=True),
                         reads=[Bi], writes=[bg1])
                for (ha, hb) in ((0, 4), (4, 6)):
                    S.op("dve", lambda e, ha=ha, hb=hb: e.tensor_tensor(out=s1[:, ha:hb, :], in0=g1v[:, ha:hb, :], in1=m1[:, ha:hb, :], op=ALU.mult), reads=[bg1, B_ld0], writes=[bS1[0 if ha == 0 else 1]])
                yield
                for h in range(6):
                    S.op("pe", lambda e, h=h: e.matmul(g1[:, h * 128:(h + 1) * 128], F_[:, h, 3, :], F_[:, h, 0:2, :].rearrange("p a b -> p (a b)"), start=True, stop=True),
                         reads=[Bi], writes=[bg1])
                for (ha, hb) in ((0, 4), (4, 6)):
                    S.op("dve", lambda e, ha=ha, hb=hb: e.tensor_tensor(out=s2[:, ha:hb, :], in0=g1v[:, ha:hb, :], in1=m1[:, ha:hb, :], op=ALU.mult), reads=[bg1, B_ld0], writes=[bS2[0 if ha == 0 else 1]])
                yield
                for h in range(6):
                    S.op("pe", lambda e, h=h: e.matmul(g3[:, h * 64:(h + 1) * 64], F_[:, h, 0, :], F_[:, h, 2, :], start=True, stop=True),
                         reads=[Bi], writes=[bg3])
                S.op("dve", lambda e: e.tensor_tensor(out=PM[d][:, :, 0:64], in0=g3v, in1=m3, op=ALU.mult), reads=[bg3, B_ld0], writes=bPM)
                S.op("act", lambda e: e.activation(out=PM[d][:, :, 64:128], in_=s1[:, :, 0:64], func=AF.Copy), reads=bS1, writes=bPM)
                S.op("dve", lambda e: e.tensor_tensor(out=ttb[:], in0=s1[:, :, 0:64], in1=ident_f[0:64, 0:64].unsqueeze(1).to_broadcast([64, 6, 64]), op=ALU.add),
                     reads=bS1 + [B_id], writes=[bTT])
                yield
                for it in range(5):
                    lastit = it == 4
                    for h in range(6):
                        S.op("pe", lambda e, h=h: e.matmul(g1[:, h * 128:h * 128 + 64], PM[d][:, h, 64:128], PM[d][:, h, 0:64], start=True, stop=True),
                             reads=[bPM[h // 4]], writes=[bg1])
                        if not lastit:
                            S.op("pe", lambda e, h=h: e.matmul(g1[:, h * 128 + 64:(h + 1) * 128], PM[d][:, h, 0:64], PM[d][:, h, 64:128], start=True, stop=True),
                                 reads=[bPM[h // 4]], writes=[bg1])
                    wc = 64 if lastit else 128
                    for (ha, hb) in ((0, 4), (4, 6)):
                        S.op("act", lambda e, ha=ha, hb=hb: e.activation(out=PM[d][:, ha:hb, 0:wc], in_=g1v[:, ha:hb, 0:wc], func=AF.Copy), reads=[bg1], writes=[bPM[0 if ha == 0 else 1]])
                    yield
                    for h in range(6):
                        S.op("pe", lambda e, h=h: e.matmul(g3[:, h * 64:(h + 1) * 64], PM[d][:, h, 0:64], ttb[:, h, :], start=True, stop=True),
                             reads=[bPM[h // 4], bTT], writes=[bg3])
                    S.op("dve", lambda e: e.tensor_tensor(out=ttb[:], in0=ttb[:], in1=g3v, op=ALU.add), reads=[bTT, bg3], writes=[bTT])
                    yield

            def gen_chain(i):
                sl, par = i % NB, i % 2
                hb_old, hb_new = Hb[i % 2], Hb[(i + 1) % 2]
                for d in range(2):
                    c = chunk_of(i, d)
                    cs = c * 64
                    F_, BK, V_, Bi = fmt[sl][d], tbk[sl][d], tv[sl][d], B_in[sl][d]
                    s1, s2, ttb = SC1[par][d], SC2[par][d], TTb[par][d]
                    bS1, bS2, bTT = B_S1[par][d], B_S2[par][d], B_TT[par][d]
                    bHo, bHn = B_Hb[i % 2][d], B_Hb[(i + 1) % 2][d]
                    for h in range(6):
                        S.op("pe", lambda e, h=h: e.matmul(pZ[:, h * 64:(h + 1) * 64], s2[:, h, 0:64], V_[:, h, :], start=True, stop=False),
                             reads=[bS2[h // 4], Bi], writes=[B_pZ])
                        S.op("pe", lambda e, h=h: e.matmul(pZ[:, h * 64:(h + 1) * 64], F_[:, h, 0, :], hb_old[:, d, h, :], start=False, stop=True),
                             reads=[Bi, bHo], writes=[B_pZ])
                    for h in range(6):
                        S.op("pe", lambda e, h=h: e.matmul(pY[:, h * 64:(h + 1) * 64], F_[:, h, 1, :], hb_old[:, d, h, :], start=(h == 0), stop=False, skip_group_check=True),
                             reads=[Bi, bHo], writes=[B_pY])
                        S.op("pe", lambda e, h=h: e.matmul(pY[:, h * 64:(h + 1) * 64], s2[:, h, 64:128], V_[:, h, :], start=False, stop=False, skip_group_check=True),
                             reads=[bS2[h // 4], Bi], writes=[B_pY])
                    S.op("act", lambda e: e.activation(out=ZXb[d][:].rearrange("p a b -> p (a b)"), in_=pZ[:, 0:384], func=AF.Copy), reads=[B_pZ], writes=[Bc["ZXb"][d]])
                    yield
                    for h in range(6):
                        S.op("pe", lambda e, h=h: e.matmul(pZ[:, h * 64:(h + 1) * 64], ttb[:, h, :], ZXb[d][:, h, :], start=True, stop=True),
                             reads=[bTT, Bc["ZXb"][d]], writes=[B_pZ])
                    S.op("act", lambda e: e.activation(out=Ub[d][:].rearrange("p a b -> p (a b)"), in_=pZ[:, 0:384], func=AF.Copy), reads=[B_pZ], writes=[Bc["Ub"][d]])
                    yield
                    for h in range(6):
                        S.op("pe", lambda e, h=h: e.matmul(pZ[:, h * 64:(h + 1) * 64], BK[:, 0, h, :], Ub[d][:, h, :], start=True, stop=False),
                             reads=[Bi, Bc["Ub"][d]], writes=[B_pZ])
                        S.op("pe", lambda e, h=h: e.matmul(pZ[:, h * 64:(h + 1) * 64], BK[:, 1, h, :], V_[:, h, :], start=False, stop=True),
                             reads=[Bi], writes=[B_pZ])
                    for h in range(6):
                        S.op("pe", lambda e, h=h: e.matmul(pY[:, h * 64:(h + 1) * 64], s1[:, h, 64:128], Ub[d][:, h, :], start=False, stop=True, skip_group_check=True),
                             reads=[bS1[h // 4], Bc["Ub"][d]], writes=[B_pY])
                    S.op("dve", lambda e: e.tensor_tensor(out=Hf[:, d].rearrange("p a b -> p (a b)"), in0=Hf[:, d].rearrange("p a b -> p (a b)"), in1=pZ[:, 0:384], op=ALU.add),
                         reads=[B_Hf[d], B_pZ], writes=[B_Hf[d]])
                    S.op("dve", lambda e: e.tensor_tensor(out=Hf[:, d], in0=Hf[:, d], in1=gC[:, d, :, c:c + 1].to_broadcast([64, 6, 64]), op=ALU.mult),
                         reads=[B_Hf[d], B_ld0], writes=[B_Hf[d]])
                    S.op("act", lambda e: e.activation(out=hb_new[:, d], in_=Hf[:, d], func=AF.Copy), reads=[B_Hf[d]], writes=[bHn])
                    S.op("act", lambda e: e.activation(out=Ysb[d][:], in_=pY[:, 0:384], func=AF.Copy), reads=[B_pY], writes=[Bc["Ysb"][d]])
                    S.dma("sp", Y_d.ap()[d, cs:cs + 64, :], Ysb[d][:], [Bc["Ysb"][d]], B_Y)
                    yield

            loads(0)
            loads(1)
            run_streams([gen_pre(0, 0), gen_pre(0, 1)])
            for i in range(36 if not DBG.get("skip_scan") else 0):
                if i + 2 < 36:
                    loads(i + 2)
                streams = [gen_chain(i)]
                if i + 1 < 36:
                    streams += [gen_pre(i + 1, 0), gen_pre(i + 1, 1)]
                run_streams(streams)
            S.barrier()

    def rwkv_post(l):
        with ExitStack() as es:
            yf = [sb(es, "yf%d" % i, [128, 384], F32) for i in range(2)]
            yb = [sb(es, "yb%d" % i, [128, 384], F32) for i in range(2)]
            Byl = [Buf("po_yl%d" % i) for i in range(2)]
            ysq = [sb(es, "ysq%d" % i, [128, 384], F32) for i in range(2)]
            ynb = [sb(es, "ynb%d" % i, [128, 384], BF16) for i in range(2)]
            st6 = [sb(es, "st6%d" % i, [128, 6], F32) for i in range(2)]
            Bw_ = [Buf("po_w%d" % i) for i in range(2)]
            ylT = sb(es, "ylT", [128, 3, T], F32)
            B_ylT = Buf("po_ylT")
            ptt = [ps(es, "po_ptt%d" % i, [128, 1024], BF16) for i in range(2)]
            B_ptt = [Buf("po_ptt%d" % i) for i in range(2)]

            def gen_tile(tt_, s_):
                t0 = tt_ * 128
                B_w = Bw_[s_]
                S.dma("sp", yf[s_][:], Y_d.ap()[0, t0:t0 + 128, :], [B_Y], Byl[s_])
                S.dma("sp", yb[s_][:], Y_d.ap()[1, t0:t0 + 128, :], [B_Y], Byl[s_])
                y3 = yf[s_][:].rearrange("p (h c) -> p h c", c=64)
                q3 = ysq[s_][:].rearrange("p (h c) -> p h c", c=64)
                sx = st6[s_]
                S.op("dve", lambda e: e.tensor_tensor(out=yf[s_][:], in0=yf[s_][:], in1=yb[s_][:], op=ALU.add), reads=[Byl[s_]], writes=[Byl[s_]])
                yield
                S.op("dve", lambda e: e.tensor_reduce(out=sx[:], in_=y3, axis=mybir.AxisListType.X, op=ALU.add), reads=[Byl[s_]], writes=[B_w])
                yield
                S.op("dve", lambda e: e.tensor_scalar(out=sx[:], in0=sx[:], scalar1=1.0 / 64, scalar2=None, op0=ALU.mult), reads=[B_w], writes=[B_w])
                yield
                S.op("dve", lambda e: e.tensor_tensor(out=y3, in0=y3, in1=sx[:].unsqueeze(2).to_broadcast([128, 6, 64]), op=ALU.subtract), reads=[Byl[s_], B_w], writes=[Byl[s_]])
                yield
                S.op("dve", lambda e: e.tensor_tensor(out=ysq[s_][:], in0=yf[s_][:], in1=yf[s_][:], op=ALU.mult), reads=[Byl[s_]], writes=[B_w])
                yield
                S.op("dve", lambda e: e.tensor_reduce(out=sx[:], in_=q3, axis=mybir.AxisListType.X, op=ALU.add), reads=[B_w], writes=[B_w])
                yield
                S.op("act", lambda e: e.activation(out=sx[:], in_=sx[:], func=AF.Sqrt, bias=GN_EPS, scale=1.0 / 64), reads=[B_w], writes=[B_w])
                yield
                S.op("dve", lambda e: e.reciprocal(out=sx[:], in_=sx[:]), reads=[B_w], writes=[B_w])
                yield
                S.op("dve", lambda e: e.tensor_tensor(out=ynb[s_][:].rearrange("p (h c) -> p h c", c=64), in0=y3, in1=sx[:].unsqueeze(2).to_broadcast([128, 6, 64]), op=ALU.mult),
                     reads=[Byl[s_], B_w], writes=[B_w])
                yield
                for j in range(3):
                    S.op("pe", lambda e, j=j: e.transpose(ptt[s_][:, j * 128:(j + 1) * 128], ynb[s_][:, j * 128:(j + 1) * 128], ident_b[:]), reads=[B_w, B_const], writes=[B_ptt[s_]])
                for j in range(3):
                    S.op("act", lambda e, j=j: e.activation(out=ylT[:, j, t0:t0 + 128], in_=ptt[s_][:, j * 128:(j + 1) * 128], func=AF.Identity,
                                                            scale=ppc(l, "lnxg", j), bias=ppc(l, "lnxb", j)), reads=[B_ptt[s_], B_pp], writes=[B_ylT])
                yield

            for pr in range(9):
                run_streams([gen_tile(2 * pr, 0), gen_tile(2 * pr + 1, 1)])
            bo = sb(es, "po_bo", [128, T], F32)
            ga = sb(es, "po_ga", [128, T], F32)
            ob_ = sb(es, "po_ob", [128, T], BF16)
            B_bo, B_ob = Buf("po_bo"), Buf("po_ob")
            for j in range(3):
                S.dma("sp", bo[:], bon_d.ap()[j * 128:(j + 1) * 128, :], [B_bg], B_bo)
                S.dma("sp", ga[:], gat_d.ap()[j * 128:(j + 1) * 128, :], [B_bg], B_bo)
                S.op("dve", lambda e: e.tensor_tensor(out=bo[:], in0=bo[:], in1=ylT[:, j, :], op=ALU.add), reads=[B_bo, B_ylT], writes=[B_bo])
                S.op("dve", lambda e: e.tensor_tensor(out=ob_[:], in0=bo[:], in1=ga[:], op=ALU.mult), reads=[B_bo], writes=[B_ob])
                S.dma("sp", cat_d.ap()[j * 128:(j + 1) * 128, :], ob_[:], [B_ob], B_cat)
            S.barrier()

    def mla_stage(l, last):
        with ExitStack() as es:
            B_qn = Buf("qnb")
            wuq = sb(es, "wuq", [128, 6, 576], BF16)
            wukv = sb(es, "wukv", [128, 2, 768], BF16)
            B_wu = Buf("wu")
            S.dma("pool", wuq[:], w_uq_d.ap()[l].rearrange("(kc p) n -> p kc n", p=128), [], B_wu)
            S.dma("pool", wukv[:], w_ukv_d.ap()[l].rearrange("(kc p) n -> p kc n", p=128), [], B_wu)
            QT = sb(es, "QT", [96, 6, T], BF16)
            KT = sb(es, "KT", [96, 6, T], BF16)
            Vt = sb(es, "Vt", [128, 18, 6, 65], BF16)
            B_QT, B_KT, B_V = Buf("QT"), Buf("KT"), Buf("Vt")
            gbc = sb(es, "gbc", [128, 192], F32)
            ropet = sb(es, "ropet", [128, 16, 32], F32)
            S.dma("sp", gbc[:], gbc_d.ap()[l], [], B_const)
            S.dma("sp", ropet[:], rope_d.ap().rearrange("(n p) c -> p n c", p=128), [], B_const)
            S.op("dve", lambda e: e.memset(Vt[:], 1.0), writes=[B_V])
            with ExitStack() as es2:
                pq0 = ps(es2, "pq0", [128, 512])
                pq1 = ps(es2, "pq1", [128, 512])
                pk0 = ps(es2, "pk0", [128, 512])
                pk1 = ps(es2, "pk1", [128, 512])
                pkr = ps(es2, "pkr", [128, 512])
                ptq = ps(es2, "ptq", [128, 1024], BF16)
                ptk = ps(es2, "ptk", [128, 1024], BF16)
                Bp = {n: Buf(n) for n in ("pq0", "pq1", "pk0", "pk1", "pkr", "ptq", "ptk")}
                sets = []
                for k_ in range(2):
                    Z = {}
                    for (n_, shp, dt_) in (("q_sb", [128, 576], F32), ("kv_sb", [128, 768], F32), ("kr_in", [32, 128], F32), ("kr_sb", [128, 32], F32),
                                           ("qr", [128, 6, 32], F32), ("kr", [128, 1, 32], F32), ("krr", [128, 32], F32), ("rt1", [128, 6, 2, 8], F32),
                                           ("rt2", [128, 6, 2, 8], F32), ("Qtok", [128, 6, 96], BF16), ("Ktok", [128, 6, 96], BF16),
                                           ("sqt", [128, 6, 64], F32), ("ssq", [128, 6], F32)):
                        Z[n_] = sb(es2, "%s_%d" % (n_, k_), shp, dt_)
                    Z["B"] = {n: Buf("%s_%d" % (n, k_)) for n in ("q_sb", "kv_sb", "kr_in", "kr_sb", "qr", "kr", "krr", "rt", "Qtok", "Ktok", "scr")}
                    sets.append(Z)
                qin = sb(es2, "qin", [128, 8, 256], F32)
                B_qin = Buf("qin")
                sq = sb(es2, "sq", [128, 8, 256], BF16)
                tmps = [sb(es2, "tmp%d" % i, [128, 256], F32) for i in range(2)]
                B_tmps = [Buf("ltmp0"), Buf("ltmp1")]
                rsq = sb(es2, "rsq", [128, 256], F32)
                rskv = sb(es2, "rskv", [128, 256], F32)
                qnb = sb(es2, "qnb", [128, 8, 256], BF16)
                B_sq, B_tmp, B_rsq, B_rskv = Buf("sq"), Buf("tmp"), Buf("rsq"), Buf("rskv")
                pbv = pB_d.ap()[0:1024, :].rearrange("(kc p) t -> p kc t", p=128)

                def latent_norm(t0):
                    w = 256
                    t1 = t0 + w
                    S.dma("sp", qin[:, :, 0:w], pbv[:, :, t0:t1], [B_pB], B_qin)
                    S.op("act", lambda e: e.activation(out=sq[:, :, 0:w], in_=qin[:, :, 0:w], func=AF.Square), reads=[B_qin], writes=[B_sq])
                    for kc in range(6):
                        S.op("pe", lambda e, kc=kc: e.matmul(pq0[:, 0:w], ones_bf[:], sq[:, kc, 0:w], start=(kc == 0), stop=(kc == 5)),
                             reads=[B_sq, B_const], writes=[Bp["pq0"]])
                    for kc in range(6, 8):
                        S.op("pe", lambda e, kc=kc: e.matmul(pk0[:, 0:w], ones_bf[:], sq[:, kc, 0:w], start=(kc == 6), stop=(kc == 7)),
                             reads=[B_sq, B_const], writes=[Bp["pk0"]])
                    S.op("act", lambda e: e.activation(out=rsq[:, 0:w], in_=pq0[:, 0:w], func=AF.Sqrt, bias=EPS, scale=1.0 / 768), reads=[Bp["pq0"]], writes=[B_rsq])
                    S.op("dve", lambda e: e.reciprocal(out=rsq[:, 0:w], in_=rsq[:, 0:w]), reads=[B_rsq], writes=[B_rsq])
                    S.op("act", lambda e: e.activation(out=rskv[:, 0:w], in_=pk0[:, 0:w], func=AF.Sqrt, bias=EPS, scale=1.0 / 256), reads=[Bp["pk0"]], writes=[B_rskv])
                    S.op("dve", lambda e: e.reciprocal(out=rskv[:, 0:w], in_=rskv[:, 0:w]), reads=[B_rskv], writes=[B_rskv])
                    for kc in range(8):
                        rs, B_rs = (rsq, B_rsq) if kc < 6 else (rskv, B_rskv)
                        g_ap = ppc(l, "qng", kc) if kc < 6 else ppc(l, "kvng", kc - 6)
                        tmp, B_tmp = tmps[kc % 2], B_tmps[kc % 2]
                        S.op("dve", lambda e, kc=kc: e.tensor_tensor(out=tmp[:, 0:w], in0=qin[:, kc, 0:w], in1=rs[:, 0:w], op=ALU.mult),
                             reads=[B_qin, B_rs], writes=[B_tmp])
                        S.op("act", lambda e, kc=kc: e.activation(out=qnb[:, kc, 0:w], in_=tmp[:, 0:w], func=AF.Identity, scale=g_ap, bias=0.0),
                             reads=[B_tmp, B_pp], writes=[B_qn])

                def rope(Z, src3, H, dst3, tt, B_src, B_dst):
                    Bn = Z["B"]
                    sv = src3.rearrange("p h (a f e) -> p h a f e", a=2, f=2, e=8)
                    dv = dst3.rearrange("p h (a f e) -> p h a f e", a=2, f=2, e=8)
                    cos = ropet[:, tt - 2, 0:16].rearrange("p (a e) -> p a e", a=2).unsqueeze(1).to_broadcast([128, H, 2, 8])
                    sin = ropet[:, tt - 2, 16:32].rearrange("p (a e) -> p a e", a=2).unsqueeze(1).to_broadcast([128, H, 2, 8])
                    x1, x2 = sv[:, :, :, 0, :], sv[:, :, :, 1, :]
                    a_, b_ = Z["rt1"][:, 0:H], Z["rt2"][:, 0:H]
                    S.op("dve", lambda e: e.tensor_tensor(out=a_, in0=x1, in1=cos, op=ALU.mult), reads=[B_src, B_const], writes=[Bn["rt"]])
                    yield
                    S.op("dve", lambda e: e.tensor_tensor(out=b_, in0=x2, in1=sin, op=ALU.mult), reads=[B_src, B_const], writes=[Bn["rt"]])
                    yield
                    S.op("dve", lambda e: e.tensor_tensor(out=dv[:, :, :, 0, :], in0=a_, in1=b_, op=ALU.subtract), reads=[Bn["rt"]], writes=[B_dst])
                    yield
                    S.op("dve", lambda e: e.tensor_tensor(out=a_, in0=x2, in1=cos, op=ALU.mult), reads=[B_src, B_const, B_dst], writes=[Bn["rt"]])
                    yield
                    S.op("dve", lambda e: e.tensor_tensor(out=b_, in0=x1, in1=sin, op=ALU.mult), reads=[B_src, B_const], writes=[Bn["rt"]])
                    yield
                    S.op("dve", lambda e: e.tensor_tensor(out=dv[:, :, :, 1, :], in0=a_, in1=b_, op=ALU.add), reads=[Bn["rt"]], writes=[B_dst])
                    yield

                def gen_tile(tt, Z):
                    Bn = Z["B"]
                    q_sb, kv_sb, kr_in, kr_sb, qr, kr, krr, Qtok, Ktok = (Z[n] for n in ("q_sb", "kv_sb", "kr_in", "kr_sb", "qr", "kr", "krr", "Qtok", "Ktok"))
                    q3 = q_sb[:].rearrange("p (h c) -> p h c", h=6)
                    kv3 = kv_sb[:].rearrange("p (h c) -> p h c", h=6)
                    scr = (Z["sqt"], Z["ssq"])
                    t0 = tt * 128
                    lo = (tt % 2) * 128
                    need_q = (tt >= 2) or (not last)
                    if need_q:
                        for kc in range(6):
                            S.op("pe", lambda e, kc=kc: e.matmul(pq0[:, 0:512], qnb[:, kc, lo:lo + 128], wuq[:, kc, 0:512], start=(kc == 0), stop=(kc == 5)),
                                 reads=[B_qn, B_wu], writes=[Bp["pq0"]])
                        for kc in range(6):
                            S.op("pe", lambda e, kc=kc: e.matmul(pq1[:, 0:64], qnb[:, kc, lo:lo + 128], wuq[:, kc, 512:576], start=(kc == 0), stop=(kc == 5)),
                                 reads=[B_qn, B_wu], writes=[Bp["pq1"]])
                        S.op("act", lambda e: e.activation(out=q_sb[:, 0:512], in_=pq0[:, 0:512], func=AF.Copy), reads=[Bp["pq0"]], writes=[Bn["q_sb"]])
                        S.op("act", lambda e: e.activation(out=q_sb[:, 512:576], in_=pq1[:, 0:64], func=AF.Copy), reads=[Bp["pq1"]], writes=[Bn["q_sb"]])
                        yield
                    for kc in range(2):
                        S.op("pe", lambda e, kc=kc: e.matmul(pk0[:, 0:512], qnb[:, 6 + kc, lo:lo + 128], wukv[:, kc, 0:512], start=(kc == 0), stop=(kc == 1)),
                             reads=[B_qn, B_wu], writes=[Bp["pk0"]])
                    for kc in range(2):
                        S.op("pe", lambda e, kc=kc: e.matmul(pk1[:, 0:256], qnb[:, 6 + kc, lo:lo + 128], wukv[:, kc, 512:768], start=(kc == 0), stop=(kc == 1)),
                             reads=[B_qn, B_wu], writes=[Bp["pk1"]])
                    S.op("act", lambda e: e.activation(out=kv_sb[:, 0:512], in_=pk0[:, 0:512], func=AF.Copy), reads=[Bp["pk0"]], writes=[Bn["kv_sb"]])
                    S.op("act", lambda e: e.activation(out=kv_sb[:, 512:768], in_=pk1[:, 0:256], func=AF.Copy), reads=[Bp["pk1"]], writes=[Bn["kv_sb"]])
                    S.dma("sp", kr_in[:], pB_d.ap()[1024:1056, t0:t0 + 128], [B_pB], Bn["kr_in"])
                    S.op("pe", lambda e: e.transpose(pkr[:, 0:32], kr_in[:], ident_f[0:32, 0:32]), reads=[Bn["kr_in"], B_id], writes=[Bp["pkr"]])
                    S.op("act", lambda e: e.activation(out=kr_sb[:], in_=pkr[:, 0:32], func=AF.Copy), reads=[Bp["pkr"]], writes=[Bn["kr_sb"]])
                    yield
                    if need_q:
                        yield from headnorm(q3[:, :, 0:64], 6, 64, gbc[:, 0:64], Qtok[:, :, 0:64], scr, Bn["q_sb"], Bn["Qtok"], Bn["scr"])
                        if tt >= 2:
                            yield from headnorm(q3[:, :, 64:96], 6, 32, gbc[:, 128:160], qr[:], scr, Bn["q_sb"], Bn["qr"], Bn["scr"])
                            yield from rope(Z, qr[:], 6, Qtok[:, :, 64:96], tt, Bn["qr"], Bn["Qtok"])
                        else:
                            yield from headnorm(q3[:, :, 64:96], 6, 32, gbc[:, 128:160], Qtok[:, :, 64:96], scr, Bn["q_sb"], Bn["Qtok"], Bn["scr"])
                    yield from headnorm(kv3[:, :, 0:64], 6, 64, gbc[:, 64:128], Ktok[:, :, 0:64], scr, Bn["kv_sb"], Bn["Ktok"], Bn["scr"])
                    S.op("act", lambda e: e.activation(out=Vt[:, tt, :, 0:64], in_=kv3[:, :, 64:128], func=AF.Copy), reads=[Bn["kv_sb"]], writes=[B_V])
                    if tt >= 2:
                        yield from headnorm(kr_sb[:].unsqueeze(1), 1, 32, gbc[:, 160:192], kr[:], scr, Bn["kr_sb"], Bn["kr"], Bn["scr"])
                        yield from rope(Z, kr[:], 1, krr[:].unsqueeze(1), tt, Bn["kr"], Bn["krr"])
                    else:
                        yield from headnorm(kr_sb[:].unsqueeze(1), 1, 32, gbc[:, 160:192], krr[:].unsqueeze(1), scr, Bn["kr_sb"], Bn["krr"], Bn["scr"])
                    S.op("dve", lambda e: e.tensor_copy(out=Ktok[:, :, 64:96], in_=krr[:].unsqueeze(1).to_broadcast([128, 6, 32])),
                         reads=[Bn["krr"]], writes=[Bn["Ktok"]])
                    yield
                    if need_q:
                        for h in range(6):
                            S.op("pe", lambda e, h=h: e.transpose(ptq[0:96, h * 128:(h + 1) * 128], Qtok[:, h, :], ident_b[:]),
                                 reads=[Bn["Qtok"], B_const], writes=[Bp["ptq"]])
                        S.op("act", lambda e: e.activation(out=QT[:, :, t0:t0 + 128], in_=ptq[0:96, 0:768].rearrange("p (h t) -> p h t", h=6), func=AF.Copy),
                             reads=[Bp["ptq"]], writes=[B_QT])
                        yield
                    for h in range(6):
                        S.op("pe", lambda e, h=h: e.transpose(ptk[0:96, h * 128:(h + 1) * 128], Ktok[:, h, :], ident_b[:]),
                             reads=[Bn["Ktok"], B_const], writes=[Bp["ptk"]])
                    S.op("act", lambda e: e.activation(out=KT[:, :, t0:t0 + 128], in_=ptk[0:96, 0:768].rearrange("p (h t) -> p h t", h=6), func=AF.Copy),
                         reads=[Bp["ptk"]], writes=[B_KT])
                    yield

                for pr in range(9 if not DBG.get("skip_mla_b") else 0):
                    latent_norm(pr * 256)
                    run_streams([gen_tile(2 * pr, sets[0]), gen_tile(2 * pr + 1, sets[1])])
                S.barrier()
            with ExitStack() as es2:
                pss = [ps(es2, "pss%d" % i, [128, 512]) for i in range(2)]
                pso = [ps(es2, "pso%d" % i, [128, 512]) for i in range(4)]
                Bpss = [Buf("pss%d" % i) for i in range(2)]
                Bpso = [Buf("pso%d" % i) for i in range(4)]
                pts = [sb(es2, "pt%d" % i, [128, 512], BF16) for i in range(2)]
                Bpt = [Buf("pt%d" % i) for i in range(2)]
                ytok = sb(es2, "ytok", [128, 18, 384], BF16)
                B_yt = Buf("ytok")
                rec = sb(es2, "rec", [128, 4], F32)
                B_rec = Buf("rec")
                ycT = sb(es2, "ycT", [128, 3, T], BF16)
                B_yc = Buf("ycT")
                ptt = ps(es2, "ptt", [128, 1024], BF16)
                B_ptt = Buf("ptt")
                qblocks = [(256 + i * 512, 512, list(range(18))) for i in range(4)]
                if not last:
                    qblocks = [(0, 256, [0, 1])] + qblocks
                its = []
                for h in range(6 if not DBG.get("skip_mla_c") else 0):
                    for (q0, qw, kts) in qblocks:
                        for ki, kt in enumerate(kts):
                            its.append((h, q0, qw, kt, ki, len(kts)))

                def score(i):
                    h, q0, qw, kt, ki, nk = its[i]
                    p = i % 2
                    S.op("pe", lambda e: e.matmul(pss[p][:, 0:qw], KT[:, h, kt * 128:(kt + 1) * 128], QT[:, h, q0:q0 + qw], start=True, stop=True),
                         reads=[B_KT, B_QT], writes=[Bpss[p]])
                    S.op("act", lambda e: e.activation(out=pts[p][:, 0:qw], in_=pss[p][:, 0:qw], func=AF.Exp, scale=ATTN_SCALE),
                         reads=[Bpss[p]], writes=[Bpt[p]])

                if its:
                    score(0)
                for i in range(len(its)):
                    h, q0, qw, kt, ki, nk = its[i]
                    p = i % 2
                    nqs = qw // 128
                    if i + 1 < len(its):
                        score(i + 1)
                    for qs in range(nqs):
                        S.op("pe", lambda e, qs=qs: e.matmul(pso[qs][:, 0:65], pts[p][:, qs * 128:(qs + 1) * 128], Vt[:, kt, h, :],
                                                            start=(ki == 0), stop=(ki == nk - 1)),
                             reads=[Bpt[p], B_V], writes=[Bpso[qs]])
                    if ki == nk - 1:
                        for qs in range(nqs):
                            tq = (q0 + qs * 128) // 128
                            S.op("dve", lambda e, qs=qs: e.reciprocal(out=rec[:, qs:qs + 1], in_=pso[qs][:, 64:65]), reads=[Bpso[qs]], writes=[B_rec])
                            S.op("dve", lambda e, qs=qs, tq=tq: e.tensor_scalar(out=ytok[:, tq, h * 64:(h + 1) * 64], in0=pso[qs][:, 0:64], scalar1=rec[:, qs:qs + 1],
                                                                                scalar2=None, op0=ALU.mult), reads=[Bpso[qs], B_rec], writes=[B_yt])
                tts = range(18) if not last else range(2, 18)
                for tt in tts:
                    for j in range(3):
                        S.op("pe", lambda e, j=j: e.transpose(ptt[:, j * 128:(j + 1) * 128], ytok[:, tt, j * 128:(j + 1) * 128], ident_b[:]),
                             reads=[B_yt, B_const], writes=[B_ptt])
                    S.op("act", lambda e: e.activation(out=ycT[:, :, tt * 128:(tt + 1) * 128], in_=ptt[:, 0:384].rearrange("p (j t) -> p j t", j=3), func=AF.Copy),
                         reads=[B_ptt], writes=[B_yc])
                c_lo = 0 if not last else 256
                for j in range(3):
                    S.dma("sp", cat_d.ap()[384 + j * 128:384 + (j + 1) * 128, c_lo:T], ycT[:, j, c_lo:T], [B_yc], B_cat)
                S.barrier()

    for l in range(n_layers):
        last = l == 1
        with ExitStack() as es:
            wts = [sb(es, "adw%d" % i, [128, KC, 512], BF16) for i in range(2)]
            Bw = [Buf("adw%d" % i) for i in range(2)]
            psm = ps(es, "psm", [128, 96])
            B_psm = Buf("psm")
            prow = [ps(es, "prow%d" % i, [2, 512]) for i in range(2)]
            B_prow = [Buf("prow%d" % i) for i in range(2)]
            mrow = sb(es, "mrow", [2, 6 * D], F32)
            B_mrow = Buf("mrow")
            wv = ada_w_d.ap()[l].rearrange("(kc p) n -> p kc n", p=128)
            for og in range(12):
                s = og % 2
                S.dma("pool", wts[s][:], wv[:, :, og * 512:(og + 1) * 512], [], Bw[s])
                for kc in range(KC):
                    S.op("pe", lambda e, s=s, kc=kc: e.matmul(prow[s][:, 0:512], scT[:, kc, :], wts[s][:, kc, :], start=(kc == 0), stop=(kc == KC - 1)),
                         reads=[Bw[s], B_scT], writes=[B_prow[s]])
                S.op("act", lambda e, s=s, og=og: e.activation(out=mrow[:, og * 512:(og + 1) * 512], in_=prow[s][:, 0:512], func=AF.Copy),
                     reads=[B_prow[s]], writes=[B_mrow])
            for ch in range(48):
                S.op("pe", lambda e, ch=ch: e.transpose(psm[:, 2 * ch:2 * ch + 2], mrow[:, ch * 128:(ch + 1) * 128], ident_f[0:2, 0:2]),
                     reads=[B_mrow, B_id], writes=[B_psm])
            o, _ = PP["adab"]
            S.op("dve", lambda e: e.tensor_tensor(out=mod[:], in0=psm[:].rearrange("p (a b) -> p a b", b=2),
                                                  in1=ppt[:, l, o:o + 48].unsqueeze(2).to_broadcast([128, 48, 2]), op=ALU.add),
                 reads=[B_psm, B_pp], writes=[B_mod])
            for (gs, sc0, gname) in ((gs1, 8, "n1g"), (gs2, 32, "n2g")):
                og_, _ = PP[gname]
                S.op("dve", lambda e, gs=gs, sc0=sc0: e.tensor_scalar(out=gs[:], in0=mod[:, sc0:sc0 + 8, :], scalar1=1.0, scalar2=None, op0=ALU.add),
                     reads=[B_mod], writes=[B_mod])
                S.op("dve", lambda e, gs=gs, og_=og_: e.tensor_tensor(out=gs[:], in0=gs[:], in1=ppt[:, l, og_:og_ + 8].unsqueeze(2).to_broadcast([128, 8, 2]), op=ALU.mult),
                     reads=[B_mod, B_pp], writes=[B_mod])
            o0, _ = PP["mu0"]
            o1, _ = PP["mu1"]
            S.op("dve", lambda e: e.tensor_tensor(out=cmix[:], in0=ppt[:, l, o0:o0 + 11], in1=ppt[:, l, o1:o1 + 11], op=ALU.add),
                 reads=[B_pp], writes=[B_mod])
            S.op("dve", lambda e: e.tensor_scalar(out=cmix[:], in0=cmix[:], scalar1=-1.0, scalar2=1.0, op0=ALU.mult, op1=ALU.add),
                 reads=[B_mod], writes=[B_mod])
            S.barrier()

        if "mix" in stages:
            with ExitStack() as es:
                xnT = sb(es, "xnT", [128, KC, T], BF16)
                B_xn = Buf("xnT")
                tmp = [sb(es, "tmp%d" % i, [128, 512], F32) for i in range(2)]
                sq = sb(es, "sq", [128, KC, 512], BF16)
                rstd = sb(es, "rstd", [128, 512], F32)
                ps_s = ps(es, "ps_s", [128, 512])
                B_tmp, B_sq, B_pss, B_rstd = [Buf("tmpa"), Buf("tmpb")], Buf("sq"), Buf("pss"), Buf("rstd")
                for bi in range(len(BLKS)):
                    norm_mod(es, bi, gs1, 0, xnT, B_xn, BLKS[bi][0], tmp, B_tmp, sq, B_sq, ps_s, B_pss, rstd, B_rstd)

                wts = [sb(es, "wi%d" % i, [128, KC, 512], BF16) for i in range(2)]
                Bw = [Buf("wi%d" % i) for i in range(2)]
                feat = [sb(es, "feat%d" % i, [128, T], F32) for i in range(5)]
                Bf = [Buf("feat%d" % i) for i in range(5)]
                ybf = sb(es, "ybf", [128, T], BF16)
                B_ybf = Buf("ybf")
                pmm = [ps(es, "pmm%d" % i, [128, 512]) for i in range(2)]
                Bpm = [Buf("pmm%d" % i) for i in range(2)]
                wv = w_in_d.ap()[l].rearrange("(kc p) n -> p kc n", p=128)
                state = {"w": 0, "p": 0, "f": 0}

                def load_w(pieces):
                    s = state["w"] % 2
                    state["w"] += 1
                    o = 0
                    for (c0, wd) in pieces:
                        S.dma("pool", wts[s][:, :, o:o + wd], wv[:, :, c0:c0 + wd], [], Bw[s])
                        o += wd
                    return s

                def proj_unit(s, o, M, fi):
                    for (t0, t1) in BLKS:
                        w = t1 - t0
                        p = state["p"] % 2
                        state["p"] += 1
                        for kc in range(KC):
                            S.op("pe", lambda e, kc=kc, p=p: e.matmul(pmm[p][0:M, 0:w], wts[s][:, kc, o:o + M], xnT[:, kc, t0:t1],
                                                                    start=(kc == 0), stop=(kc == KC - 1)),
                                 reads=[Bw[s], B_xn], writes=[Bpm[p]])
                        S.op("act", lambda e, p=p: e.activation(out=feat[fi][0:M, t0:t1], in_=pmm[p][0:M, 0:w], func=AF.Copy),
                             reads=[Bpm[p]], writes=[Bf[fi]])

                def shift3(src, dst, Bs, Bd, M, c_ap, m0_ap, m1_ap):
                    S.op("act", lambda e: e.activation(out=dst[0:M, :], in_=src[0:M, :], func=AF.Identity, scale=c_ap, bias=0.0),
                         reads=[Bs, B_mod, B_pp], writes=[Bd])
                    for (a0, a1, sh, sc) in ((1, NCTX, -1, m0_ap), (NCTX + 1, T, -1, m0_ap), (0, NCTX - 1, 1, m1_ap), (NCTX, T - 1, 1, m1_ap)):
                        S.op("dve", lambda e, a0=a0, a1=a1, sh=sh, sc=sc: e.scalar_tensor_tensor(
                            out=dst[0:M, a0:a1], in0=src[0:M, a0 + sh:a1 + sh], scalar=sc, in1=dst[0:M, a0:a1],
                            op0=ALU.mult, op1=ALU.add), reads=[Bs, Bd, B_pp], writes=[Bd])

                a_segs = [[(0, 512)], [(512, 512)], [(1024, 384)]]
                ch = 0
                for pieces in a_segs:
                    s = load_w(pieces)
                    for o in range(0, pieces[0][1], 128):
                        fi = state["f"] % 2
                        state["f"] += 1
                        proj_unit(s, o, 128, fi)
                        shift3(feat[fi], feat[2 + fi], Bf[fi], Bf[2 + fi], 128, cmix[:, ch:ch + 1], ppc(l, "mu0", ch), ppc(l, "mu1", ch))
                        S.dma("sp", pA_d.ap()[ch * 128:(ch + 1) * 128, :], feat[2 + fi][:, :], [Bf[2 + fi]], B_pA)
                        ch += 1
                b_segs = [[(1408, 512)], [(1920, 512)], [(2432, 32)]]
                ch = 0
                for pieces in b_segs:
                    s = load_w(pieces)
                    for o in range(0, pieces[0][1], 128):
                        M = min(128, pieces[0][1] - o)
                        fi = state["f"] % 2
                        state["f"] += 1
                        proj_unit(s, o, M, fi)
                        S.dma("sp", pB_d.ap()[ch * 128:ch * 128 + M, :], feat[fi][0:M, :], [Bf[fi]], B_pB)
                        ch += 1
                c0 = A_IN + 1056
                for j in range(2):
                    s = load_w([(c0 + j * 128, 128), (c0 + 256 + j * 128, 128), (c0 + 512 + j * 128, 128)])
                    proj_unit(s, 0, 128, 0)
                    proj_unit(s, 128, 128, 1)
                    proj_unit(s, 256, 128, 2)
                    S.op("dve", lambda e: e.tensor_tensor(out=feat[1][:], in0=feat[1][:], in1=feat[2][:], op=ALU.mult),
                         reads=[Bf[1], Bf[2]], writes=[Bf[1]])
                    shift3(feat[1], feat[3], Bf[1], Bf[3], 128, ppc(l, "conv", 2 + j), ppc(l, "conv", 0 + j), ppc(l, "conv", 4 + j))
                    S.op("dve", lambda e: e.tensor_tensor(out=ybf[:], in0=feat[0][:], in1=feat[3][:], op=ALU.mult),
                         reads=[Bf[0], Bf[3]], writes=[B_ybf])
                    S.dma("sp", cat_d.ap()[768 + j * 128:768 + (j + 1) * 128, :], ybf[:], [B_ybf], B_cat)
                zrows = ([] if "rwkv" in stages else [0, 1, 2]) + ([] if "mla" in stages else [3, 4, 5])
                if zrows:
                    S.op("dve", lambda e: e.memset(ybf[:], 0.0), writes=[B_ybf])
                    for zr in zrows:
                        S.dma("sp", cat_d.ap()[zr * 128:(zr + 1) * 128, :], ybf[:], [B_ybf], B_cat)
                S.barrier()

            if "rwkv" in stages:
                rwkv_prep(l)
                rwkv_scan(l)
                rwkv_post(l)
            if "mla" in stages:
                mla_stage(l, last)

            with ExitStack() as es:
                wo = sb(es, "wo", [128, KC, D], BF16)
                B_wo = Buf("wo")
                S.dma("pool", wo[:, :, 0:512], w_out_d.ap()[l].rearrange("(kc p) n -> p kc n", p=128)[:, :, 0:512], [], B_wo)
                S.dma("pool", wo[:, :, 512:1024], w_out_d.ap()[l].rearrange("(kc p) n -> p kc n", p=128)[:, :, 512:1024], [], B_wo)
                cats = [sb(es, "catb%d" % i, [128, KC, 512], BF16) for i in range(2)]
                Bc = [Buf("catb%d" % i) for i in range(2)]
                pmm = [ps(es, "pmo%d" % i, [128, 512]) for i in range(2)]
                Bpm = [Buf("pmo%d" % i) for i in range(2)]
                pc = 0
                for bi, (t0, t1) in enumerate(BLKS):
                    if last and bi == 0:
                        continue
                    w = t1 - t0
                    ci = 1 if bi == 0 else 0
                    s = bi % 2
                    S.dma("sp", cats[s][:, :, 0:w], cat_d.ap().rearrange("(kc p) t -> p kc t", p=128)[:, :, t0:t1], [B_cat], Bc[s])
                    for fo in range(KC):
                        p = pc % 2
                        pc += 1
                        for kc in range(KC):
                            S.op("pe", lambda e, kc=kc, p=p, fo=fo: e.matmul(pmm[p][:, 0:w], wo[:, kc, fo * 128:(fo + 1) * 128], cats[s][:, kc, 0:w],
                                                                           start=(kc == 0), stop=(kc == KC - 1)),
                                 reads=[B_wo, Bc[s]], writes=[Bpm[p]])
                        S.op("dve", lambda e, p=p, fo=fo: e.scalar_tensor_tensor(
                            out=xT[:, fo, t0:t1], in0=pmm[p][:, 0:w], scalar=mod[:, 16 + fo, ci:ci + 1], in1=xT[:, fo, t0:t1],
                            op0=ALU.mult, op1=ALU.add), reads=[Bpm[p], B_mod, XB[bi]], writes=[XB[bi]])
                S.barrier()

        if "ffn" in stages:
            sbs = [[0, 1, 2], [3, 4]] if not last else [[1, 2], [3, 4]]
            for sbl in sbs:
                with ExitStack() as es:
                    ntok = sum(BLKS[b][1] - BLKS[b][0] for b in sbl)
                    hT = sb(es, "hT", [128, KC, ntok], BF16)
                    B_h = Buf("hT")
                    actT = sb(es, "actT", [128, NJ, ntok], BF16)
                    B_act = [Buf("actT%d" % j) for j in range(NJ)]
                    tmp = [sb(es, "tmp%d" % i, [128, 512], F32) for i in range(2)]
                    sq = sb(es, "sq", [128, KC, 512], BF16)
                    rstd = sb(es, "rstd", [128, 512], F32)
                    sgs = [sb(es, "sg%d" % i, [128, 512], F32) for i in range(2)]
                    B_sgs = [Buf("sg%d" % i) for i in range(2)]
                    ps_s = ps(es, "ps_s", [128, 512])
                    B_tmp, B_sq, B_pss, B_rstd = [Buf("tmpa"), Buf("tmpb")], Buf("sq"), Buf("pss"), Buf("rstd")
                    loc = {}
                    o = 0
                    for b in sbl:
                        loc[b] = o
                        norm_mod(es, b, gs2, 24, hT, B_h, o, tmp, B_tmp, sq, B_sq, ps_s, B_pss, rstd, B_rstd)
                        o += BLKS[b][1] - BLKS[b][0]
                    wts = [sb(es, "wf%d" % i, [128, KC, 256], BF16) for i in range(2)]
                    Bw = [Buf("wf%d" % i) for i in range(2)]
                    pg = [ps(es, "pg%d" % i, [128, 512]) for i in range(2)]
                    pu = [ps(es, "pu%d" % i, [128, 512]) for i in range(2)]
                    Bpg = [Buf("pg%d" % i) for i in range(2)]
                    Bpu = [Buf("pu%d" % i) for i in range(2)]
                    wv = w_fi_d.ap()[l].rearrange("(kc p) n -> p kc n", p=128)
                    pc = 0
                    for j in range(NJ):
                        s = j % 2
                        S.dma("pool", wts[s][:, :, 0:128], wv[:, :, j * 128:(j + 1) * 128], [], Bw[s])
                        S.dma("pool", wts[s][:, :, 128:256], wv[:, :, DFF + j * 128:DFF + (j + 1) * 128], [], Bw[s])
                        for b in sbl:
                            w = BLKS[b][1] - BLKS[b][0]
                            lo = loc[b]
                            p = pc % 2
                            pc += 1
                            for kc in range(KC):
                                S.op("pe", lambda e, kc=kc, p=p: e.matmul(pg[p][:, 0:w], wts[s][:, kc, 0:128], hT[:, kc, lo:lo + w],
                                                                        start=(kc == 0), stop=(kc == KC - 1)),
                                     reads=[Bw[s], B_h], writes=[Bpg[p]])
                            for kc in range(KC):
                                S.op("pe", lambda e, kc=kc, p=p: e.matmul(pu[p][:, 0:w], wts[s][:, kc, 128:256], hT[:, kc, lo:lo + w],
                                                                        start=(kc == 0), stop=(kc == KC - 1)),
                                     reads=[Bw[s], B_h], writes=[Bpu[p]])
                            sg, B_sg = sgs[p], B_sgs[p]
                            S.op("act", lambda e, p=p: e.activation(out=sg[:, 0:w], in_=pg[p][:, 0:w], func=AF.Silu),
                                 reads=[Bpg[p]], writes=[B_sg])
                            S.op("dve", lambda e, p=p, j=j: e.tensor_tensor(out=actT[:, j, lo:lo + w], in0=sg[:, 0:w], in1=pu[p][:, 0:w], op=ALU.mult),
                                 reads=[B_sg, Bpu[p]], writes=[B_act[j]])
                    wos = [sb(es, "wfo%d" % i, [128, NJ, 128], BF16) for i in range(2)]
                    Bwo = [Buf("wfo%d" % i) for i in range(2)]
                    wov = w_fo_d.ap()[l].rearrange("(j p) n -> p j n", p=128)
                    for fo in range(KC):
                        s = fo % 2
                        S.dma("pool", wos[s][:, 0:11, :], wov[:, 0:11, fo * 128:(fo + 1) * 128], [], Bwo[s])
                        S.dma("pool", wos[s][:, 11:22, :], wov[:, 11:22, fo * 128:(fo + 1) * 128], [], Bwo[s])
                        for b in sbl:
                            t0, t1 = BLKS[b]
                            w = t1 - t0
                            lo = loc[b]
                            ci = 1 if b == 0 else 0
                            p = pc % 2
                            pc += 1
                            for j in range(NJ):
                                S.op("pe", lambda e, j=j, p=p: e.matmul(pg[p][:, 0:w], wos[s][:, j, :], actT[:, j, lo:lo + w],
                                                                      start=(j == 0), stop=(j == NJ - 1)),
                                     reads=[Bwo[s], B_act[j]], writes=[Bpg[p]])
                            S.op("dve", lambda e, p=p, fo=fo: e.scalar_tensor_tensor(
                                out=xT[:, fo, t0:t1], in0=pg[p][:, 0:w], scalar=mod[:, 40 + fo, ci:ci + 1], in1=xT[:, fo, t0:t1],
                                op0=ALU.mult, op1=ALU.add), reads=[Bpg[p], B_mod, XB[b]], writes=[XB[b]])
                    S.barrier()

    yv = yT_d.ap().rearrange("(kc p) t -> p kc t", p=128)
    for bi in range(1, len(BLKS)):
        t0, t1 = BLKS[bi]
        S.dma("sp", yv[:, :, t0 - NCTX:t1 - NCTX], xT[:, :, t0:t1], [XB[bi]], B_y)
    S.E["sp"].wait_ge(B_y.grp.sem, 16 * B_y.grp.cnt)
    es_top.close()
    return nc, S


def rope_table():
    n = np.arange(NLAT)
    r_pos = (n // 64).astype(np.float32)
    c_pos = (n % 64).astype(np.float32)
    inv_freq = (1.0 / (np.float32(10000.0) ** (np.arange(0, 16, 2, dtype=np.float32) / np.float32(16)))).astype(np.float32)
    ang_r = r_pos[:, None] * inv_freq[None, :]
    ang_c = c_pos[:, None] * inv_freq[None, :]
    return np.concatenate([np.cos(ang_r), np.cos(ang_c), np.sin(ang_r), np.sin(ang_c)], axis=1).astype(np.float32)


def make_in_maps(inp):
    pps = np.stack([pack_pp(inp, l) for l in range(2)], axis=0)
    maps = []
    gbc = np.zeros((2, 128, 192), np.float32)
    for l in range(2):
        gbc[l] = np.concatenate([inp["q_nope_g"][l], inp["k_nope_g"][l], inp["q_rope_g"][l], inp["k_rope_g"][l]])[None, :]
    rope = rope_table()
    ii = np.arange(64)
    masks = np.zeros((64, 2, 192), np.float32)
    lt = (ii[:, None] < ii[None, :]).astype(np.float32)
    le = (ii[:, None] <= ii[None, :]).astype(np.float32)
    masks[:, 0, 0:64], masks[:, 0, 64:128], masks[:, 0, 128:192] = lt, le, lt.T
    masks[:, 1, 0:64], masks[:, 1, 64:128], masks[:, 1, 128:192] = lt.T, le.T, lt
    f = lambda a: np.ascontiguousarray(np.asarray(a, np.float32))
    for b in range(8):
        xcat = np.concatenate([inp["ctx"][b], inp["x"][b]], axis=0)
        cT = np.zeros((128, 16), np.float32)
        cT[:, 0::2] = _cols(inp["c"][b])
        cT[:, 1::2] = _cols(inp["c_ctx"])
        maps.append({
            "xT": f(xcat.T), "cT": cT, "pp": pps,
            "ada_w": f(inp["ada_w"]), "w_in": f(inp["w_in"]), "w_out": f(inp["w_out"]),
            "w_ffn_in": f(inp["w_ffn_in"]), "w_ffn_out": f(inp["w_ffn_out"]),
            "decay_up": f(inp["decay_up"]), "icl_up": f(inp["icl_up"]), "gate_up": f(inp["gate_up"]), "masks": masks,
            "w_uq": f(inp["w_uq"]), "w_ukv": f(inp["w_ukv"]), "gbc": gbc, "rope": rope, "ident": np.eye(128, dtype=np.float32),
        })
    return maps


def kernel(**inputs):
    inp = {k: np.asarray(v) for k, v in inputs.items()}
    nc, _ = build_program()
    maps = make_in_maps(inp)
    res = run_bass_kernel_spmd(nc, maps, core_ids=list(range(8)))
    out = np.stack([np.ascontiguousarray(res.results[b]["yT"].T) for b in range(8)], axis=0)
    return out.astype(np.float32)
```
